# Optimizing a Trainium2 kernel written in Bass

```python
import math
import jax, jax.numpy as jnp
from jax import lax
import numpy as np

D_MODEL = 1024
BATCH = 8
SEQ = 4096
DEPTH = 1
DEC_BATCH = 16
DEC_SEQ = 2048
PAST_LEN = 128

GRID_W = 64
N_HEADS = 8
HEAD_DIM = 64
D_ATT = N_HEADS * HEAD_DIM
NA_KH = 8
NA_KW = 16
BAND_W = 2 * NA_KW
N_COL_BLOCKS = GRID_W // NA_KW
SSM_GROUP = 16
D_SSM = 512
N_GROUPS = D_SSM // SSM_GROUP
STATE_P = 64
DT_MIN = 1e-3
DT_MAX = 1e-1
D_FF = 2816
CONV_W = 3
D_IN = 3 * D_ATT + D_SSM + 2 * D_MODEL
EPS = 1e-6
NEG_BIG = -1e30

kernel_name = 'hybrid_natten_s5_encoder'


def rms_norm(x, g):
    x32 = x.astype(jnp.float32)
    y = x32 * lax.rsqrt(jnp.mean(x32 * x32, axis=-1, keepdims=True) + EPS) * g.astype(jnp.float32)
    return y.astype(x.dtype)


def neighborhood_attention(q, k, v, rpb):
    bsz, L = q.shape[0], q.shape[1]
    rows = L // GRID_W
    kh = min(NA_KH, rows)
    n_keys = kh * BAND_W
    scale = HEAD_DIM ** -0.5
    q5 = q.reshape(bsz, rows, GRID_W, N_HEADS, HEAD_DIM)
    k5 = k.reshape(bsz, rows, GRID_W, N_HEADS, HEAD_DIM)
    v5 = v.reshape(bsz, rows, GRID_W, N_HEADS, HEAD_DIM)
    cols = np.arange(GRID_W)
    col_start = np.clip(cols - NA_KW // 2, 0, GRID_W - NA_KW)
    band_start = np.clip(np.arange(N_COL_BLOCKS) * NA_KW - NA_KW // 2, 0, GRID_W - BAND_W)
    band_cols = band_start[:, None] + np.arange(BAND_W)
    qcol = cols.reshape(N_COL_BLOCKS, NA_KW)
    qstart = col_start[qcol][:, :, None]
    kc = band_cols[:, None, :]
    col_ok = (kc >= qstart) & (kc < qstart + NA_KW)
    dc_idx = np.clip(kc - qcol[:, :, None] + NA_KW - 1, 0, 2 * NA_KW - 2)
    mask = np.broadcast_to(col_ok[:, :, None, :], (N_COL_BLOCKS, NA_KW, kh, BAND_W)).reshape(
        N_COL_BLOCKS, NA_KW, n_keys)
    rpb32 = rpb.astype(jnp.float32)

    def one_row(args):
        r, q_r = args
        rs = jnp.clip(r - kh // 2, 0, rows - kh)
        k_r = lax.dynamic_slice_in_dim(k5, rs, kh, axis=1)
        v_r = lax.dynamic_slice_in_dim(v5, rs, kh, axis=1)
        k_b = jnp.moveaxis(k_r[:, :, band_cols], 2, 1).reshape(bsz, N_COL_BLOCKS, n_keys, N_HEADS, HEAD_DIM)
        v_b = jnp.moveaxis(v_r[:, :, band_cols], 2, 1).reshape(bsz, N_COL_BLOCKS, n_keys, N_HEADS, HEAD_DIM)
        q_b = q_r.reshape(bsz, N_COL_BLOCKS, NA_KW, N_HEADS, HEAD_DIM)
        s = jnp.einsum('bnqhd,bnkhd->bhnqk', q_b, k_b,
                       preferred_element_type=jnp.float32) * scale
        dr = rs + jnp.arange(kh) - r
        bias = rpb32[:, dr + NA_KH - 1][:, :, dc_idx]
        bias = jnp.transpose(bias, (0, 2, 3, 1, 4)).reshape(N_HEADS, N_COL_BLOCKS, NA_KW, n_keys)
        s = jnp.where(mask, s + bias, NEG_BIG)
        p = jax.nn.softmax(s, axis=-1)
        o = jnp.einsum('bhnqk,bnkhd->bnqhd', p.astype(v.dtype), v_b)
        return o.reshape(bsz, GRID_W, D_ATT)

    out = lax.map(one_row, (jnp.arange(rows), jnp.moveaxis(q5, 1, 0)))
    return jnp.moveaxis(out, 0, 1).reshape(bsz, L, D_ATT)


def _ssm_combine(left, right):
    a_l, b_l = left
    a_r, b_r = right
    return a_r * a_l, a_r * b_l + b_r


def s5_mixer(u, lam_re, lam_im, log_dt, b_re, b_im, c_re, c_im, d_skip, w_glu, b_glu):
    bsz, L = u.shape[0], u.shape[1]
    f32 = jnp.float32
    u32 = u.astype(f32).reshape(bsz, L, N_GROUPS, SSM_GROUP)
    uc = u32.astype(jnp.complex64)
    y = d_skip.astype(f32).reshape(N_GROUPS, SSM_GROUP) * u32
    for dirn, rev in enumerate((False, True)):
        lam = lax.complex(lam_re[dirn].astype(f32), lam_im[dirn].astype(f32))
        dt = jnp.exp(log_dt[dirn].astype(f32))[:, None]
        lam_bar = jnp.exp(lam * dt)
        bmat = lax.complex(b_re[dirn].astype(f32), b_im[dirn].astype(f32))
        b_bar = ((lam_bar - 1.0) / lam)[:, :, None] * bmat
        bu = jnp.einsum('blgh,gph->blgp', uc, b_bar)
        a = jnp.broadcast_to(lam_bar, bu.shape)
        _, h = lax.associative_scan(_ssm_combine, (a, bu), reverse=rev, axis=1)
        cmat = lax.complex(c_re[dirn].astype(f32), c_im[dirn].astype(f32))
        y = y + jnp.real(jnp.einsum('ghp,blgp->blgh', cmat, h))
    y = jax.nn.gelu(y.reshape(bsz, L, D_SSM))
    y = y * jax.nn.sigmoid(y @ w_glu.astype(f32) + b_glu.astype(f32))
    return y.astype(u.dtype)


def dwconv3_centred(x, w, b):
    xp = jnp.pad(x, ((0, 0), (1, 1), (0, 0)))
    return xp[:, :-2] * w[0] + xp[:, 1:-1] * w[1] + xp[:, 2:] * w[2] + b


def encoder_layer(x, g_mix_pre, g_mix_post, w_in, attn_rpb, ssm_lam_re, ssm_lam_im, ssm_log_dt,
                  ssm_b_re, ssm_b_im, ssm_c_re, ssm_c_im, ssm_d, w_glu, b_glu, w_branch_att,
                  w_branch_ssm, w_out, g_ffn_pre, g_ffn_post, w_up, conv_w, conv_b, w_down):
    bsz, L = x.shape[0], x.shape[1]
    h = rms_norm(x, g_mix_pre)
    proj = h @ w_in
    splits = np.cumsum([D_ATT, D_ATT, D_ATT, D_SSM, D_MODEL])
    q, k, v, u_ssm, gate_att, gate_ssm = jnp.split(proj, splits, axis=-1)
    q = q.reshape(bsz, L, N_HEADS, HEAD_DIM)
    k = k.reshape(bsz, L, N_HEADS, HEAD_DIM)
    v = v.reshape(bsz, L, N_HEADS, HEAD_DIM)
    att = neighborhood_attention(q, k, v, attn_rpb)
    ssm = s5_mixer(u_ssm, ssm_lam_re, ssm_lam_im, ssm_log_dt, ssm_b_re, ssm_b_im,
                   ssm_c_re, ssm_c_im, ssm_d, w_glu, b_glu)
    merged = (jax.nn.sigmoid(gate_att) * (att @ w_branch_att)
              + jax.nn.sigmoid(gate_ssm) * (ssm @ w_branch_ssm))
    x = x + rms_norm(merged @ w_out, g_mix_post)
    h = rms_norm(x, g_ffn_pre)
    up = dwconv3_centred(h @ w_up, conv_w, conv_b)
    a, b = jnp.split(up, 2, axis=-1)
    x = x + rms_norm((jax.nn.gelu(a) * b) @ w_down, g_ffn_post)
    return x


def setup_inputs(seed: int = 0) -> dict:
    key = jax.random.key(seed)
    ks = jax.random.split(key, 32)
    f32 = jnp.float32

    def nrm(k, shape, scale):
        return jax.random.normal(k, shape, f32) * scale

    n = jnp.arange(STATE_P, dtype=f32)
    return {
        'x_prompt': nrm(ks[0], (BATCH, SEQ, D_MODEL), 1.0),
        'x_sample': nrm(ks[1], (DEC_BATCH, DEC_SEQ, D_MODEL), 1.0),
        'g_mix_pre': 1.0 + nrm(ks[2], (DEPTH, D_MODEL), 0.05),
        'g_mix_post': 1.0 + nrm(ks[3], (DEPTH, D_MODEL), 0.05),
        'w_in': nrm(ks[4], (DEPTH, D_MODEL, D_IN), D_MODEL ** -0.5),
        'attn_rpb': nrm(ks[5], (DEPTH, N_HEADS, 2 * NA_KH - 1, 2 * NA_KW - 1), 0.02),
        'ssm_lam_re': -0.5 + nrm(ks[6], (DEPTH, 2, N_GROUPS, STATE_P), 0.01),
        'ssm_lam_im': math.pi * n + nrm(ks[7], (DEPTH, 2, N_GROUPS, STATE_P), 0.01),
        'ssm_log_dt': jax.random.uniform(ks[8], (DEPTH, 2, N_GROUPS), f32,
                                         minval=math.log(DT_MIN), maxval=math.log(DT_MAX)),
        'ssm_b_re': nrm(ks[9], (DEPTH, 2, N_GROUPS, STATE_P, SSM_GROUP), (2 * SSM_GROUP) ** -0.5),
        'ssm_b_im': nrm(ks[10], (DEPTH, 2, N_GROUPS, STATE_P, SSM_GROUP), (2 * SSM_GROUP) ** -0.5),
        'ssm_c_re': nrm(ks[11], (DEPTH, 2, N_GROUPS, SSM_GROUP, STATE_P), (2 * STATE_P) ** -0.5),
        'ssm_c_im': nrm(ks[12], (DEPTH, 2, N_GROUPS, SSM_GROUP, STATE_P), (2 * STATE_P) ** -0.5),
        'ssm_d': nrm(ks[13], (DEPTH, D_SSM), 1.0),
        'w_glu': nrm(ks[14], (DEPTH, D_SSM, D_SSM), D_SSM ** -0.5),
        'b_glu': nrm(ks[15], (DEPTH, D_SSM), 0.01),
        'w_branch_att': nrm(ks[16], (DEPTH, D_ATT, D_MODEL), D_ATT ** -0.5),
        'w_branch_ssm': nrm(ks[17], (DEPTH, D_SSM, D_MODEL), D_SSM ** -0.5),
        'w_out': nrm(ks[18], (DEPTH, D_MODEL, D_MODEL), D_MODEL ** -0.5),
        'g_ffn_pre': 1.0 + nrm(ks[19], (DEPTH, D_MODEL), 0.05),
        'g_ffn_post': 1.0 + nrm(ks[20], (DEPTH, D_MODEL), 0.05),
        'w_up': nrm(ks[21], (DEPTH, D_MODEL, 2 * D_FF), D_MODEL ** -0.5),
        'conv_w': nrm(ks[22], (DEPTH, CONV_W, 2 * D_FF), CONV_W ** -0.5),
        'conv_b': nrm(ks[23], (DEPTH, 2 * D_FF), 0.01),
        'w_down': nrm(ks[24], (DEPTH, D_FF, D_MODEL), D_FF ** -0.5),
    }


def reference(x_prompt, x_sample, g_mix_pre, g_mix_post, w_in, attn_rpb, ssm_lam_re, ssm_lam_im,
              ssm_log_dt, ssm_b_re, ssm_b_im, ssm_c_re, ssm_c_im, ssm_d, w_glu, b_glu, w_branch_att,
              w_branch_ssm, w_out, g_ffn_pre, g_ffn_post, w_up, conv_w, conv_b, w_down):
    def trunk(x):
        for l in range(DEPTH):
            x = encoder_layer(x, g_mix_pre[l], g_mix_post[l], w_in[l], attn_rpb[l], ssm_lam_re[l],
                              ssm_lam_im[l], ssm_log_dt[l], ssm_b_re[l], ssm_b_im[l], ssm_c_re[l],
                              ssm_c_im[l], ssm_d[l], w_glu[l], b_glu[l], w_branch_att[l],
                              w_branch_ssm[l], w_out[l], g_ffn_pre[l], g_ffn_post[l], w_up[l],
                              conv_w[l], conv_b[l], w_down[l])
        return x

    y_prompt = trunk(x_prompt)
    y_sample = trunk(x_sample)
    return (y_prompt, y_sample)
```

```python
import numpy as np
from contextlib import ExitStack
import concourse.bass as bass
import concourse.mybir as mybir
from concourse.bass_utils import run_bass_kernel_spmd

F32 = mybir.dt.float32
BF16 = mybir.dt.bfloat16
I32 = mybir.dt.int32
AF = mybir.ActivationFunctionType
ALU = mybir.AluOpType

D = 1024
DFF = 2816
EPS = 1e-6
GC1 = 1.5957691216057308
GC2 = GC1 * 0.044715
NEG = -30000.0
STORE_Q = 'sp'
STRICT = True
import os
P3_TINY = os.environ.get('P3_TINY', 'act')
P3_POLY = os.environ.get('P3_POLY', 'pool')
P3_DBG = os.environ.get('P3_DBG', '')


class KB:
    NDS = 24

    def __init__(self, nc, es):
        self.nc = nc
        self.E = {'pe': nc.tensor, 'act': nc.scalar, 'dve': nc.vector, 'pool': nc.gpsimd, 'sp': nc.sync}
        self.sem = {e: es.enter_context(nc.semaphore('s_' + e)) for e in ['pe', 'act', 'dve', 'pool']}
        self.cnt = {e: 0 for e in self.sem}
        self.dsem = [es.enter_context(nc.semaphore('d%d' % i)) for i in range(self.NDS)]
        self.dcnt = [0] * self.NDS
        self.dnext = 0
        self.seen = {e: {} for e in self.E}
        self.lastw = {}
        self.readers = {}
        self.unsig = {e: False for e in self.sem}
        self.excl = set(['B%d' % i for i in range(8)] + ['B7_%d' % i for i in range(4)] + ['bu0', 'bu1', 'cu0', 'cu1', 'yps']
                        + ['pup%d' % i for i in range(4)] + ['pdn0', 'pdn1', 'ptr0', 'ptr1'])

    def _wait(self, eng, ev):
        key, val = ev
        if self.seen[eng].get(key, 0) >= val:
            return
        sem = self.sem[key] if isinstance(key, str) else self.dsem[key]
        self.E[eng].wait_ge(sem, val)
        self.seen[eng][key] = val

    def _deps(self, eng, reads, writes):
        for b in reads:
            ev = self.lastw.get(b)
            if ev is not None and not (ev[0] == eng and eng == 'pe'):
                self._wait(eng, ev)
            if b in self.excl:
                for rv in self.readers.get(b, ()):
                    if rv[0] != eng:
                        self._wait(eng, rv)
        for b in writes:
            ev = self.lastw.get(b)
            if ev is not None and (ev[0] != eng or (STRICT and eng != 'pe')):
                self._wait(eng, ev)
            for rv in self.readers.get(b, ()):
                if rv[0] != eng or (STRICT and eng != 'pe'):
                    self._wait(eng, rv)

    def _reg(self, ev, reads, writes):
        for b in reads:
            self.readers.setdefault(b, []).append(ev)
        for b in writes:
            self.lastw[b] = ev
            self.readers[b] = []

    def op(self, eng, fn, reads=(), writes=(), signal=True):
        self._deps(eng, reads, writes)
        ins = fn(self.E[eng])
        if signal:
            self.cnt[eng] += 1
            ins.then_inc(self.sem[eng], 1)
            ev = (eng, self.cnt[eng])
            self.unsig[eng] = False
        else:
            ev = (eng, self.cnt[eng] + 1)
            self.unsig[eng] = True
        self._reg(ev, reads, writes)
        return ev

    def dma(self, q, pairs, reads=(), writes=()):
        if q == 'pool' and STORE_Q != 'pool':
            q = STORE_Q
        i = self.dnext
        self.dnext = (i + 1) % self.NDS
        if self.dcnt[i] > 0:
            self._wait(q, (i, self.dcnt[i]))
        self._deps(q, reads, writes)
        for (o, s) in pairs:
            self.E[q].dma_start(out=o, in_=s).then_inc(self.dsem[i], 16)
        self.dcnt[i] += 16 * len(pairs)
        ev = (i, self.dcnt[i])
        self._reg(ev, reads, writes)
        return ev

    def barrier(self):
        evs = [(e, self.cnt[e]) for e in self.sem if self.cnt[e] > 0]
        evs += [(i, self.dcnt[i]) for i in range(self.NDS) if self.dcnt[i] > 0]
        for eng in self.E:
            for ev in evs:
                if ev[0] != eng:
                    self._wait(eng, ev)

    def tt(self, eng, out, in0, in1, op, reads, writes):
        return self.op(eng, lambda e: e.tensor_tensor(out=out, in0=in0, in1=in1, op=op), reads, writes)

    def ts(self, eng, out, in0, s1, s2, op0, op1, reads, writes):
        if s2 is None:
            return self.op(eng, lambda e: e.tensor_scalar(out=out, in0=in0, scalar1=s1, scalar2=None, op0=op0),
                           reads, writes)
        return self.op(eng, lambda e: e.tensor_scalar(out=out, in0=in0, scalar1=s1, scalar2=s2, op0=op0, op1=op1),
                       reads, writes)

    def stt(self, eng, out, in0, scalar, in1, op0, op1, reads, writes):
        return self.op(eng, lambda e: e.scalar_tensor_tensor(out=out, in0=in0, scalar=scalar, in1=in1, op0=op0,
                                                             op1=op1), reads, writes)

    def act(self, out, in_, func, reads, writes, scale=1.0, bias=None, accum=None):
        kw = {}
        if bias is not None:
            kw['bias'] = bias
        if accum is not None:
            kw['accum_out'] = accum
        return self.op('act', lambda e: e.activation(out=out, in_=in_, func=func, scale=scale, **kw), reads, writes)

    def cp(self, eng, out, in_, reads, writes):
        if eng == 'act':
            return self.op('act', lambda e: e.activation(out=out, in_=in_, func=AF.Copy), reads, writes)
        return self.op(eng, lambda e: e.tensor_copy(out=out, in_=in_), reads, writes)

    def finish(self, q='sp'):
        for i in range(self.NDS):
            if self.dcnt[i] > 0:
                self._wait(q, (i, self.dcnt[i]))


def rsqrt_dve(k, out, outk, v, vk, tmp, tmpk, n):
    t0, t1, t2 = tmp
    k0, k1, k2 = tmpk
    k.op('dve', lambda e: e.tensor_single_scalar(out=t0.bitcast(I32), in_=v.bitcast(I32), scalar=1,
                                                 op=ALU.arith_shift_right), reads=[vk], writes=[k0])
    k.op('dve', lambda e: e.tensor_scalar(out=out.bitcast(I32), in0=t0.bitcast(I32), scalar1=-1,
                                          scalar2=0x5f3759df, op0=ALU.mult, op1=ALU.add),
         reads=[k0], writes=[outk])
    for _ in range(4):
        k.op('dve', lambda e: e.tensor_tensor(out=t1, in0=out, in1=out, op=ALU.mult), reads=[outk], writes=[k1])
        k.op('dve', lambda e: e.tensor_tensor(out=t2, in0=t1, in1=v, op=ALU.mult), reads=[k1, vk], writes=[k2])
        k.op('dve', lambda e: e.tensor_scalar(out=t1, in0=t2, scalar1=-0.5, scalar2=1.5, op0=ALU.mult,
                                              op1=ALU.add), reads=[k2], writes=[k1])
        k.op('dve', lambda e: e.tensor_tensor(out=out, in0=out, in1=t1, op=ALU.mult), reads=[outk, k1],
             writes=[outk])


def norm_transpose(k, pfx, xt, xtk, hn, hnk, hT, hTk, col0, ptr, ptrk, ident, st, stk, eps_eff, evac_eng):
    junk = st['junk']
    k.op('act', lambda e: e.activation(out=junk, in_=xt, func=AF.Square, accum_out=st['ss']),
         reads=[xtk], writes=[stk + 'junk', stk + 'ss'])
    k.op('dve', lambda e: e.tensor_scalar(out=st['v'], in0=st['ss'], scalar1=1.0 / D, scalar2=eps_eff,
                                          op0=ALU.mult, op1=ALU.add), reads=[stk + 'ss'], writes=[stk + 'v'])
    rsqrt_dve(k, st['r'], stk + 'r', st['v'], stk + 'v', (st['t0'], st['t1'], st['t2']),
              (stk + 't0', stk + 't1', stk + 't2'), 1)
    k.op('dve', lambda e: e.tensor_scalar(out=hn, in0=xt, scalar1=st['r'], scalar2=None, op0=ALU.mult),
         reads=[xtk, stk + 'r'], writes=[hnk])
    for kt in range(8):
        k.op('pe', lambda e, kt=kt: e.transpose(ptr[:, kt, :], hn[:, kt * 128:(kt + 1) * 128], ident),
             reads=[hnk], writes=[ptrk], signal=(kt == 7))
    if evac_eng == 'act':
        k.op('act', lambda e: e.activation(out=hT[:, :, col0:col0 + 128], in_=ptr[:, :, :], func=AF.Copy),
             reads=[ptrk], writes=[hTk])
    else:
        k.op(evac_eng, lambda e: e.tensor_copy(out=hT[:, :, col0:col0 + 128], in_=ptr[:, :, :]),
             reads=[ptrk], writes=[hTk])


def load_convert_weight(k, Wdst, Wk, wsrc, nkt, ncols, stage_bufs, stage_keys, scale_col=None,
                        const_scale=None, chunk=512):
    ci = 0
    cwid = min(512, ncols)
    kg = max(1, 1024 // cwid)
    for c0 in range(0, ncols, cwid):
        cw = min(cwid, ncols - c0)
        for k0 in range(0, nkt, kg):
            nk = min(kg, nkt - k0)
            sbuf = stage_bufs[ci % len(stage_bufs)]
            skey = KL(stage_keys[ci % len(stage_bufs)])
            view = sbuf[:, 0:nk * cw].rearrange("p (a b) -> p a b", a=nk)
            k.dma('sp', [(view, wsrc[:, k0:k0 + nk, c0:c0 + cw])], writes=skey)
            o = Wdst[:, k0:k0 + nk, c0:c0 + cw]
            if scale_col is not None:
                bc = scale_col[:, k0:k0 + nk].unsqueeze(2).broadcast_to([128, nk, cw])
                k.op('dve', lambda e, o=o, view=view, bc=bc: e.tensor_tensor(out=o, in0=view, in1=bc, op=ALU.mult),
                     reads=skey, writes=[Wk])
            else:
                cs = 1.0 if const_scale is None else const_scale
                if ci % 2 == 1:
                    k.op('act', lambda e, o=o, view=view: e.activation(out=o, in_=view, func=AF.Copy, scale=cs),
                         reads=skey, writes=[Wk])
                else:
                    k.op('dve', lambda e, o=o, view=view: e.tensor_scalar(out=o, in0=view, scalar1=cs, scalar2=None,
                                                                       op0=ALU.mult), reads=skey, writes=[Wk])
            ci += 1


def KL(x):
    return [x] if isinstance(x, str) else list(x)


def pass3(k, cfg, T):
    nc = k.nc
    with ExitStack() as es:
        def sb(n, s, d=F32):
            return es.enter_context(nc.sbuf_tensor("p3_" + n, s, d))

        def pp(n, s, d=F32):
            return es.enter_context(nc.psum_tensor("p3_" + n, s, d))
        Wup = sb("Wup", [128, 8, 2 * DFF], BF16)
        Wd = sb("Wd", [128, 22, D], BF16)
        tokf = [sb("tok%d" % i, [128, 1028]) for i in range(4)]
        tok = [t[:, 0:D] for t in tokf]
        tokk = [('tok%d' % i, 'tok%db' % i) for i in range(4)]
        hn = [sb("hn%d" % i, [128, D], BF16) for i in range(1)]
        h2T = sb("h2T", [128, 8, 512], BF16)
        gT = sb("gT", [128, 22, 512], BF16)
        gbc = sb("gbc", [128, D])
        g2col = sb("g2col", [128, 8])
        cw = sb("cw", [128, 44, 3])
        cb = sb("cb", [128, 44])
        carry = sb("carry", [128, 22, 2, 2])
        ident = sb("ident", [128, 128], BF16)
        identf = sb("identf", [128, 128])
        extA = sb("extA", [128, 2, 514])
        caccA = sb("caccA", [128, 2, 512])
        tmpA = sb("tmpA", [128, 2, 512])
        stt = sb("stt", [128, 40])
        junk = sb("junk", [128, D], BF16)
        ptr = [pp("ptr0", [128, 8, 128], BF16)]
        pup = [pp("pup%d" % i, [128, 512]) for i in range(4)]
        pdn = [pp("pdn%d" % i, [128, 512]) for i in range(2)]
        EXT = [extA, tokf[2][:, 0:1028].rearrange("p (h x) -> p h x", h=2)]
        EXTK = [('extA_a', 'extA_b'), tokk[2]]
        CAC = [caccA, tokf[3][:, 0:1024].rearrange("p (h x) -> p h x", h=2)]
        CACK = [('cacA_a', 'cacA_b'), tokk[3]]
        TMP = [tmpA, tokf[1][:, 0:1024].rearrange("p (h x) -> p h x", h=2)]
        TMPK = [('tmpA_a', 'tmpA_b'), tokk[1]]

        k.dma('sp', [(gbc[:], T['g_ffn_post_bc'][:, :]), (g2col[:], T['g_ffn_pre_col'][:, :]),
                     (cw[:], T['conv_w_l'][:, :, :]), (cb[:], T['conv_b_l'][:, :]),
                     (identf[:], T['identf'][:, :])], writes=['gbc', 'g2col', 'cw', 'cb', 'identf'])
        k.cp('dve', ident[:], identf[:], ['identf'], ['ident'])
        wup_v = T['w_up'].rearrange("(kt p) c -> p kt c", p=128)
        load_convert_weight(k, Wup, 'Wup', wup_v, 8, 2 * DFF, tok, tokk, scale_col=g2col, chunk=128)
        wd_v = T['w_down'].rearrange("(kt p) c -> p kt c", p=128)
        load_convert_weight(k, Wd, 'Wd', wd_v, 22, D, tok, tokk, chunk=32)

        tokc = 0
        x1s = T['x1s']
        y = T['y']
        starts3 = [off_ + m_ * 512 for (off_, L_) in cfg['seqs'] for m_ in range(L_ // 512)]
        pre3 = set()
        XB = [caccA[:].rearrange("p h x -> p (h x)"), extA[:].rearrange("p h x -> p (h x)")[:, 0:D]]
        XBK = [('cacA_a', 'cacA_b'), ('extA_a', 'extA_b')]
        TB = tmpA[:].rearrange("p h x -> p (h x)")
        TBK = ('tmpA_a', 'tmpA_b')
        for (off, L) in cfg['seqs']:
            nmb = L // 512
            k.op('dve', lambda e: e.memset(carry[:], 0.0), writes=['carry'])
            for mb in range(nmb + 1):
                tail = (mb == nmb)
                wdt = 4 if tail else 512
                t0 = off + mb * 512
                if not tail:
                    prenorm4(k, x1s, t0, tok, tokk, hn, ['p3hn0'], h2T, 'h2T', ptr, ['ptr0'], ident[:], stt, 'p3s_',
                             junk, EPS, skip_load=(0, 1, 2, 3) if t0 in pre3 else ())

                def st_U(j):
                    if tail:
                        return
                    for half in range(2):
                        tl = j + 22 * half
                        pb = pup[(2 * j + half) % 4]
                        pk = 'pup%d' % ((2 * j + half) % 4)
                        for kt in range(8):
                            k.op('pe', lambda e, kt=kt, tl=tl, pb=pb: e.matmul(
                                pb[:, :], lhsT=Wup[:, kt, tl * 128:(tl + 1) * 128], rhs=h2T[:, kt, :],
                                start=(kt == 0), stop=(kt == 7)), reads=['Wup', 'h2T'], writes=[pk],
                                signal=(kt == 7))

                def st_E(j):
                    par = j % 2
                    ex, exk = EXT[par], EXTK[par]
                    ca, cak = CAC[par], CACK[par]
                    for hf in range(2):
                        k.cp(P3_TINY, ex[:, hf, 0:2], carry[:, j, hf, :], ['carry'], [exk[hf]])
                    for half in range(2):
                        tl = j + 22 * half
                        pb = pup[(2 * j + half) % 4]
                        pk = 'pup%d' % ((2 * j + half) % 4)
                        if not tail:
                            k.act(ex[:, half, 2:514], pb[:, :], AF.Copy, [pk], [exk[half]])
                            k.act(ca[:, half, :], pb[:, :], AF.Copy, [pk, 'cw'], [cak[half]], scale=cw[:, tl, 2:3])
                        else:
                            k.op('pool', lambda e, ex=ex, half=half: e.memset(ex[:, half, 2:2 + wdt], 0.0),
                                 writes=[exk[half]])
                            k.act(ca[:, half, 0:wdt], ex[:, half, 2:2 + wdt], AF.Copy, [exk[half], 'cw'], [cak[half]],
                                  scale=cw[:, tl, 2:3])
                    if not tail:
                        for hf in range(2):
                            k.cp(P3_TINY, carry[:, j, hf, :], ex[:, hf, 512:514], [exk[hf]], ['carry'])

                def st_C(j):
                    par = j % 2
                    ex, exk = EXT[par], EXTK[par]
                    ca, cak = CAC[par], CACK[par]
                    for (c0_, wi) in ((0, 0), (1, 1)):
                        for half in range(2):
                            tl = j + 22 * half
                            k.stt('dve', ca[:, half, 0:wdt], ex[:, half, c0_:c0_ + wdt], cw[:, tl, wi:wi + 1],
                                  ca[:, half, 0:wdt], ALU.mult, ALU.add, [exk[half], cak[half], 'cw'], [cak[half]])

                def st_B(j):
                    par = j % 2
                    ca, cak = CAC[par], CACK[par]
                    k.act(ca[:, 0, 0:wdt], ca[:, 0, 0:wdt], AF.Identity, [cak[0], 'cb'], [cak[0]], bias=cb[:, j:j + 1])

                def st_Sq(j):
                    par = j % 2
                    k.act(TMP[par][:, 0, 0:wdt], CAC[par][:, 0, 0:wdt], AF.Square, [CACK[par][0]], [TMPK[par][0]])

                def st_P(j):
                    par = j % 2
                    ta, tak = TMP[par][:, 0, 0:wdt], TMPK[par][0]
                    k.ts(P3_POLY, ta, ta, GC2, GC1, ALU.mult, ALU.add, [tak], [tak])
                    k.tt(P3_POLY, ta, ta, CAC[par][:, 0, 0:wdt], ALU.mult, [tak, CACK[par][0]], [tak])

                def st_T(j):
                    par = j % 2
                    k.act(TMP[par][:, 1, 0:wdt], TMP[par][:, 0, 0:wdt], AF.Sigmoid, [TMPK[par][0]], [TMPK[par][1]])

                def st_G(j):
                    par = j % 2
                    tb2, tbk2 = TMP[par][:, 1, 0:wdt], TMPK[par][1]
                    k.tt(P3_POLY, tb2, tb2, CAC[par][:, 0, 0:wdt], ALU.mult, [tbk2, CACK[par][0]], [tbk2])
                    k.stt('dve', gT[:, j, 0:wdt], CAC[par][:, 1, 0:wdt], cb[:, 22 + j:23 + j], tb2, ALU.add, ALU.mult,
                          [tbk2, CACK[par][1], 'cb'], ['gT'])

                if 'nogelu' in P3_DBG:
                    def st_Sq(j): pass
                    def st_B(j): pass
                    def st_P(j): pass
                    def st_T(j): pass
                    def st_G(j):
                        k.cp('dve', gT[:, j, :], CAC[j % 2][:, 0, :], [CACK[j % 2][0]], ['gT'])
                if 'noconv' in P3_DBG:
                    def st_C(j): pass
                if 'noE' in P3_DBG:
                    def st_E(j): pass
                if 'noU' in P3_DBG:
                    def st_U(j): pass
                for j in range(23):
                    if j >= 1:
                        st_B(j - 1)
                        st_Sq(j - 1)
                    if j < 22:
                        st_U(j)
                        st_E(j)
                    if j >= 1:
                        st_P(j - 1)
                    if j < 22:
                        st_C(j)
                    if j >= 1:
                        st_T(j - 1)
                        st_G(j - 1)

                if not tail and mb + 1 < nmb:
                    nxt = starts3.index(t0) + 1
                    if nxt < len(starts3):
                        tn = starts3[nxt]
                        for i_ in range(4):
                            k.dma('sp', [(tok[i_][:], x1s[tn + i_ * 128:tn + (i_ + 1) * 128, :])], reads=['x1s_dram'],
                                  writes=KL(tokk[i_]))
                        pre3.add(tn)
                ntile = 1 if tail else 4
                if 'nodown' in P3_DBG:
                    ntile = 0
                for i in range(ntile):
                    tk0 = t0 - 1 + i * 128
                    p_lo = 1 if tk0 < off else 0
                    p_hi = 1 if tail else 128
                    xb, xk = XB[tokc % 2], XBK[tokc % 2]
                    tokc += 1
                    tb_, tbk = TB, TBK
                    if i % 2 == 0 or 'pdn' in P3_DBG:
                        pd, pdk = pdn, ['pdn0', 'pdn1']
                    else:
                        pd, pdk = pup[0:2], ['pup0', 'pup1']
                    if 'noload' not in P3_DBG:
                        k.dma('sp', [(xb[p_lo:p_hi, :], x1s[tk0 + p_lo:tk0 + p_hi, :])], reads=['x1s_dram'],
                              writes=list(xk))
                    for h in range(2 if 'nomm' not in P3_DBG else 0):
                        for kt in range(22):
                            k.op('pe', lambda e, kt=kt, h=h, i=i, pd=pd: e.matmul(
                                pd[h][:, :], lhsT=gT[:, kt, i * 128:(i + 1) * 128],
                                rhs=Wd[:, kt, h * 512:(h + 1) * 512], start=(kt == 0), stop=(kt == 21)),
                                reads=['gT', 'Wd'], writes=[pdk[h]], signal=(kt == 21))
                    if 'nonorm' in P3_DBG:
                        continue
                    k.act(junk[:, 0:512], pd[0][:, :], AF.Square, [pdk[0]], ['p3s_junk', 'p3s_a0'], accum=stt[:, 24:25])
                    k.act(junk[:, 512:1024], pd[1][:, :], AF.Square, [pdk[1]], ['p3s_junk', 'p3s_a1'],
                          accum=stt[:, 25:26])
                    for h in range(2):
                        k.tt('dve', tb_[:, h * 512:(h + 1) * 512], pd[h][:, :], gbc[:, h * 512:(h + 1) * 512],
                             ALU.mult, [pdk[h], 'gbc'], list(tbk))
                    k.tt('dve', stt[:, 26:27], stt[:, 24:25], stt[:, 25:26], ALU.add, ['p3s_a0', 'p3s_a1'], ['p3s_a2'])
                    k.ts('dve', stt[:, 27:28], stt[:, 26:27], 1.0 / D, EPS, ALU.mult, ALU.add, ['p3s_a2'],
                         ['p3s_a3'])
                    rsqrt_dve(k, stt[:, 28:29], 'p3s_r2', stt[:, 27:28], 'p3s_a3',
                              (stt[:, 29:30], stt[:, 30:31], stt[:, 31:32]), ('p3s_u0', 'p3s_u1', 'p3s_u2'), 1)
                    k.stt('dve', xb[:, :], tb_[:, :], stt[:, 28:29], xb[:, :], ALU.mult, ALU.add,
                          list(tbk) + ['p3s_r2'] + list(xk), list(xk))
                    if 'nostore' not in P3_DBG:
                        k.dma('pool', [(y[tk0 + p_lo:tk0 + p_hi, :], xb[p_lo:p_hi, :])], reads=list(xk),
                              writes=['y_dram'])


TWO_PI = 2.0 * np.pi


def _table_from(k, pfx, expo, ek, ang, ak, re, rek, im, imk, tmp, tmpk, ti, tik):
    E, f, sc = tmp
    Ek, fk, sck = tmpk
    k.act(E, expo, AF.Exp, [ek], [Ek])
    k.cp('dve', ti, ang, [ak], [tik])
    k.cp('dve', f, ti, [tik], [fk])
    k.tt('dve', f, ang, f, ALU.subtract, [ak, fk], [fk])
    k.act(sc, f, AF.Sin, [fk], [sck], scale=TWO_PI)
    k.tt('dve', im, E, sc, ALU.mult, [Ek, sck], [imk])
    k.ts('dve', f, ang, 0.25, None, ALU.add, None, [ak], [fk])
    k.cp('dve', ti, f, [fk], [tik])
    k.cp('dve', sc, ti, [tik], [sck])
    k.tt('dve', f, f, sc, ALU.subtract, [fk, sck], [fk])
    k.act(sc, f, AF.Sin, [fk], [sck], scale=TWO_PI)
    k.tt('dve', re, E, sc, ALU.mult, [Ek, sck], [rek])


def ssm_prep(k, T, d, C):
    nc = k.nc
    k.barrier()
    with ExitStack() as es:
        def sb(n, s, dt=F32):
            return es.enter_context(nc.sbuf_tensor("sp%d_" % d + n, s, dt))
        N = 2048
        lre = sb("lre", [128, N]); lim = sb("lim", [128, N]); ldt = sb("ldt", [128, N])
        bre = sb("bre", [128, N]); bim = sb("bim", [128, N])
        xr = sb("xr", [128, N]); xi = sb("xi", [128, N])
        are = sb("are", [128, N]); aim = sb("aim", [128, N])
        t0 = sb("t0", [128, N]); t1 = sb("t1", [128, N]); t2 = sb("t2", [128, N]); t3 = ldt
        ti = sb("ti", [128, N], I32)
        sm = sb("sm", [128, 8, 16])
        smi = sb("smi", [128, 16], I32)
        k.dma('sp', [(lre[:], T['lam_re_bp'][d]), (lim[:], T['lam_im_bp'][d]), (ldt[:], T['log_dt_bp'][d]),
                     (bre[:], T['b_reT_pad'][d]), (bim[:], T['b_imT_pad'][d])],
              writes=['p_lre', 'p_lim', 'p_ldt', 'p_bre', 'p_bim'])
        k.dma('sp', [(sm[:, 0, :], T['lam_re_sp'][d]), (sm[:, 1, :], T['lam_im_sp'][d]),
                     (sm[:, 2, :], T['log_dt_sp'][d]), (C['sig'][:], T['nsig'][d]), (C['trow'][:], T['trow'][d]),
                     (C['tri'][:], T['tri'][d])],
              writes=['p_sm', 'c_sig', 'c_trow', 'c_tri'])
        k.act(t0[:], ldt[:], AF.Exp, ['p_ldt'], ['p_t0'])
        k.tt('dve', xr[:], lre[:], t0[:], ALU.mult, ['p_lre', 'p_t0'], ['p_xr'])
        k.tt('dve', xi[:], lim[:], t0[:], ALU.mult, ['p_lim', 'p_t0'], ['p_xi'])
        k.ts('dve', xi[:], xi[:], 1.0 / TWO_PI, None, ALU.mult, None, ['p_xi'], ['p_xi'])
        _table_from(k, 'a', xr[:], 'p_xr', xi[:], 'p_xi', are[:], 'p_are', aim[:], 'p_aim',
                    (t0[:], t1[:], t2[:]), ('p_t0', 'p_t1', 'p_t2'), ti[:], 'p_ti')
        k.ts('dve', t0[:], are[:], -1.0, None, ALU.add, None, ['p_are'], ['p_t0'])
        k.tt('dve', t1[:], lre[:], lre[:], ALU.mult, ['p_lre'], ['p_t1'])
        k.tt('dve', t2[:], lim[:], lim[:], ALU.mult, ['p_lim'], ['p_t2'])
        k.tt('dve', t1[:], t1[:], t2[:], ALU.add, ['p_t1', 'p_t2'], ['p_t1'])
        k.op('dve', lambda e: e.reciprocal(out=t1[:], in_=t1[:]), ['p_t1'], ['p_t1'])
        k.tt('dve', t2[:], t0[:], lre[:], ALU.mult, ['p_t0', 'p_lre'], ['p_t2'])
        k.tt('dve', t3[:], aim[:], lim[:], ALU.mult, ['p_aim', 'p_lim'], ['p_ldt'])
        k.tt('dve', t2[:], t2[:], t3[:], ALU.add, ['p_t2', 'p_ldt'], ['p_t2'])
        k.tt('dve', t2[:], t2[:], t1[:], ALU.mult, ['p_t2', 'p_t1'], ['p_t2'])
        k.tt('dve', t3[:], aim[:], lre[:], ALU.mult, ['p_aim', 'p_lre'], ['p_ldt'])
        k.tt('dve', t0[:], t0[:], lim[:], ALU.mult, ['p_t0', 'p_lim'], ['p_t0'])
        k.tt('dve', t3[:], t3[:], t0[:], ALU.subtract, ['p_ldt', 'p_t0'], ['p_ldt'])
        k.tt('dve', t3[:], t3[:], t1[:], ALU.mult, ['p_ldt', 'p_t1'], ['p_ldt'])
        Bp = C['Bpad']
        v4 = lambda a: a.rearrange("p (c x) -> p c x", c=4)
        k.tt('dve', t0[:], t2[:], bre[:], ALU.mult, ['p_t2', 'p_bre'], ['p_t0'])
        k.tt('dve', t1[:], t3[:], bim[:], ALU.mult, ['p_ldt', 'p_bim'], ['p_t1'])
        k.tt('dve', Bp[:, :, 0, :], v4(t0[:]), v4(t1[:]), ALU.subtract, ['p_t0', 'p_t1'], ['c_Bpad'])
        k.tt('dve', t0[:], t2[:], bim[:], ALU.mult, ['p_t2', 'p_bim'], ['p_t0'])
        k.tt('dve', t1[:], t3[:], bre[:], ALU.mult, ['p_ldt', 'p_bre'], ['p_t1'])
        k.tt('dve', Bp[:, :, 1, :], v4(t0[:]), v4(t1[:]), ALU.add, ['p_t0', 'p_t1'], ['c_Bpad'])
        k.ts('dve', t3[:], xr[:], C['sig'][:, 0:1], None, ALU.mult, None, ['p_xr', 'c_sig'], ['p_ldt'])
        k.ts('dve', bre[:], xi[:], C['sig'][:, 0:1], None, ALU.mult, None, ['p_xi', 'c_sig'], ['p_bre'])
        _table_from(k, 'pre', t3[:], 'p_ldt', bre[:], 'p_bre', C['pre'][:, 0, :], 'c_pre0', C['pre'][:, 1, :],
                    'c_pre1', (t0[:], t1[:], t2[:]), ('p_t0', 'p_t1', 'p_t2'), ti[:], 'p_ti')
        k.act(sm[:, 3, :], sm[:, 2, :], AF.Exp, ['p_sm'], ['p_sm3'])
        k.tt('dve', sm[:, 4, :], sm[:, 0, :], sm[:, 3, :], ALU.mult, ['p_sm', 'p_sm3'], ['p_sm4'])
        k.tt('dve', sm[:, 5, :], sm[:, 1, :], sm[:, 3, :], ALU.mult, ['p_sm', 'p_sm3'], ['p_sm5'])
        k.ts('dve', sm[:, 5, :], sm[:, 5, :], 1.0 / TWO_PI, None, ALU.mult, None, ['p_sm5'], ['p_sm5'])
        _table_from(k, 'as', sm[:, 4, :], 'p_sm4', sm[:, 5, :], 'p_sm5', C['a_sp'][:, 0, :], 'c_asp0',
                    C['a_sp'][:, 1, :], 'c_asp1', (sm[:, 6, :], sm[:, 7, :], sm[:, 3, :]),
                    ('p_sm6', 'p_sm7', 'p_sm3'), smi[:], 'p_smi')
        tr_b = C['trow'][:, :].unsqueeze(1).broadcast_to([128, 16, 128])
        v16 = lambda a: a.rearrange("p (c x) -> p c x", c=16)
        k.tt('dve', v16(t3[:]), tr_b, sm[:, 4, :].unsqueeze(2).broadcast_to([128, 16, 128]), ALU.mult,
             ['c_trow', 'p_sm4'], ['p_ldt'])
        k.tt('dve', v16(bre[:]), tr_b, sm[:, 5, :].unsqueeze(2).broadcast_to([128, 16, 128]), ALU.mult,
             ['c_trow', 'p_sm5'], ['p_bre'])
        _table_from(k, 'post', t3[:], 'p_ldt', bre[:], 'p_bre',
                    C['post'][:, 0].rearrange("p c x -> p (c x)"), 'c_post0',
                    C['post'][:, 1].rearrange("p c x -> p (c x)"), 'c_post1',
                    (t0[:], t1[:], t2[:]), ('p_t0', 'p_t1', 'p_t2'), ti[:], 'p_ti')
        c16 = lambda a: a.rearrange("p (c x) -> p c x", c=16)
        k.dma('sp', [(c16(t0[:]), T['c_reT_pad'][d]), (c16(t1[:]), T['c_imT_pad'][d])], writes=['p_t0', 'p_t1'])
        k.cp('dve', C['Cpad'][:, 0], c16(t0[:]), ['p_t0'], ['c_Cpad'])
        k.ts('dve', C['Cpad'][:, 1], c16(t1[:]), -1.0, None, ALU.mult, None, ['p_t1'], ['c_Cpad'])
        k.cp('dve', C['trib'][:], C['tri'][:], ['c_tri'], ['c_trib'])
        k.barrier()


def ssm_alloc(nc, es, pfx):
    def sb(n, s, dt=F32):
        return es.enter_context(nc.sbuf_tensor(pfx + n, s, dt))
    C = {}
    C['Bpad'] = sb("Bpad", [128, 4, 2, 512], BF16)
    C['pre'] = sb("pre", [128, 2, 2048])
    C['post'] = sb("post", [128, 2, 16, 128])
    C['Cpad'] = sb("Cpad", [128, 2, 16, 128], BF16)
    C['a_sp'] = sb("a_sp", [128, 2, 16])
    C['sig'] = sb("sig", [128, 1])
    C['trow'] = sb("trow", [128, 128])
    C['tri'] = sb("tri", [128, 128])
    C['trib'] = sb("trib", [128, 128], BF16)
    C['zre'] = sb("zre", [128, 2048], BF16)
    C['zim'] = sb("zim", [128, 2048], BF16)
    C['H'] = sb("H", [128, 2, 16, 128], BF16)
    C['cc'] = [sb("cc%d" % i, [128, 2, 16], BF16) for i in range(2)]
    C['ccf'] = sb("ccf", [128, 4, 16])
    C['w'] = [sb("w%d" % i, [128, 512]) for i in range(4)]
    return C


def ssm_block(k, C, uT, uTk, c0, PS, d, blk, first, ident, yps, ypsk, PK=None):
    if PK is None:
        PK = {n: n for n in ('bu0', 'bu1', 'cu0', 'cu1')}
    cin = C['cc'][blk % 2]
    cink = 'c_cc%d' % (blk % 2)
    cout = C['cc'][(blk + 1) % 2]
    coutk = 'c_cc%d' % ((blk + 1) % 2)
    pre = C['pre']
    post = C['post']
    H = C['H']
    w = C['w']
    for ct in range(4):
        for comp in range(2):
            k.op('pe', lambda e, ct=ct, comp=comp: e.matmul(PS['bu%d' % comp][:, :], lhsT=uT[:, ct, c0:c0 + 128],
                                                          rhs=C['Bpad'][:, ct, comp, :], start=True, stop=True),
                 reads=[uTk, 'c_Bpad'], writes=[PK['bu%d' % comp]])
        pr = pre[:, 0, ct * 512:(ct + 1) * 512]
        pi_ = pre[:, 1, ct * 512:(ct + 1) * 512]
        bur = PS['bu0'][:, :]
        bui = PS['bu1'][:, :]
        k.tt('dve', w[0][:], pr, bur, ALU.mult, ['c_pre0', PK['bu0']], ['c_w0'])
        k.tt('dve', w[1][:], pi_, bui, ALU.mult, ['c_pre1', PK['bu1']], ['c_w1'])
        k.tt('dve', w[2][:], pr, bui, ALU.mult, ['c_pre0', PK['bu1']], ['c_w2'])
        k.tt('dve', w[3][:], pi_, bur, ALU.mult, ['c_pre1', PK['bu0']], ['c_w3'])
        k.tt('pool', C['zre'][:, ct * 512:(ct + 1) * 512], w[0][:], w[1][:], ALU.subtract, ['c_w0', 'c_w1'],
             ['c_zre%d' % ct])
        k.tt('pool', C['zim'][:, ct * 512:(ct + 1) * 512], w[2][:], w[3][:], ALU.add, ['c_w2', 'c_w3'],
             ['c_zim%d' % ct])
        for comp, zz, zk in ((0, C['zre'], 'c_zre%d' % ct), (1, C['zim'], 'c_zim%d' % ct)):
            for q in range(4):
                pair = ct * 4 + q
                k.op('pe', lambda e, zz=zz, q=q, ct=ct, comp=comp: e.matmul(
                    PS['cu%d' % comp][:, q * 128:(q + 1) * 128],
                    lhsT=zz[:, ct * 512 + q * 128:ct * 512 + (q + 1) * 128], rhs=C['trib'][:, :],
                    start=True, stop=first), reads=[zk, 'c_trib'], writes=[PK['cu%d' % comp]], signal=first)
                if not first:
                    k.op('pe', lambda e, q=q, pair=pair, comp=comp: e.matmul(
                        PS['cu%d' % comp][:, q * 128:(q + 1) * 128], lhsT=ident,
                        rhs=cin[:, comp, pair:pair + 1].broadcast_to([128, 128]), start=False, stop=True),
                        reads=[cink, 'ident'], writes=[PK['cu%d' % comp]])
        por = post[:, 0, ct * 4:(ct + 1) * 4, :].rearrange("p c x -> p (c x)")
        poi = post[:, 1, ct * 4:(ct + 1) * 4, :].rearrange("p c x -> p (c x)")
        cur = PS['cu0'][:, :]
        cui = PS['cu1'][:, :]
        k.tt('dve', w[0][:], por, cur, ALU.mult, ['c_post0', PK['cu0']], ['c_w0'])
        k.tt('dve', w[1][:], poi, cui, ALU.mult, ['c_post1', PK['cu1']], ['c_w1'])
        k.tt('dve', w[2][:], por, cui, ALU.mult, ['c_post0', PK['cu1']], ['c_w2'])
        k.tt('dve', w[3][:], poi, cur, ALU.mult, ['c_post1', PK['cu0']], ['c_w3'])
        k.tt('pool', H[:, 0, ct * 4:(ct + 1) * 4, :].rearrange("p c x -> p (c x)"), w[0][:], w[1][:], ALU.subtract,
             ['c_w0', 'c_w1'], ['c_H0_%d' % ct])
        k.tt('pool', H[:, 1, ct * 4:(ct + 1) * 4, :].rearrange("p c x -> p (c x)"), w[2][:], w[3][:], ALU.add,
             ['c_w2', 'c_w3'], ['c_H1_%d' % ct])
        n = 0
        for comp in range(2):
            for q in range(4):
                pair = ct * 4 + q
                k.op('pe', lambda e, comp=comp, pair=pair, ct=ct, n=n: e.matmul(
                    yps[:, ct, :], lhsT=C['Cpad'][:, comp, pair, :], rhs=H[:, comp, pair, :],
                    start=(n == 0), stop=(n == 7)), reads=['c_Cpad', 'c_H%d_%d' % (comp, ct)], writes=[ypsk],
                    signal=(n == 7))
                n += 1
    last = 127 if d == 0 else 0
    hk = ['c_H%d_%d' % (comp, ct) for comp in range(2) for ct in range(4)]
    hr = H[:, 0, :, last]
    hi = H[:, 1, :, last]
    ar = C['a_sp'][:, 0, :]
    ai = C['a_sp'][:, 1, :]
    f = C['ccf']
    k.tt('pool', f[:, 0, :], ar, hr, ALU.mult, ['c_asp0'] + hk, ['c_ccf0'])
    k.tt('pool', f[:, 1, :], ai, hi, ALU.mult, ['c_asp1'] + hk, ['c_ccf1'])
    k.tt('pool', f[:, 2, :], ar, hi, ALU.mult, ['c_asp0'] + hk, ['c_ccf2'])
    k.tt('pool', f[:, 3, :], ai, hr, ALU.mult, ['c_asp1'] + hk, ['c_ccf3'])
    k.tt('pool', cout[:, 0, :], f[:, 0, :], f[:, 1, :], ALU.subtract, ['c_ccf0', 'c_ccf1'], [coutk])
    k.tt('pool', cout[:, 1, :], f[:, 2, :], f[:, 3, :], ALU.add, ['c_ccf2', 'c_ccf3'], [coutk])


def ssm_host_layouts(lam_re, lam_im, log_dt, b_re, b_im, c_re, c_im):
    f = np.float32
    out = {}
    out['lam_re_bp'] = np.ascontiguousarray(np.broadcast_to(lam_re.reshape(2, 1, 2048), (2, 128, 2048))).astype(f)
    out['lam_im_bp'] = np.ascontiguousarray(np.broadcast_to(lam_im.reshape(2, 1, 2048), (2, 128, 2048))).astype(f)
    out['log_dt_bp'] = np.ascontiguousarray(np.broadcast_to(np.repeat(log_dt, 64, axis=1).reshape(2, 1, 2048),
                                                            (2, 128, 2048))).astype(f)
    bre = np.zeros((2, 128, 4, 8, 64), f)
    bim = np.zeros((2, 128, 4, 8, 64), f)
    for d in range(2):
        for ct in range(4):
            for gl in range(8):
                bre[d, gl * 16:(gl + 1) * 16, ct, gl, :] = b_re[d, ct * 8 + gl].T
                bim[d, gl * 16:(gl + 1) * 16, ct, gl, :] = b_im[d, ct * 8 + gl].T
    out['b_reT_pad'] = bre.reshape(2, 128, 2048)
    out['b_imT_pad'] = bim.reshape(2, 128, 2048)
    out['lam_re_sp'] = np.ascontiguousarray(lam_re.reshape(2, 16, 2, 64).transpose(0, 2, 3, 1).reshape(2, 128, 16))
    out['lam_im_sp'] = np.ascontiguousarray(lam_im.reshape(2, 16, 2, 64).transpose(0, 2, 3, 1).reshape(2, 128, 16))
    ld = log_dt.reshape(2, 16, 2).transpose(0, 2, 1)
    out['log_dt_sp'] = np.ascontiguousarray(np.repeat(ld[:, :, None, :], 64, axis=2).reshape(2, 128, 16)).astype(f)
    cre = np.zeros((2, 128, 16, 128), f)
    cim = np.zeros((2, 128, 16, 128), f)
    for d in range(2):
        for g in range(32):
            pair, g2 = g // 2, g % 2
            col = (g % 8) * 16
            cre[d, g2 * 64:(g2 + 1) * 64, pair, col:col + 16] = c_re[d, g].T
            cim[d, g2 * 64:(g2 + 1) * 64, pair, col:col + 16] = c_im[d, g].T
    out['c_reT_pad'] = cre
    out['c_imT_pad'] = cim
    sig = np.arange(128, dtype=f)
    out['nsig'] = np.stack([-sig, -(127 - sig)]).reshape(2, 128, 1).astype(f)
    tr = np.broadcast_to(sig[None, :], (128, 128))
    out['trow'] = np.ascontiguousarray(np.stack([tr, 127 - tr])).astype(f)
    ii = np.arange(128)
    out['tri'] = np.stack([(ii[:, None] <= ii[None, :]), (ii[:, None] >= ii[None, :])]).astype(f)
    return out


SSM_SHAPES = {'lam_re_bp': [2, 128, 2048], 'lam_im_bp': [2, 128, 2048], 'log_dt_bp': [2, 128, 2048],
              'b_reT_pad': [2, 128, 2048], 'b_imT_pad': [2, 128, 2048], 'lam_re_sp': [2, 128, 16],
              'lam_im_sp': [2, 128, 16], 'log_dt_sp': [2, 128, 16], 'c_reT_pad': [2, 128, 16, 128],
              'c_imT_pad': [2, 128, 16, 128], 'nsig': [2, 128, 1], 'trow': [2, 128, 128], 'tri': [2, 128, 128]}


def prenorm4(k, src, t0, xt, xtk, hn, hnk, hT, hTk, ptr, ptrk, ident, stt, sk, junk, eps_eff, skip_load=()):
    for i in range(4):
        if i not in skip_load:
            k.dma('sp', [(xt[i][:], src[t0 + i * 128:t0 + (i + 1) * 128, :])], writes=KL(xtk[i]))
        k.act(junk[:], xt[i][:], AF.Square, KL(xtk[i]), [sk + 'junk', sk + 'ss%d' % i], accum=stt[:, i:i + 1])
    k.ts('dve', stt[:, 4:8], stt[:, 0:4], 1.0 / D, eps_eff, ALU.mult, ALU.add, [sk + 'ss%d' % i for i in range(4)],
         [sk + 'v'])
    rsqrt_dve(k, stt[:, 8:12], sk + 'r', stt[:, 4:8], sk + 'v', (stt[:, 12:16], stt[:, 16:20], stt[:, 20:24]),
              (sk + 't0', sk + 't1', sk + 't2'), 4)
    for i in range(4):
        h = hn[i % len(hn)]
        hk = hnk[i % len(hn)]
        k.ts('dve', h[:], xt[i][:], stt[:, 8 + i:9 + i], None, ALU.mult, None, KL(xtk[i]) + [sk + 'r'], [hk])
        p = ptr[i % len(ptr)]
        pk = ptrk[i % len(ptr)]
        for kt in range(8):
            k.op('pe', lambda e, kt=kt, p=p, h=h: e.transpose(p[:, kt, :], h[:, kt * 128:(kt + 1) * 128], ident),
                 reads=[hk], writes=[pk], signal=(kt == 7))
        k.cp('act' if i % 2 == 0 else 'dve', hT[:, :, i * 128:(i + 1) * 128], p[:, :, :], [pk], [hTk])


def att_rows(R):
    out = []
    for r in range(R):
        rs = min(max(r - 4, 0), R - 8)
        pr0, pr1 = rs // 2, (rs + 7) // 2
        lst = []
        for pr in range(pr0, pr1 + 1):
            lo_ok = (2 * pr >= rs)
            hi_ok = (2 * pr + 1 <= rs + 7)
            drf = 2 * pr - r
            if lo_ok and hi_ok:
                lst.append((pr, drf + 7))
            elif not lo_ok:
                assert drf == -5
                lst.append((pr, 14))
            else:
                assert drf == 3
                lst.append((pr, 18))
        out.append(lst)
    return out


def att_table_host(rpb):
    tbl = np.full((128, 8, 19, 64), NEG, np.float32)
    qc = np.arange(64)
    cs = np.clip(qc - 8, 0, 48)
    entries = [(e - 7, True, True) for e in range(14)] + [(-5, False, True), (-3, True, True), (-1, True, True),
                                                          (1, True, True), (3, True, False)]
    for e, (drf, lo, hi) in enumerate(entries):
        for half, ok in ((0, lo), (1, hi)):
            if not ok:
                continue
            dr = drf + half
            for kc in range(64):
                okc = (kc >= cs) & (kc < cs + 16)
                dc = kc - qc + 15
                vals = rpb[:, dr + 7, np.clip(dc, 0, 30)]
                tbl[half * 64 + kc, :, e, :] = np.where(okc[None, :], vals, NEG)
    return tbl


def pass1(k, cfg, T):
    nc = k.nc
    with ExitStack() as es:
        def sb(n, s, d=F32):
            return es.enter_context(nc.sbuf_tensor("p1_" + n, s, d))

        def pp(n, s, d=F32):
            return es.enter_context(nc.psum_tensor("p1_" + n, s, d))
        C = ssm_alloc(nc, es, "p1s_")
        ssm_prep(k, T, 0, C)
        W = sb("W", [128, 8, 2048], BF16)
        tbl = sb("tbl", [128, 8, 19, 64], BF16)
        xt = [sb("xt%d" % i, [128, D]) for i in range(4)]
        xtk = ['p1xt%d' % i for i in range(4)]
        hn = [sb("hn%d" % i, [128, D], BF16) for i in range(2)]
        hT = sb("hT", [128, 8, 512], BF16)
        qT = [sb("qT%d" % i, [128, 4, 512], BF16) for i in range(2)]
        kT = [sb("kT%d" % i, [128, 4, 512], BF16) for i in range(3)]
        va = [sb("va%d" % i, [128, 4, 8, 65], BF16) for i in range(3)]
        uT = sb("uT", [128, 4, 512], BF16)
        attT = sb("attT", [128, 4, 512], BF16)
        att = sb("att", [128, 512], BF16)
        sadd = [sb("sadd%d" % i, [128, 320]) for i in range(2)]
        pT = [sb("pT%d" % i, [128, 320], BF16) for i in range(2)]
        rec = sb("rec", [128, 8])
        yfb = [sb("yfb%d" % i, [128, 4, 128]) for i in range(2)]
        g1col = sb("g1col", [128, 8])
        ident = sb("ident", [128, 128], BF16)
        identf = sb("identf", [128, 128])
        stt = sb("stt", [128, 24])
        junk = sb("junk", [128, D], BF16)
        B0 = [pp("B0_%d" % i, [128, 8, 128], BF16) for i in range(1)]
        B1 = pp("B1", [128, 512])
        B2 = pp("B2", [128, 512])
        B3 = pp("B3", [128, 512])
        B4 = pp("B4", [128, 512])
        B5 = pp("B5", [128, 512])
        B6 = pp("B6", [128, 512])
        B7 = pp("B7", [128, 4, 128])
        PS = {'bu0': B3, 'bu1': B4, 'cu0': B5, 'cu1': B6}
        PK = {'bu0': 'B3', 'bu1': 'B4', 'cu0': 'B5', 'cu1': 'B6'}
        B7K = ['B7_%d' % i for i in range(4)]
        WH = [xt[2][:, 0:512], xt[2][:, 512:1024], xt[3][:, 0:512], xt[3][:, 512:1024]]
        WHK = [xtk[2], xtk[2] + 'b', xtk[3], xtk[3] + 'b']

        k.dma('sp', [(g1col[:], T['g_mix_pre_col'][:, :]), (identf[:], T['identf'][:, :])],
              writes=['p1g1col', 'p1identf'])
        k.cp('dve', ident[:], identf[:], ['p1identf'], ['ident'])
        win_v = T['w_in'].rearrange("(kt p) c -> p kt c", p=128)
        load_convert_weight(k, W, 'p1W', win_v[:, :, 0:2048], 8, 2048, [t[:] for t in xt], xtk, scale_col=g1col,
                            chunk=128)
        tb_v = T['att_tbl'].rearrange("p h e q -> p (h e q)")
        tb_d = tbl[:].rearrange("p h e q -> p (h e q)")
        for c0 in range(0, 8 * 19 * 64, 1024):
            cwid = min(1024, 8 * 19 * 64 - c0)
            i = (c0 // 1024) % 4
            k.dma('sp', [(xt[i][:, 0:cwid], tb_v[:, c0:c0 + cwid])], writes=[xtk[i]])
            k.cp('dve', tb_d[:, c0:c0 + cwid], xt[i][:, 0:cwid], [xtk[i]], ['p1tbl'])
        for i in range(3):
            k.op('dve', lambda e, i=i: e.memset(va[i][:], 1.0), writes=['va%d' % i])

        def make_att_items(am, off, rows):
            items = []
            q_ = qT[am % 2]
            qk = 'qT%d' % (am % 2)
            ta0 = off + am * 512
            state = {'ui': 0}

            SB = [(B1, 'B1'), (B0[0][:].rearrange("p a b -> p (a b)").bitcast(F32), 'B0')]

            def unit_s(tile, lr2, h):
                lr = tile * 2 + lr2
                lst = rows[am * 8 + lr]
                na = len(lst)
                ct, hp = h // 2, h % 2
                ps0, ps1 = 64 * hp, 64 * hp + 64
                ui = state['ui']
                state['ui'] += 1
                sbk, sk_ = SB[ui % 2]
                for a, (pr, ent) in enumerate(lst):
                    pm, pt = pr // 4, pr % 4
                    kt_ = kT[pm % 3]
                    k.op('pe', lambda e, a=a, kt_=kt_, pt=pt: e.matmul(
                        sbk[:, a * 64:(a + 1) * 64], lhsT=kt_[ps0:ps1, ct, pt * 128:(pt + 1) * 128],
                        rhs=q_[ps0:ps1, ct, lr * 64:(lr + 1) * 64], start=True, stop=True),
                        reads=['kT%d' % (pm % 3), qk], writes=[sk_], signal=(a == na - 1))
                sa = sadd[ui % 2]
                sak = 'sadd%d' % (ui % 2)
                e0 = lst[0][1]
                tv = tbl[:, h, e0:e0 + 7:2, :] if na == 4 else tbl[:, h, 14:19, :]
                k.tt('dve', sa[:, 0:na * 64].rearrange("p (a q) -> p a q", a=na),
                     sbk[:, 0:na * 64].rearrange("p (a q) -> p a q", a=na), tv, ALU.add, [sk_, 'p1tbl'], [sak])
                pt_ = pT[ui % 2]
                ptk = 'pT%d' % (ui % 2)
                k.act(pt_[:, 0:na * 64], sa[:, 0:na * 64], AF.Exp, [sak], [ptk])
                return (lr2, h, lst, pt_, ptk)

            def unit_pv(info):
                lr2, h, lst, pt_, ptk = info
                na = len(lst)
                hh = h % 4
                for a, (pr, ent) in enumerate(lst):
                    pm, pt = pr // 4, pr % 4
                    k.op('pe', lambda e, a=a, pm=pm, pt=pt: e.matmul(
                        B2[64 * lr2:64 * lr2 + 64, hh * 65:hh * 65 + 65],
                        lhsT=pt_[:, a * 64:(a + 1) * 64], rhs=va[pm % 3][:, pt, h, :],
                        start=(a == 0), stop=(a == na - 1)),
                        reads=[ptk, 'va%d' % (pm % 3)], writes=['B2'], signal=(a == na - 1))

            def norm(g):
                ov = B2[:, 0:260].rearrange("p (h d) -> p h d", h=4)
                k.op('dve', lambda e: e.reciprocal(out=rec[:, g * 4:(g + 1) * 4], in_=ov[:, :, 64]),
                     reads=['B2'], writes=['rec%d' % g])
                k.tt('dve', att[:, g * 256:(g + 1) * 256].rearrange("p (h d) -> p h d", h=4),
                     ov[:, :, 0:64], rec[:, g * 4:(g + 1) * 4].unsqueeze(2).broadcast_to([128, 4, 64]),
                     ALU.mult, ['B2', 'rec%d' % g], ['att'])

            def trans(tile):
                for ct in range(4):
                    k.op('pe', lambda e, ct=ct: e.transpose(B0[0][:, ct, :], att[:, ct * 128:(ct + 1) * 128],
                                                            ident[:]),
                         reads=['att'], writes=['B0'], signal=(ct == 3))
                k.cp('act', attT[:, :, tile * 128:(tile + 1) * 128], B0[0][:, 0:4, :], ['B0'], ['attT'])

            def spill():
                k.dma('pool', [(T['attT_s'][ct, :, ta0:ta0 + 512], attT[:, ct, :]) for ct in range(4)],
                      reads=['attT'], writes=['attT_s'])

            seq = []
            for tile in range(4):
                for g in range(2):
                    for lr2 in range(2):
                        for hh in range(4):
                            seq.append(('u', tile, lr2, g * 4 + hh))
                    seq.append(('n', g))
                seq.append(('t', tile))
            pend = {'info': None, 'post': []}

            def step(ent_):
                if ent_[0] == 'u':
                    info = unit_s(ent_[1], ent_[2], ent_[3])
                    flush()
                    pend['info'] = info
                else:
                    pend['post'].append(ent_)

            def flush():
                if pend['info'] is not None:
                    unit_pv(pend['info'])
                    pend['info'] = None
                for p_ in pend['post']:
                    if p_[0] == 'n':
                        norm(p_[1])
                    else:
                        trans(p_[1])
                pend['post'] = []

            for ent_ in seq:
                items.append(lambda ent_=ent_: step(ent_))
            items.append(flush)
            items.append(spill)
            return items

        x = T['x']
        starts1 = [off_ + m_ * 512 for (off_, L_) in cfg['seqs'] for m_ in range(L_ // 512)]
        pre1 = set()
        pre1b = set()
        for (off, L) in cfg['seqs']:
            R = L // 64
            nmb = L // 512
            rows = att_rows(R)
            att_items = []
            for mb in range(nmb + 1):
                t0 = off + mb * 512
                if mb < nmb:
                    xk2 = [(q_, q_ + 'b') for q_ in xtk]
                    prenorm4(k, x, t0, xt, xk2, hn, ['p1hn0', 'p1hn1'], hT, 'p1hT', B0, ['B0'], ident[:], stt,
                             'p1s_', junk, EPS, skip_load=((0, 1) if t0 in pre1 else ()) + ((2, 3) if t0 in pre1b else ()))
                    nxt = starts1.index(t0) + 1
                    if nxt < len(starts1):
                        tn = starts1[nxt]
                        for i_ in range(2):
                            k.dma('sp', [(xt[i_][:], x[tn + i_ * 128:tn + (i_ + 1) * 128, :])], writes=KL(xk2[i_]))
                        pre1.add(tn)
                    q_ = qT[mb % 2]
                    qk = 'qT%d' % (mb % 2)
                    k_ = kT[mb % 3]
                    kk = 'kT%d' % (mb % 3)
                    v_ = va[mb % 3]
                    vk = 'va%d' % (mb % 3)
                    nacc = 0
                    for grp, dst, dk, c0w, scale in ((0, q_, qk, 0, 0.125), (1, k_, kk, 512, 1.0),
                                                      (2, uT, 'uT', 1536, 1.0)):
                        for j in range(4):
                            pb, pbk = (B1, 'B1') if nacc % 2 == 0 else (B2, 'B2')
                            nacc += 1
                            for kt in range(8):
                                k.op('pe', lambda e, kt=kt, j=j, pb=pb, c0w=c0w: e.matmul(
                                    pb[:, :], lhsT=W[:, kt, c0w + j * 128:c0w + (j + 1) * 128], rhs=hT[:, kt, :],
                                    start=(kt == 0), stop=(kt == 7)), reads=['p1W', 'p1hT'], writes=[pbk],
                                    signal=(kt == 7))
                            if nacc % 2 == 0:
                                k.act(dst[:, j, :], pb[:, :], AF.Copy, [pbk], [dk], scale=scale)
                            else:
                                k.ts('dve', dst[:, j, :], pb[:, :], scale, None, ALU.mult, None, [pbk], [dk])
                    for i in range(4):
                        pb, pbk = (B1, 'B1') if nacc % 2 == 0 else (B2, 'B2')
                        nacc += 1
                        for kt in range(8):
                            k.op('pe', lambda e, kt=kt, i=i, pb=pb: e.matmul(
                                pb[:, :], lhsT=hT[:, kt, i * 128:(i + 1) * 128], rhs=W[:, kt, 1024:1536],
                                start=(kt == 0), stop=(kt == 7)), reads=['p1W', 'p1hT'], writes=[pbk],
                                signal=(kt == 7))
                        k.cp('act' if i % 2 == 0 else 'dve', v_[:, i, :, 0:64],
                             pb[:, :].rearrange("p (h d) -> p h d", h=8), [pbk], [vk])
                    k.dma('pool', [(T['uT_s'][ct, :, t0:t0 + 512], uT[:, ct, :]) for ct in range(4)], reads=['uT'],
                          writes=['uT_s'])
                if mb < nmb:
                    def on_blk(gb, c0, t0=t0):
                        b = c0 // 128
                        yb_ = yfb[b % 2]
                        ybk = 'yfb%d' % (b % 2)
                        k.cp('act', yb_[:], B7[:, :, :], B7K, [ybk])
                        k.dma('pool', [(T['yf_s'][ct, :, t0 + c0:t0 + c0 + 128], yb_[:, ct, :]) for ct in range(4)],
                              reads=[ybk], writes=['yf_s'])
                    ssm_mb(k, C, uT, 'uT', PS, PK, 0, [(mb * 4 + b, b * 128) for b in range(4)], ident[:], B7, B7K,
                           WH, WHK, on_blk, filler=att_items)
                    nxt = starts1.index(t0) + 1
                    if nxt < len(starts1):
                        tn = starts1[nxt]
                        for i_ in (2, 3):
                            k.dma('sp', [(xt[i_][:], x[tn + i_ * 128:tn + (i_ + 1) * 128, :])], writes=KL(xk2[i_]))
                        pre1b.add(tn)
                while att_items:
                    att_items.pop(0)()
                if mb < nmb:
                    att_items = make_att_items(mb, off, rows)
    k.barrier()


def pass2(k, cfg, T):
    nc = k.nc
    with ExitStack() as es:
        def sb(n, s, d=F32):
            return es.enter_context(nc.sbuf_tensor("p2_" + n, s, d))

        def pp(n, s, d=F32):
            return es.enter_context(nc.psum_tensor("p2_" + n, s, d))
        C = ssm_alloc(nc, es, "p2s_")
        ssm_prep(k, T, 1, C)
        Wg = sb("Wg", [128, 8, 2048], BF16)
        Wba = sb("Wba", [128, 4, D], BF16)
        Wbs = sb("Wbs", [128, 4, D], BF16)
        Wout = sb("Wout", [128, 8, D], BF16)
        Wglu = sb("Wglu", [128, 4, 512], BF16)
        xt = [sb("xt%d" % i, [128, D]) for i in range(4)]
        xtk = ['p2xt%d' % i for i in range(4)]
        hn = [sb("hn0", [128, D], BF16)]
        hT = sb("hT", [128, 8, 512], BF16)
        uT = sb("uT", [128, 4, 512], BF16)
        attT = sb("attT", [128, 4, 512], BF16)
        ssmT = sb("ssmT", [128, 4, 512], BF16)
        mT = sb("mT", [128, 8, 512], BF16)
        yfb = sb("yfb", [128, 4, 128])
        yw = [sb("yw%d" % i, [128, 4, 128]) for i in range(3)]
        ygT = sb("ygT", [128, 4, 128], BF16)
        ybuf = sb("ybuf", [128, 4, 128])
        gbc = sb("gbc", [128, D])
        g1col = sb("g1col", [128, 8])
        dcol = sb("dcol", [128, 4])
        bglu = sb("bglu", [128, 4])
        ident = sb("ident", [128, 128], BF16)
        identf = sb("identf", [128, 128])
        stt = sb("stt", [128, 32])
        junk = sb("junk", [128, D], BF16)
        B0 = [pp("B0", [128, 8, 128], BF16)]
        B1 = pp("B1", [128, 512])
        B2 = pp("B2", [128, 512])
        B3 = pp("B3", [128, 512])
        B4 = pp("B4", [128, 512])
        B5 = pp("B5", [128, 512])
        B6 = pp("B6", [128, 512])
        B7 = pp("B7", [128, 4, 128])
        PS = {'bu0': B3, 'bu1': B4, 'cu0': B5, 'cu1': B6}
        PK = {'bu0': 'B3', 'bu1': 'B4', 'cu0': 'B5', 'cu1': 'B6'}
        B7K = ['B7_%d' % i for i in range(4)]
        B0f = B0[0][:].rearrange("p a b -> p (a b)").bitcast(F32)
        WH = [xt[2][:, 0:512], xt[2][:, 512:1024], xt[3][:, 0:512], xt[3][:, 512:1024]]
        WHK = [xtk[2], xtk[2] + 'b', xtk[3], xtk[3] + 'b']

        k.dma('sp', [(g1col[:], T['g_mix_pre_col'][:, :]), (identf[:], T['identf'][:, :]),
                     (gbc[:], T['g_mix_post_bc'][:, :]), (dcol[:], T['ssm_d_col'][:, :]),
                     (bglu[:], T['b_glu_col'][:, :])],
              writes=['p2g1col', 'p2identf', 'p2gbc', 'p2dcol', 'p2bglu'])
        k.cp('dve', ident[:], identf[:], ['p2identf'], ['ident'])
        k.ts('dve', bglu[:], bglu[:], 0.5, None, ALU.mult, None, ['p2bglu'], ['p2bglu'])
        win_v = T['w_in'].rearrange("(kt p) c -> p kt c", p=128)
        xs = [t[:] for t in xt]
        load_convert_weight(k, Wg, 'p2Wg', win_v[:, :, 2048:4096], 8, 2048, xs, xtk, scale_col=g1col, chunk=128)
        load_convert_weight(k, Wba, 'p2Wba', T['w_branch_att'].rearrange("(kt p) c -> p kt c", p=128), 4, D, xs,
                            xtk, chunk=256)
        load_convert_weight(k, Wbs, 'p2Wbs', T['w_branch_ssm'].rearrange("(kt p) c -> p kt c", p=128), 4, D, xs,
                            xtk, const_scale=0.25, chunk=256)
        load_convert_weight(k, Wout, 'p2Wout', T['w_out'].rearrange("(kt p) c -> p kt c", p=128), 8, D, xs, xtk,
                            chunk=128)
        load_convert_weight(k, Wglu, 'p2Wglu', T['w_glu'].rearrange("(kt p) c -> p kt c", p=128), 4, 512, xs, xtk,
                            chunk=256)
        x = T['x']
        starts2 = [off_ + m_ * 512 for (off_, L_) in cfg['seqs'] for m_ in reversed(range(L_ // 512))]
        pre2 = set()
        for (off, L) in cfg['seqs']:
            nmb = L // 512
            for mi, mb in enumerate(reversed(range(nmb))):
                t0 = off + mb * 512
                xk2 = [(q_, q_ + 'b') for q_ in xtk]
                prenorm4(k, x, t0, xt, xk2, hn, ['p2hn0'], hT, 'p2hT', B0, ['B0'], ident[:], stt, 'p2s_',
                         junk, EPS, skip_load=(0, 1, 2, 3) if t0 in pre2 else ())
                k.dma('sp', [(uT[:, ct, :], T['uT_s'][ct, :, t0:t0 + 512]) for ct in range(4)], reads=['uT_s'],
                      writes=['uT'])
                k.dma('sp', [(attT[:, ct, :], T['attT_s'][ct, :, t0:t0 + 512]) for ct in range(4)],
                      reads=['attT_s'], writes=['attT'])
                fill = []

                def on_blk(gb, c0, t0=t0):
                    y0, y1, y2 = yw
                    pend_ = [f for f in fill if getattr(f, 'chain', False)]
                    for f in pend_:
                        fill.remove(f)
                        f()
                    it = []
                    k.cp('act', ybuf[:], B7[:, :, :], B7K, ['ybuf'])
                    it.append(lambda: k.dma('sp', [(yfb[:, ct, :], T['yf_s'][ct, :, t0 + c0:t0 + c0 + 128])
                                                   for ct in range(4)], reads=['yf_s'], writes=['yfb']))
                    it.append(lambda: k.tt('dve', y0[:], uT[:, :, c0:c0 + 128],
                                           dcol[:, :].unsqueeze(2).broadcast_to([128, 4, 128]), ALU.mult,
                                           ['uT', 'p2dcol'], ['yw0']))
                    it.append(lambda: k.tt('pool', y0[:], y0[:], yfb[:], ALU.add, ['yw0', 'yfb'], ['yw0']))
                    it.append(lambda: k.tt('dve', y0[:], y0[:], ybuf[:], ALU.add, ['yw0', 'ybuf'], ['yw0']))
                    it.append(lambda: k.act(y1[:], y0[:], AF.Square, ['yw0'], ['yw1']))
                    it.append(lambda: k.ts('pool', y1[:], y1[:], GC2, GC1, ALU.mult, ALU.add, ['yw1'], ['yw1']))
                    it.append(lambda: k.tt('pool', y1[:], y1[:], y0[:], ALU.mult, ['yw1', 'yw0'], ['yw1']))
                    it.append(lambda: k.act(y2[:], y1[:], AF.Tanh, ['yw1'], ['yw2'], scale=0.5))
                    it.append(lambda: k.stt('dve', y1[:], y2[:], 1.0, y0[:], ALU.add, ALU.mult, ['yw2', 'yw0'],
                                            ['yw1']))
                    it.append(lambda: k.cp('pool', ygT[:], y1[:], ['yw1'], ['ygT']))

                    def glu_mm():
                        for cto in range(4):
                            for kt in range(4):
                                k.op('pe', lambda e, cto=cto, kt=kt: e.matmul(
                                    B0f[:, cto * 128:(cto + 1) * 128], lhsT=Wglu[:, kt, cto * 128:(cto + 1) * 128],
                                    rhs=ygT[:, kt, :], start=(kt == 0), stop=(kt == 3)), reads=['p2Wglu', 'ygT'],
                                    writes=['B0'], signal=(kt == 3))
                    it.append(glu_mm)

                    def glu_tanh():
                        for cto in range(4):
                            k.act(y2[:, cto, :], B0f[:, cto * 128:(cto + 1) * 128], AF.Tanh, ['B0', 'p2bglu'], ['yw2'],
                                  scale=0.25, bias=bglu[:, cto:cto + 1])
                    it.append(glu_tanh)
                    it.append(lambda: k.stt('dve', ssmT[:, :, c0:c0 + 128], y2[:], 1.0, y1[:], ALU.add, ALU.mult,
                                            ['yw2', 'yw1'], ['ssmT']))
                    for f in it:
                        f.chain = True
                    fill[0:0] = it

                def gate_a(j):
                    for kt in range(8):
                        k.op('pe', lambda e, kt=kt: e.matmul(B1[:, :], lhsT=Wg[:, kt, j * 128:(j + 1) * 128],
                                                          rhs=hT[:, kt, :], start=(kt == 0), stop=(kt == 7)),
                             reads=['p2Wg', 'p2hT'], writes=['B1'], signal=(kt == 7))
                    for kt in range(4):
                        k.op('pe', lambda e, kt=kt: e.matmul(B2[:, :], lhsT=Wba[:, kt, j * 128:(j + 1) * 128],
                                                          rhs=attT[:, kt, :], start=(kt == 0), stop=(kt == 3)),
                             reads=['p2Wba', 'attT'], writes=['B2'], signal=(kt == 3))
                    ga = xt[0][:, 0:512]
                    k.act(ga, B1[:, :], AF.Tanh, ['B1'], [xtk[0]], scale=0.5)
                    k.stt('dve', mT[:, j, :], ga, 1.0, B2[:, :], ALU.add, ALU.mult, [xtk[0], 'B2'], ['mT'])
                fill.extend([(lambda j=j: gate_a(j)) for j in range(8)])
                ssm_mb(k, C, uT, 'uT', PS, PK, 1, [(mi * 4 + bi, b * 128) for bi, b in enumerate(reversed(range(4)))],
                       ident[:], B7, B7K, WH, WHK, on_blk, filler=fill)
                while fill:
                    fill.pop(0)()
                for j in range(8):
                    pa, pak = (B1, 'B1') if j % 2 == 0 else (B3, 'B3')
                    pb_, pbk = (B2, 'B2') if j % 2 == 0 else (B4, 'B4')
                    for kt in range(8):
                        k.op('pe', lambda e, kt=kt, j=j, pa=pa: e.matmul(
                            pa[:, :], lhsT=Wg[:, kt, 1024 + j * 128:1024 + (j + 1) * 128], rhs=hT[:, kt, :],
                            start=(kt == 0), stop=(kt == 7)), reads=['p2Wg', 'p2hT'], writes=[pak], signal=(kt == 7))
                    for kt in range(4):
                        k.op('pe', lambda e, kt=kt, j=j, pb_=pb_: e.matmul(
                            pb_[:, :], lhsT=Wbs[:, kt, j * 128:(j + 1) * 128], rhs=ssmT[:, kt, :],
                            start=(kt == 0), stop=(kt == 3)), reads=['p2Wbs', 'ssmT'], writes=[pbk], signal=(kt == 3))
                    gs = xt[0][:, (j % 2) * 512:(j % 2 + 1) * 512]
                    gsk = xtk[0] if j % 2 == 0 else xtk[0] + 'b'
                    m2 = xt[1][:, (j % 2) * 512:(j % 2 + 1) * 512]
                    m2k = xtk[1] if j % 2 == 0 else xtk[1] + 'b'
                    k.act(gs, pa[:, :], AF.Tanh, [pak], [gsk], scale=0.5)
                    k.stt('dve', m2, gs, 1.0, pb_[:, :], ALU.add, ALU.mult, [gsk, pbk], [m2k])
                    k.tt('pool', mT[:, j, :], mT[:, j, :], m2, ALU.add, ['mT', m2k], ['mT'])
                nxt = starts2.index(t0) + 1
                if nxt < len(starts2):
                    tn = starts2[nxt]
                    for i_ in range(4):
                        k.dma('sp', [(xt[i_][:], x[tn + i_ * 128:tn + (i_ + 1) * 128, :])], writes=KL(xk2[i_]))
                    pre2.add(tn)
                for i in range(4):
                    for h, (pb, pbk) in enumerate(((B1, 'B1'), (B2, 'B2'))):
                        for kt in range(8):
                            k.op('pe', lambda e, kt=kt, i=i, h=h, pb=pb: e.matmul(
                                pb[:, :], lhsT=mT[:, kt, i * 128:(i + 1) * 128], rhs=Wout[:, kt, h * 512:(h + 1) * 512],
                                start=(kt == 0), stop=(kt == 7)), reads=['mT', 'p2Wout'], writes=[pbk],
                                signal=(kt == 7))
                    if i % 2 == 0:
                        xb, xk = attT[:].rearrange("p a b -> p (a b)").bitcast(F32), 'attT'
                    else:
                        xb, xk = uT[:].rearrange("p a b -> p (a b)").bitcast(F32), 'uT'
                    tb_, tbk = ssmT[:].rearrange("p a b -> p (a b)").bitcast(F32), 'ssmT'
                    k.dma('sp', [(xb[:], x[t0 + i * 128:t0 + (i + 1) * 128, :])], writes=[xk])
                    k.act(junk[:, 0:512], B1[:, :], AF.Square, ['B1'], ['p2s_junk', 'p2s_a0'], accum=stt[:, 24:25])
                    k.act(junk[:, 512:1024], B2[:, :], AF.Square, ['B2'], ['p2s_junk', 'p2s_a1'], accum=stt[:, 25:26])
                    k.tt('dve', stt[:, 26:27], stt[:, 24:25], stt[:, 25:26], ALU.add, ['p2s_a0', 'p2s_a1'], ['p2s_a2'])
                    k.ts('dve', stt[:, 27:28], stt[:, 26:27], 1.0 / D, 4.0 * EPS, ALU.mult, ALU.add, ['p2s_a2'],
                         ['p2s_a3'])
                    rsqrt_dve(k, stt[:, 28:29], 'p2s_r2', stt[:, 27:28], 'p2s_a3',
                              (stt[:, 29:30], stt[:, 30:31], stt[:, 31:32]), ('p2s_u0', 'p2s_u1', 'p2s_u2'), 1)
                    k.tt('dve', tb_[:, 0:512], B1[:, :], gbc[:, 0:512], ALU.mult, ['B1', 'p2gbc'], [tbk])
                    k.tt('dve', tb_[:, 512:1024], B2[:, :], gbc[:, 512:1024], ALU.mult, ['B2', 'p2gbc'], [tbk])
                    k.stt('dve', xb[:], tb_[:], stt[:, 28:29], xb[:], ALU.mult, ALU.add, [tbk, 'p2s_r2', xk], [xk])
                    k.dma('pool', [(T['x1s'][t0 + i * 128:t0 + (i + 1) * 128, :], xb[:])], reads=[xk],
                          writes=['x1s_dram'])
    k.barrier()


IN_SHAPES = {
    'x': None, 'w_in': [D, 4096], 'w_up': [D, 2 * DFF], 'w_down': [DFF, D], 'w_branch_att': [512, D],
    'w_branch_ssm': [512, D], 'w_out': [D, D], 'w_glu': [512, 512],
    'g_mix_pre_col': [128, 8], 'g_ffn_pre_col': [128, 8], 'g_mix_post_bc': [128, D], 'g_ffn_post_bc': [128, D],
    'conv_w_l': [128, 44, 3], 'conv_b_l': [128, 44], 'identf': [128, 128], 'ssm_d_col': [128, 4],
    'b_glu_col': [128, 4], 'att_tbl': [128, 8, 19, 64],
}
IN_SHAPES.update(SSM_SHAPES)


def build_program(cfg):
    nc = bass.Bass("TRN2", target_bir_lowering=False)
    NT = cfg['NT']
    T = {}
    for n, shp in IN_SHAPES.items():
        shp = [NT, D] if n == 'x' else shp
        T[n] = nc.dram_tensor(n, shp, F32, kind="ExternalInput").ap()
    T['y'] = nc.dram_tensor('y', [NT, D], F32, kind="ExternalOutput").ap()
    T['x1s'] = nc.dram_tensor('x1s', [NT, D], F32, kind="Internal").ap()
    T['uT_s'] = nc.dram_tensor('uT_s', [4, 128, NT], BF16, kind="Internal").ap()
    T['attT_s'] = nc.dram_tensor('attT_s', [4, 128, NT], BF16, kind="Internal").ap()
    T['yf_s'] = nc.dram_tensor('yf_s', [4, 128, NT], F32, kind="Internal").ap()
    with ExitStack() as es:
        k = KB(nc, es)
        pass1(k, cfg, T)
        pass2(k, cfg, T)
        pass3(k, cfg, T)
        k.finish('pool')
        k.finish('sp')
    return nc


def host_consts(g_mix_pre, g_mix_post, attn_rpb, ssm_lam_re, ssm_lam_im, ssm_log_dt, ssm_b_re, ssm_b_im,
                ssm_c_re, ssm_c_im, ssm_d, b_glu, g_ffn_pre, g_ffn_post, conv_w, conv_b):
    f = np.float32
    c = {}
    c['g_mix_pre_col'] = np.ascontiguousarray(g_mix_pre.reshape(8, 128).T).astype(f)
    c['g_ffn_pre_col'] = np.ascontiguousarray(g_ffn_pre.reshape(8, 128).T).astype(f)
    c['g_mix_post_bc'] = np.ascontiguousarray(np.broadcast_to(g_mix_post.reshape(1, D), (128, D))).astype(f)
    c['g_ffn_post_bc'] = np.ascontiguousarray(np.broadcast_to(g_ffn_post.reshape(1, D), (128, D))).astype(f)
    c['conv_w_l'] = np.ascontiguousarray(conv_w.T.reshape(44, 128, 3).transpose(1, 0, 2)).astype(f)
    c['conv_b_l'] = np.ascontiguousarray(conv_b.reshape(44, 128).T).astype(f)
    c['identf'] = np.eye(128, dtype=f)
    c['ssm_d_col'] = np.ascontiguousarray(ssm_d.reshape(4, 128).T).astype(f)
    c['b_glu_col'] = np.ascontiguousarray(b_glu.reshape(4, 128).T).astype(f)
    c['att_tbl'] = att_table_host(np.asarray(attn_rpb, f))
    c.update(ssm_host_layouts(ssm_lam_re, ssm_lam_im, ssm_log_dt, ssm_b_re, ssm_b_im, ssm_c_re, ssm_c_im))
    return c


_PROG = {}


def run_layer(xs_per_core, cfg, weights, consts, n_cores):
    key = (cfg['NT'], tuple(cfg['seqs']))
    if key not in _PROG:
        _PROG[key] = build_program(cfg)
    nc = _PROG[key]
    in_maps = []
    for c in range(n_cores):
        m = {'x': xs_per_core[c]}
        m.update(weights)
        m.update(consts)
        in_maps.append(m)
    res = run_bass_kernel_spmd(nc, in_maps, core_ids=list(range(n_cores)))
    return [r['y'] for r in res.results]


def kernel(x_prompt, x_sample, g_mix_pre, g_mix_post, w_in, attn_rpb, ssm_lam_re, ssm_lam_im, ssm_log_dt,
           ssm_b_re, ssm_b_im, ssm_c_re, ssm_c_im, ssm_d, w_glu, b_glu, w_branch_att, w_branch_ssm, w_out,
           g_ffn_pre, g_ffn_post, w_up, conv_w, conv_b, w_down):
    f = np.float32
    A = lambda a: np.asarray(a, f)
    x_prompt = A(x_prompt)
    x_sample = A(x_sample)
    consts = host_consts(A(g_mix_pre)[0], A(g_mix_post)[0], A(attn_rpb)[0], A(ssm_lam_re)[0], A(ssm_lam_im)[0],
                         A(ssm_log_dt)[0], A(ssm_b_re)[0], A(ssm_b_im)[0], A(ssm_c_re)[0], A(ssm_c_im)[0],
                         A(ssm_d)[0], A(b_glu)[0], A(g_ffn_pre)[0], A(g_ffn_post)[0], A(conv_w)[0], A(conv_b)[0])
    weights = {'w_in': np.ascontiguousarray(A(w_in)[0]), 'w_up': np.ascontiguousarray(A(w_up)[0]),
               'w_down': np.ascontiguousarray(A(w_down)[0]), 'w_branch_att': np.ascontiguousarray(A(w_branch_att)[0]),
               'w_branch_ssm': np.ascontiguousarray(A(w_branch_ssm)[0]), 'w_out': np.ascontiguousarray(A(w_out)[0]),
               'w_glu': np.ascontiguousarray(A(w_glu)[0])}
    n = 8
    Lp, Ls = x_prompt.shape[1], x_sample.shape[1]
    cfg = {'NT': Lp + 2 * Ls, 'seqs': [(0, Lp), (Lp, Ls), (Lp + Ls, Ls)]}
    xs = [np.ascontiguousarray(np.concatenate([x_prompt[c], x_sample[2 * c], x_sample[2 * c + 1]], axis=0))
          for c in range(n)]
    ys = run_layer(xs, cfg, weights, consts, n)
    y_prompt = np.stack([y[0:Lp] for y in ys]).astype(f)
    y_sample = np.stack([ys[c // 2][Lp + (c % 2) * Ls:Lp + (c % 2 + 1) * Ls] for c in range(2 * n)]).astype(f)
    return (y_prompt, y_sample)


def ssm_mb(k, C, uT, uTk, PS, PK, d, blocks, ident, yps, ypk, wh, whk, on_block, filler=None):
    steps = [(gb, c0, ct) for (gb, c0) in blocks for ct in range(4)]
    n = len(steps)
    pre, post, H, wz = C['pre'], C['post'], C['H'], C['w']
    last = 127 if d == 0 else 0

    def s_bu(gb, c0, ct):
        for comp in range(2):
            k.op('pe', lambda e, comp=comp: e.matmul(PS['bu%d' % comp][:, :], lhsT=uT[:, ct, c0:c0 + 128],
                                                    rhs=C['Bpad'][:, ct, comp, :], start=True, stop=True),
                 reads=[uTk, 'c_Bpad'], writes=[PK['bu%d' % comp]])

    def s_zm(gb, c0, ct):
        pr = pre[:, 0, ct * 512:(ct + 1) * 512]
        pi_ = pre[:, 1, ct * 512:(ct + 1) * 512]
        bur, bui = PS['bu0'][:, :], PS['bu1'][:, :]
        k.tt('dve', wz[0][:], pr, bur, ALU.mult, ['c_pre0', PK['bu0']], ['c_w0'])
        k.tt('dve', wz[1][:], pi_, bui, ALU.mult, ['c_pre1', PK['bu1']], ['c_w1'])
        k.tt('dve', wz[2][:], pr, bui, ALU.mult, ['c_pre0', PK['bu1']], ['c_w2'])
        k.tt('dve', wz[3][:], pi_, bur, ALU.mult, ['c_pre1', PK['bu0']], ['c_w3'])

    def s_zc(gb, c0, ct):
        k.tt('pool', C['zre'][:, ct * 512:(ct + 1) * 512], wz[0][:], wz[1][:], ALU.subtract, ['c_w0', 'c_w1'],
             ['c_zre%d' % ct])
        k.tt('pool', C['zim'][:, ct * 512:(ct + 1) * 512], wz[2][:], wz[3][:], ALU.add, ['c_w2', 'c_w3'],
             ['c_zim%d' % ct])

    def s_cs(gb, c0, ct):
        first = (gb == 0)
        cin = C['cc'][gb % 2]
        cink = 'c_cc%d_%d' % (gb % 2, ct)
        for comp, zz, zk in ((0, C['zre'], 'c_zre%d' % ct), (1, C['zim'], 'c_zim%d' % ct)):
            for q in range(4):
                pair = ct * 4 + q
                k.op('pe', lambda e, zz=zz, q=q, comp=comp: e.matmul(
                    PS['cu%d' % comp][:, q * 128:(q + 1) * 128],
                    lhsT=zz[:, ct * 512 + q * 128:ct * 512 + (q + 1) * 128], rhs=C['trib'][:, :],
                    start=True, stop=first), reads=[zk, 'c_trib'], writes=[PK['cu%d' % comp]], signal=first)
                if not first:
                    k.op('pe', lambda e, q=q, pair=pair, comp=comp: e.matmul(
                        PS['cu%d' % comp][:, q * 128:(q + 1) * 128], lhsT=ident,
                        rhs=cin[:, comp, pair:pair + 1].broadcast_to([128, 128]), start=False, stop=True),
                        reads=[cink, 'ident'], writes=[PK['cu%d' % comp]])

    def s_hm(gb, c0, ct):
        por = post[:, 0, ct * 4:(ct + 1) * 4, :].rearrange("p c x -> p (c x)")
        poi = post[:, 1, ct * 4:(ct + 1) * 4, :].rearrange("p c x -> p (c x)")
        cur, cui = PS['cu0'][:, :], PS['cu1'][:, :]
        k.tt('dve', wh[0], por, cur, ALU.mult, ['c_post0', PK['cu0']], [whk[0]])
        k.tt('dve', wh[1], poi, cui, ALU.mult, ['c_post1', PK['cu1']], [whk[1]])
        k.tt('dve', wh[2], por, cui, ALU.mult, ['c_post0', PK['cu1']], [whk[2]])
        k.tt('dve', wh[3], poi, cur, ALU.mult, ['c_post1', PK['cu0']], [whk[3]])

    def s_hc(gb, c0, ct):
        k.tt('pool', H[:, 0, ct * 4:(ct + 1) * 4, :].rearrange("p c x -> p (c x)"), wh[0], wh[1], ALU.subtract,
             [whk[0], whk[1]], ['c_H0_%d' % ct])
        k.tt('pool', H[:, 1, ct * 4:(ct + 1) * 4, :].rearrange("p c x -> p (c x)"), wh[2], wh[3], ALU.add,
             [whk[2], whk[3]], ['c_H1_%d' % ct])

    def s_carry(gb, c0, ct):
        cout = C['cc'][(gb + 1) % 2]
        coutk = 'c_cc%d_%d' % ((gb + 1) % 2, ct)
        sl = slice(ct * 4, (ct + 1) * 4)
        hr, hi = H[:, 0, sl, last], H[:, 1, sl, last]
        ar, ai = C['a_sp'][:, 0, sl], C['a_sp'][:, 1, sl]
        f = C['ccf']
        hk0, hk1 = 'c_H0_%d' % ct, 'c_H1_%d' % ct
        fk = ['c_ccf%d_%d' % (i, ct) for i in range(4)]
        k.tt('dve', f[:, 0, sl], ar, hr, ALU.mult, ['c_asp0', hk0], [fk[0]])
        k.tt('dve', f[:, 1, sl], ai, hi, ALU.mult, ['c_asp1', hk1], [fk[1]])
        k.tt('dve', f[:, 2, sl], ar, hi, ALU.mult, ['c_asp0', hk1], [fk[2]])
        k.tt('dve', f[:, 3, sl], ai, hr, ALU.mult, ['c_asp1', hk0], [fk[3]])
        k.tt('dve', cout[:, 0, sl], f[:, 0, sl], f[:, 1, sl], ALU.subtract, [fk[0], fk[1]], [coutk])
        k.tt('dve', cout[:, 1, sl], f[:, 2, sl], f[:, 3, sl], ALU.add, [fk[2], fk[3]], [coutk])

    def s_cp(gb, c0, ct):
        nn = 0
        for comp in range(2):
            for q in range(4):
                pair = ct * 4 + q
                k.op('pe', lambda e, comp=comp, pair=pair, nn=nn: e.matmul(
                    yps[:, ct, :], lhsT=C['Cpad'][:, comp, pair, :], rhs=H[:, comp, pair, :],
                    start=(nn == 0), stop=(nn == 7)), reads=['c_Cpad', 'c_H%d_%d' % (comp, ct)], writes=[ypk[ct]],
                    signal=(nn == 7))
                nn += 1
        if ct == 3:
            on_block(gb, c0)

    for i in range(n + 2):
        if i < n:
            s_bu(*steps[i])
            s_zm(*steps[i])
            s_zc(*steps[i])
        if 0 <= i - 1 < n:
            s_cs(*steps[i - 1])
            s_hm(*steps[i - 1])
            s_hc(*steps[i - 1])
        if 0 <= i - 2 < n:
            s_carry(*steps[i - 2])
            s_cp(*steps[i - 2])
        if filler:
            cnt = max(-(-len(filler) // (n + 2 - i)), min(3, len(filler)))
            for _ in range(cnt):
                filler.pop(0)()
```

```python
import numpy as np
from contextlib import ExitStack
import concourse.bass as bass
import concourse.mybir as mybir
from concourse.bass_utils import run_bass_kernel_spmd

F32 = mybir.dt.float32
BF16 = mybir.dt.bfloat16
I32 = mybir.dt.int32
AF = mybir.ActivationFunctionType
ALU = mybir.AluOpType

D = 1024
DFF = 2816
EPS = 1e-6
GC1 = 1.5957691216057308
GC2 = GC1 * 0.044715
NEG = -30000.0
STORE_Q = 'sp'
STRICT = True
import os
P3_TINY = os.environ.get('P3_TINY', 'act')
P3_POLY = os.environ.get('P3_POLY', 'pool')
P3_DBG = os.environ.get('P3_DBG', '')


class KB:
    NDS = 24

    def __init__(self, nc, es):
        self.nc = nc
        self.E = {'pe': nc.tensor, 'act': nc.scalar, 'dve': nc.vector, 'pool': nc.gpsimd, 'sp': nc.sync}
        self.sem = {e: es.enter_context(nc.semaphore('s_' + e)) for e in ['pe', 'act', 'dve', 'pool']}
        self.cnt = {e: 0 for e in self.sem}
        self.dsem = [es.enter_context(nc.semaphore('d%d' % i)) for i in range(self.NDS)]
        self.dcnt = [0] * self.NDS
        self.dnext = 0
        self.seen = {e: {} for e in self.E}
        self.lastw = {}
        self.readers = {}
        self.unsig = {e: False for e in self.sem}
        self.excl = set(['B%d' % i for i in range(8)] + ['B7_%d' % i for i in range(4)] + ['bu0', 'bu1', 'cu0', 'cu1', 'yps']
                        + ['pup%d' % i for i in range(4)] + ['pdn0', 'pdn1', 'ptr0', 'ptr1'])

    def _wait(self, eng, ev):
        key, val = ev
        if self.seen[eng].get(key, 0) >= val:
            return
        sem = self.sem[key] if isinstance(key, str) else self.dsem[key]
        self.E[eng].wait_ge(sem, val)
        self.seen[eng][key] = val

    def _deps(self, eng, reads, writes):
        for b in reads:
            ev = self.lastw.get(b)
            if ev is not None and not (ev[0] == eng and eng == 'pe'):
                self._wait(eng, ev)
            if b in self.excl:
                for rv in self.readers.get(b, ()):
                    if rv[0] != eng:
                        self._wait(eng, rv)
        for b in writes:
            ev = self.lastw.get(b)
            if ev is not None and (ev[0] != eng or (STRICT and eng != 'pe')):
                self._wait(eng, ev)
            for rv in self.readers.get(b, ()):
                if rv[0] != eng or (STRICT and eng != 'pe'):
                    self._wait(eng, rv)

    def _reg(self, ev, reads, writes):
        for b in reads:
            self.readers.setdefault(b, []).append(ev)
        for b in writes:
            self.lastw[b] = ev
            self.readers[b] = []

    def op(self, eng, fn, reads=(), writes=(), signal=True):
        self._deps(eng, reads, writes)
        ins = fn(self.E[eng])
        if signal:
            self.cnt[eng] += 1
            ins.then_inc(self.sem[eng], 1)
            ev = (eng, self.cnt[eng])
            self.unsig[eng] = False
        else:
            ev = (eng, self.cnt[eng] + 1)
            self.unsig[eng] = True
        self._reg(ev, reads, writes)
        return ev

    def dma(self, q, pairs, reads=(), writes=()):
        if q == 'pool' and STORE_Q != 'pool':
            q = STORE_Q
        i = self.dnext
        self.dnext = (i + 1) % self.NDS
        if self.dcnt[i] > 0:
            self._wait(q, (i, self.dcnt[i]))
        self._deps(q, reads, writes)
        for (o, s) in pairs:
            self.E[q].dma_start(out=o, in_=s).then_inc(self.dsem[i], 16)
        self.dcnt[i] += 16 * len(pairs)
        ev = (i, self.dcnt[i])
        self._reg(ev, reads, writes)
        return ev

    def barrier(self):
        evs = [(e, self.cnt[e]) for e in self.sem if self.cnt[e] > 0]
        evs += [(i, self.dcnt[i]) for i in range(self.NDS) if self.dcnt[i] > 0]
        for eng in self.E:
            for ev in evs:
                if ev[0] != eng:
                    self._wait(eng, ev)

    def tt(self, eng, out, in0, in1, op, reads, writes):
        return self.op(eng, lambda e: e.tensor_tensor(out=out, in0=in0, in1=in1, op=op), reads, writes)

    def ts(self, eng, out, in0, s1, s2, op0, op1, reads, writes):
        if s2 is None:
            return self.op(eng, lambda e: e.tensor_scalar(out=out, in0=in0, scalar1=s1, scalar2=None, op0=op0),
                           reads, writes)
        return self.op(eng, lambda e: e.tensor_scalar(out=out, in0=in0, scalar1=s1, scalar2=s2, op0=op0, op1=op1),
                       reads, writes)

    def stt(self, eng, out, in0, scalar, in1, op0, op1, reads, writes):
        return self.op(eng, lambda e: e.scalar_tensor_tensor(out=out, in0=in0, scalar=scalar, in1=in1, op0=op0,
                                                             op1=op1), reads, writes)

    def act(self, out, in_, func, reads, writes, scale=1.0, bias=None, accum=None):
        kw = {}
        if bias is not None:
            kw['bias'] = bias
        if accum is not None:
            kw['accum_out'] = accum
        return self.op('act', lambda e: e.activation(out=out, in_=in_, func=func, scale=scale, **kw), reads, writes)

    def cp(self, eng, out, in_, reads, writes):
        if eng == 'act':
            return self.op('act', lambda e: e.activation(out=out, in_=in_, func=AF.Copy), reads, writes)
        return self.op(eng, lambda e: e.tensor_copy(out=out, in_=in_), reads, writes)

    def finish(self, q='sp'):
        for i in range(self.NDS):
            if self.dcnt[i] > 0:
                self._wait(q, (i, self.dcnt[i]))


def rsqrt_dve(k, out, outk, v, vk, tmp, tmpk, n):
    t0, t1, t2 = tmp
    k0, k1, k2 = tmpk
    k.op('dve', lambda e: e.tensor_single_scalar(out=t0.bitcast(I32), in_=v.bitcast(I32), scalar=1,
                                                 op=ALU.arith_shift_right), reads=[vk], writes=[k0])
    k.op('dve', lambda e: e.tensor_scalar(out=out.bitcast(I32), in0=t0.bitcast(I32), scalar1=-1,
                                          scalar2=0x5f3759df, op0=ALU.mult, op1=ALU.add),
         reads=[k0], writes=[outk])
    for _ in range(3):
        k.op('dve', lambda e: e.tensor_tensor(out=t1, in0=out, in1=out, op=ALU.mult), reads=[outk], writes=[k1])
        k.op('dve', lambda e: e.tensor_tensor(out=t2, in0=t1, in1=v, op=ALU.mult), reads=[k1, vk], writes=[k2])
        k.op('dve', lambda e: e.tensor_scalar(out=t1, in0=t2, scalar1=-0.5, scalar2=1.5, op0=ALU.mult,
                                              op1=ALU.add), reads=[k2], writes=[k1])
        k.op('dve', lambda e: e.tensor_tensor(out=out, in0=out, in1=t1, op=ALU.mult), reads=[outk, k1],
             writes=[outk])


def norm_transpose(k, pfx, xt, xtk, hn, hnk, hT, hTk, col0, ptr, ptrk, ident, st, stk, eps_eff, evac_eng):
    junk = st['junk']
    k.op('act', lambda e: e.activation(out=junk, in_=xt, func=AF.Square, accum_out=st['ss']),
         reads=[xtk], writes=[stk + 'junk', stk + 'ss'])
    k.op('dve', lambda e: e.tensor_scalar(out=st['v'], in0=st['ss'], scalar1=1.0 / D, scalar2=eps_eff,
                                          op0=ALU.mult, op1=ALU.add), reads=[stk + 'ss'], writes=[stk + 'v'])
    rsqrt_dve(k, st['r'], stk + 'r', st['v'], stk + 'v', (st['t0'], st['t1'], st['t2']),
              (stk + 't0', stk + 't1', stk + 't2'), 1)
    k.op('dve', lambda e: e.tensor_scalar(out=hn, in0=xt, scalar1=st['r'], scalar2=None, op0=ALU.mult),
         reads=[xtk, stk + 'r'], writes=[hnk])
    for kt in range(8):
        k.op('pe', lambda e, kt=kt: e.transpose(ptr[:, kt, :], hn[:, kt * 128:(kt + 1) * 128], ident),
             reads=[hnk], writes=[ptrk], signal=(kt == 7))
    if evac_eng == 'act':
        k.op('act', lambda e: e.activation(out=hT[:, :, col0:col0 + 128], in_=ptr[:, :, :], func=AF.Copy),
             reads=[ptrk], writes=[hTk])
    else:
        k.op(evac_eng, lambda e: e.tensor_copy(out=hT[:, :, col0:col0 + 128], in_=ptr[:, :, :]),
             reads=[ptrk], writes=[hTk])


def load_convert_weight(k, Wdst, Wk, wsrc, nkt, ncols, stage_bufs, stage_keys, scale_col=None,
                        const_scale=None, chunk=512):
    ci = 0
    cwid = min(512, ncols)
    kg = max(1, 1024 // cwid)
    for c0 in range(0, ncols, cwid):
        cw = min(cwid, ncols - c0)
        for k0 in range(0, nkt, kg):
            nk = min(kg, nkt - k0)
            sbuf = stage_bufs[ci % len(stage_bufs)]
            skey = KL(stage_keys[ci % len(stage_bufs)])
            view = sbuf[:, 0:nk * cw].rearrange("p (a b) -> p a b", a=nk)
            k.dma('sp', [(view, wsrc[:, k0:k0 + nk, c0:c0 + cw])], writes=skey)
            o = Wdst[:, k0:k0 + nk, c0:c0 + cw]
            if scale_col is not None:
                bc = scale_col[:, k0:k0 + nk].unsqueeze(2).broadcast_to([128, nk, cw])
                k.op('dve', lambda e, o=o, view=view, bc=bc: e.tensor_tensor(out=o, in0=view, in1=bc, op=ALU.mult),
                     reads=skey, writes=[Wk])
            else:
                cs = 1.0 if const_scale is None else const_scale
                if ci % 2 == 1:
                    k.op('act', lambda e, o=o, view=view: e.activation(out=o, in_=view, func=AF.Copy, scale=cs),
                         reads=skey, writes=[Wk])
                else:
                    k.op('dve', lambda e, o=o, view=view: e.tensor_scalar(out=o, in0=view, scalar1=cs, scalar2=None,
                                                                       op0=ALU.mult), reads=skey, writes=[Wk])
            ci += 1


def KL(x):
    return [x] if isinstance(x, str) else list(x)


def pass3(k, cfg, T):
    nc = k.nc
    with ExitStack() as es:
        def sb(n, s, d=F32):
            return es.enter_context(nc.sbuf_tensor("p3_" + n, s, d))

        def pp(n, s, d=F32):
            return es.enter_context(nc.psum_tensor("p3_" + n, s, d))
        Wup = sb("Wup", [128, 8, 2 * DFF], BF16)
        Wd = sb("Wd", [128, 22, D], BF16)
        tokf = [sb("tok%d" % i, [128, 1028]) for i in range(4)]
        tok = [t[:, 0:D] for t in tokf]
        tokk = [('tok%d' % i, 'tok%db' % i) for i in range(4)]
        hn = [sb("hn%d" % i, [128, D], BF16) for i in range(1)]
        h2T = sb("h2T", [128, 8, 512], BF16)
        gT = sb("gT", [128, 22, 512], BF16)
        gbc = sb("gbc", [128, D])
        g2col = sb("g2col", [128, 8])
        cw = sb("cw", [128, 44, 3])
        cb = sb("cb", [128, 44])
        carry = sb("carry", [128, 22, 2, 2])
        ident = sb("ident", [128, 128], BF16)
        identf = sb("identf", [128, 128])
        extA = sb("extA", [128, 2, 514])
        caccA = sb("caccA", [128, 2, 512])
        tmpA = sb("tmpA", [128, 2, 512])
        stt = sb("stt", [128, 40])
        junk = sb("junk", [128, D], BF16)
        ptr = [pp("ptr0", [128, 8, 128], BF16)]
        pup = [pp("pup%d" % i, [128, 512]) for i in range(4)]
        pdn = [pp("pdn%d" % i, [128, 512]) for i in range(2)]
        EXT = [extA, tokf[2][:, 0:1028].rearrange("p (h x) -> p h x", h=2)]
        EXTK = [('extA_a', 'extA_b'), tokk[2]]
        CAC = [caccA, tokf[3][:, 0:1024].rearrange("p (h x) -> p h x", h=2)]
        CACK = [('cacA_a', 'cacA_b'), tokk[3]]
        TMP = [tmpA, tokf[1][:, 0:1024].rearrange("p (h x) -> p h x", h=2)]
        TMPK = [('tmpA_a', 'tmpA_b'), tokk[1]]

        k.dma('sp', [(gbc[:], T['g_ffn_post_bc'][:, :]), (g2col[:], T['g_ffn_pre_col'][:, :]),
                     (cw[:], T['conv_w_l'][:, :, :]), (cb[:], T['conv_b_l'][:, :]),
                     (identf[:], T['identf'][:, :])], writes=['gbc', 'g2col', 'cw', 'cb', 'identf'])
        k.cp('dve', ident[:], identf[:], ['identf'], ['ident'])
        wup_v = T['w_up'].rearrange("(kt p) c -> p kt c", p=128)
        load_convert_weight(k, Wup, 'Wup', wup_v, 8, 2 * DFF, tok, tokk, scale_col=g2col, chunk=128)
        wd_v = T['w_down'].rearrange("(kt p) c -> p kt c", p=128)
        load_convert_weight(k, Wd, 'Wd', wd_v, 22, D, tok, tokk, chunk=32)

        tokc = 0
        x1s = T['x1s']
        y = T['y']
        starts3 = [off_ + m_ * 512 for (off_, L_) in cfg['seqs'] for m_ in range(L_ // 512)]
        pre3 = set()
        XB = [caccA[:].rearrange("p h x -> p (h x)"), extA[:].rearrange("p h x -> p (h x)")[:, 0:D]]
        XBK = [('cacA_a', 'cacA_b'), ('extA_a', 'extA_b')]
        TB = tmpA[:].rearrange("p h x -> p (h x)")
        TBK = ('tmpA_a', 'tmpA_b')
        for (off, L) in cfg['seqs']:
            nmb = L // 512
            k.op('dve', lambda e: e.memset(carry[:], 0.0), writes=['carry'])
            for mb in range(nmb + 1):
                tail = (mb == nmb)
                wdt = 4 if tail else 512
                t0 = off + mb * 512
                if not tail:
                    prenorm4(k, x1s, t0, tok, tokk, hn, ['p3hn0'], h2T, 'h2T', ptr, ['ptr0'], ident[:], stt, 'p3s_',
                             junk, EPS, skip_load=(0, 1, 2, 3) if t0 in pre3 else ())

                def st_U(j):
                    if tail:
                        return
                    for half in range(2):
                        tl = j + 22 * half
                        pb = pup[(2 * j + half) % 4]
                        pk = 'pup%d' % ((2 * j + half) % 4)
                        for kt in range(8):
                            k.op('pe', lambda e, kt=kt, tl=tl, pb=pb: e.matmul(
                                pb[:, :], lhsT=Wup[:, kt, tl * 128:(tl + 1) * 128], rhs=h2T[:, kt, :],
                                start=(kt == 0), stop=(kt == 7)), reads=['Wup', 'h2T'], writes=[pk],
                                signal=(kt == 7))

                def st_E(j):
                    par = j % 2
                    ex, exk = EXT[par], EXTK[par]
                    ca, cak = CAC[par], CACK[par]
                    for hf in range(2):
                        k.cp(P3_TINY, ex[:, hf, 0:2], carry[:, j, hf, :], ['carry'], [exk[hf]])
                    for half in range(2):
                        tl = j + 22 * half
                        pb = pup[(2 * j + half) % 4]
                        pk = 'pup%d' % ((2 * j + half) % 4)
                        if not tail:
                            k.act(ex[:, half, 2:514], pb[:, :], AF.Copy, [pk], [exk[half]])
                            k.act(ca[:, half, :], pb[:, :], AF.Copy, [pk, 'cw'], [cak[half]], scale=cw[:, tl, 2:3])
                        else:
                            k.op('pool', lambda e, ex=ex, half=half: e.memset(ex[:, half, 2:2 + wdt], 0.0),
                                 writes=[exk[half]])
                            k.act(ca[:, half, 0:wdt], ex[:, half, 2:2 + wdt], AF.Copy, [exk[half], 'cw'], [cak[half]],
                                  scale=cw[:, tl, 2:3])
                    if not tail:
                        for hf in range(2):
                            k.cp(P3_TINY, carry[:, j, hf, :], ex[:, hf, 512:514], [exk[hf]], ['carry'])

                def st_C(j):
                    par = j % 2
                    ex, exk = EXT[par], EXTK[par]
                    ca, cak = CAC[par], CACK[par]
                    for (c0_, wi) in ((0, 0), (1, 1)):
                        for half in range(2):
                            tl = j + 22 * half
                            k.stt('dve', ca[:, half, 0:wdt], ex[:, half, c0_:c0_ + wdt], cw[:, tl, wi:wi + 1],
                                  ca[:, half, 0:wdt], ALU.mult, ALU.add, [exk[half], cak[half], 'cw'], [cak[half]])

                def st_B(j):
                    par = j % 2
                    ca, cak = CAC[par], CACK[par]
                    k.act(ca[:, 0, 0:wdt], ca[:, 0, 0:wdt], AF.Identity, [cak[0], 'cb'], [cak[0]], bias=cb[:, j:j + 1])

                def st_Sq(j):
                    par = j % 2
                    k.act(TMP[par][:, 0, 0:wdt], CAC[par][:, 0, 0:wdt], AF.Square, [CACK[par][0]], [TMPK[par][0]])

                def st_P(j):
                    par = j % 2
                    ta, tak = TMP[par][:, 0, 0:wdt], TMPK[par][0]
                    k.ts(P3_POLY, ta, ta, GC2, GC1, ALU.mult, ALU.add, [tak], [tak])
                    k.tt(P3_POLY, ta, ta, CAC[par][:, 0, 0:wdt], ALU.mult, [tak, CACK[par][0]], [tak])

                def st_T(j):
                    par = j % 2
                    k.act(TMP[par][:, 1, 0:wdt], TMP[par][:, 0, 0:wdt], AF.Sigmoid, [TMPK[par][0]], [TMPK[par][1]])

                def st_G(j):
                    par = j % 2
                    tb2, tbk2 = TMP[par][:, 1, 0:wdt], TMPK[par][1]
                    k.tt(P3_POLY, tb2, tb2, CAC[par][:, 0, 0:wdt], ALU.mult, [tbk2, CACK[par][0]], [tbk2])
                    k.stt('dve', gT[:, j, 0:wdt], CAC[par][:, 1, 0:wdt], cb[:, 22 + j:23 + j], tb2, ALU.add, ALU.mult,
                          [tbk2, CACK[par][1], 'cb'], ['gT'])

                if 'nogelu' in P3_DBG:
                    def st_Sq(j): pass
                    def st_B(j): pass
                    def st_P(j): pass
                    def st_T(j): pass
                    def st_G(j):
                        k.cp('dve', gT[:, j, :], CAC[j % 2][:, 0, :], [CACK[j % 2][0]], ['gT'])
                if 'noconv' in P3_DBG:
                    def st_C(j): pass
                if 'noE' in P3_DBG:
                    def st_E(j): pass
                if 'noU' in P3_DBG:
                    def st_U(j): pass
                for j in range(23):
                    if j >= 1:
                        st_B(j - 1)
                        st_Sq(j - 1)
                    if j < 22:
                        st_U(j)
                        st_E(j)
                    if j >= 1:
                        st_P(j - 1)
                    if j < 22:
                        st_C(j)
                    if j >= 1:
                        st_T(j - 1)
                        st_G(j - 1)

                if not tail and mb + 1 < nmb:
                    nxt = starts3.index(t0) + 1
                    if nxt < len(starts3):
                        tn = starts3[nxt]
                        for i_ in range(4):
                            k.dma('sp', [(tok[i_][:], x1s[tn + i_ * 128:tn + (i_ + 1) * 128, :])], reads=['x1s_dram'],
                                  writes=KL(tokk[i_]))
                        pre3.add(tn)
                ntile = 1 if tail else 4
                if 'nodown' in P3_DBG:
                    ntile = 0
                for i in range(ntile):
                    tk0 = t0 - 1 + i * 128
                    p_lo = 1 if tk0 < off else 0
                    p_hi = 1 if tail else 128
                    xb, xk = XB[tokc % 2], XBK[tokc % 2]
                    tokc += 1
                    tb_, tbk = TB, TBK
                    if i % 2 == 0 or 'pdn' in P3_DBG:
                        pd, pdk = pdn, ['pdn0', 'pdn1']
                    else:
                        pd, pdk = pup[0:2], ['pup0', 'pup1']
                    if 'noload' not in P3_DBG:
                        k.dma('sp', [(xb[p_lo:p_hi, :], x1s[tk0 + p_lo:tk0 + p_hi, :])], reads=['x1s_dram'],
                              writes=list(xk))
                    for h in range(2 if 'nomm' not in P3_DBG else 0):
                        for kt in range(22):
                            k.op('pe', lambda e, kt=kt, h=h, i=i, pd=pd: e.matmul(
                                pd[h][:, :], lhsT=gT[:, kt, i * 128:(i + 1) * 128],
                                rhs=Wd[:, kt, h * 512:(h + 1) * 512], start=(kt == 0), stop=(kt == 21)),
                                reads=['gT', 'Wd'], writes=[pdk[h]], signal=(kt == 21))
                    if 'nonorm' in P3_DBG:
                        continue
                    k.act(junk[:, 0:512], pd[0][:, :], AF.Square, [pdk[0]], ['p3s_junk', 'p3s_a0'], accum=stt[:, 24:25])
                    k.act(junk[:, 512:1024], pd[1][:, :], AF.Square, [pdk[1]], ['p3s_junk', 'p3s_a1'],
                          accum=stt[:, 25:26])
                    for h in range(2):
                        k.tt('dve', tb_[:, h * 512:(h + 1) * 512], pd[h][:, :], gbc[:, h * 512:(h + 1) * 512],
                             ALU.mult, [pdk[h], 'gbc'], list(tbk))
                    k.tt('dve', stt[:, 26:27], stt[:, 24:25], stt[:, 25:26], ALU.add, ['p3s_a0', 'p3s_a1'], ['p3s_a2'])
                    k.ts('dve', stt[:, 27:28], stt[:, 26:27], 1.0 / D, EPS, ALU.mult, ALU.add, ['p3s_a2'],
                         ['p3s_a3'])
                    rsqrt_dve(k, stt[:, 28:29], 'p3s_r2', stt[:, 27:28], 'p3s_a3',
                              (stt[:, 29:30], stt[:, 30:31], stt[:, 31:32]), ('p3s_u0', 'p3s_u1', 'p3s_u2'), 1)
                    k.stt('dve', xb[:, :], tb_[:, :], stt[:, 28:29], xb[:, :], ALU.mult, ALU.add,
                          list(tbk) + ['p3s_r2'] + list(xk), list(xk))
                    if 'nostore' not in P3_DBG:
                        k.dma('pool', [(y[tk0 + p_lo:tk0 + p_hi, :], xb[p_lo:p_hi, :])], reads=list(xk),
                              writes=['y_dram'])


TWO_PI = 2.0 * np.pi


def _table_from(k, pfx, expo, ek, ang, ak, re, rek, im, imk, tmp, tmpk, ti, tik):
    E, f, sc = tmp
    Ek, fk, sck = tmpk
    k.act(E, expo, AF.Exp, [ek], [Ek])
    k.cp('dve', ti, ang, [ak], [tik])
    k.cp('dve', f, ti, [tik], [fk])
    k.tt('dve', f, ang, f, ALU.subtract, [ak, fk], [fk])
    k.act(sc, f, AF.Sin, [fk], [sck], scale=TWO_PI)
    k.tt('dve', im, E, sc, ALU.mult, [Ek, sck], [imk])
    k.ts('dve', f, ang, 0.25, None, ALU.add, None, [ak], [fk])
    k.cp('dve', ti, f, [fk], [tik])
    k.cp('dve', sc, ti, [tik], [sck])
    k.tt('dve', f, f, sc, ALU.subtract, [fk, sck], [fk])
    k.act(sc, f, AF.Sin, [fk], [sck], scale=TWO_PI)
    k.tt('dve', re, E, sc, ALU.mult, [Ek, sck], [rek])


def ssm_prep(k, T, d, C):
    nc = k.nc
    k.barrier()
    with ExitStack() as es:
        def sb(n, s, dt=F32):
            return es.enter_context(nc.sbuf_tensor("sp%d_" % d + n, s, dt))
        N = 2048
        lre = sb("lre", [128, N]); lim = sb("lim", [128, N]); ldt = sb("ldt", [128, N])
        bre = sb("bre", [128, N]); bim = sb("bim", [128, N])
        xr = sb("xr", [128, N]); xi = sb("xi", [128, N])
        are = sb("are", [128, N]); aim = sb("aim", [128, N])
        t0 = sb("t0", [128, N]); t1 = sb("t1", [128, N]); t2 = sb("t2", [128, N]); t3 = ldt
        ti = sb("ti", [128, N], I32)
        sm = sb("sm", [128, 8, 16])
        smi = sb("smi", [128, 16], I32)
        k.dma('sp', [(lre[:], T['lam_re_bp'][d]), (lim[:], T['lam_im_bp'][d]), (ldt[:], T['log_dt_bp'][d]),
                     (bre[:], T['b_reT_pad'][d]), (bim[:], T['b_imT_pad'][d])],
              writes=['p_lre', 'p_lim', 'p_ldt', 'p_bre', 'p_bim'])
        k.dma('sp', [(sm[:, 0, :], T['lam_re_sp'][d]), (sm[:, 1, :], T['lam_im_sp'][d]),
                     (sm[:, 2, :], T['log_dt_sp'][d]), (C['sig'][:], T['nsig'][d]), (C['trow'][:], T['trow'][d]),
                     (C['tri'][:], T['tri'][d])],
              writes=['p_sm', 'c_sig', 'c_trow', 'c_tri'])
        k.act(t0[:], ldt[:], AF.Exp, ['p_ldt'], ['p_t0'])
        k.tt('dve', xr[:], lre[:], t0[:], ALU.mult, ['p_lre', 'p_t0'], ['p_xr'])
        k.tt('dve', xi[:], lim[:], t0[:], ALU.mult, ['p_lim', 'p_t0'], ['p_xi'])
        k.ts('dve', xi[:], xi[:], 1.0 / TWO_PI, None, ALU.mult, None, ['p_xi'], ['p_xi'])
        _table_from(k, 'a', xr[:], 'p_xr', xi[:], 'p_xi', are[:], 'p_are', aim[:], 'p_aim',
                    (t0[:], t1[:], t2[:]), ('p_t0', 'p_t1', 'p_t2'), ti[:], 'p_ti')
        k.ts('dve', t0[:], are[:], -1.0, None, ALU.add, None, ['p_are'], ['p_t0'])
        k.tt('dve', t1[:], lre[:], lre[:], ALU.mult, ['p_lre'], ['p_t1'])
        k.tt('dve', t2[:], lim[:], lim[:], ALU.mult, ['p_lim'], ['p_t2'])
        k.tt('dve', t1[:], t1[:], t2[:], ALU.add, ['p_t1', 'p_t2'], ['p_t1'])
        k.op('dve', lambda e: e.reciprocal(out=t1[:], in_=t1[:]), ['p_t1'], ['p_t1'])
        k.tt('dve', t2[:], t0[:], lre[:], ALU.mult, ['p_t0', 'p_lre'], ['p_t2'])
        k.tt('dve', t3[:], aim[:], lim[:], ALU.mult, ['p_aim', 'p_lim'], ['p_ldt'])
        k.tt('dve', t2[:], t2[:], t3[:], ALU.add, ['p_t2', 'p_ldt'], ['p_t2'])
        k.tt('dve', t2[:], t2[:], t1[:], ALU.mult, ['p_t2', 'p_t1'], ['p_t2'])
        k.tt('dve', t3[:], aim[:], lre[:], ALU.mult, ['p_aim', 'p_lre'], ['p_ldt'])
        k.tt('dve', t0[:], t0[:], lim[:], ALU.mult, ['p_t0', 'p_lim'], ['p_t0'])
        k.tt('dve', t3[:], t3[:], t0[:], ALU.subtract, ['p_ldt', 'p_t0'], ['p_ldt'])
        k.tt('dve', t3[:], t3[:], t1[:], ALU.mult, ['p_ldt', 'p_t1'], ['p_ldt'])
        Bp = C['Bpad']
        v4 = lambda a: a.rearrange("p (c x) -> p c x", c=4)
        k.tt('dve', t0[:], t2[:], bre[:], ALU.mult, ['p_t2', 'p_bre'], ['p_t0'])
        k.tt('dve', t1[:], t3[:], bim[:], ALU.mult, ['p_ldt', 'p_bim'], ['p_t1'])
        k.tt('dve', Bp[:, :, 0, :], v4(t0[:]), v4(t1[:]), ALU.subtract, ['p_t0', 'p_t1'], ['c_Bpad'])
        k.tt('dve', t0[:], t2[:], bim[:], ALU.mult, ['p_t2', 'p_bim'], ['p_t0'])
        k.tt('dve', t1[:], t3[:], bre[:], ALU.mult, ['p_ldt', 'p_bre'], ['p_t1'])
        k.tt('dve', Bp[:, :, 1, :], v4(t0[:]), v4(t1[:]), ALU.add, ['p_t0', 'p_t1'], ['c_Bpad'])
        k.ts('dve', t3[:], xr[:], C['sig'][:, 0:1], None, ALU.mult, None, ['p_xr', 'c_sig'], ['p_ldt'])
        k.ts('dve', bre[:], xi[:], C['sig'][:, 0:1], None, ALU.mult, None, ['p_xi', 'c_sig'], ['p_bre'])
        _table_from(k, 'pre', t3[:], 'p_ldt', bre[:], 'p_bre', C['pre'][:, 0, :], 'c_pre0', C['pre'][:, 1, :],
                    'c_pre1', (t0[:], t1[:], t2[:]), ('p_t0', 'p_t1', 'p_t2'), ti[:], 'p_ti')
        k.act(sm[:, 3, :], sm[:, 2, :], AF.Exp, ['p_sm'], ['p_sm3'])
        k.tt('dve', sm[:, 4, :], sm[:, 0, :], sm[:, 3, :], ALU.mult, ['p_sm', 'p_sm3'], ['p_sm4'])
        k.tt('dve', sm[:, 5, :], sm[:, 1, :], sm[:, 3, :], ALU.mult, ['p_sm', 'p_sm3'], ['p_sm5'])
        k.ts('dve', sm[:, 5, :], sm[:, 5, :], 1.0 / TWO_PI, None, ALU.mult, None, ['p_sm5'], ['p_sm5'])
        _table_from(k, 'as', sm[:, 4, :], 'p_sm4', sm[:, 5, :], 'p_sm5', C['a_sp'][:, 0, :], 'c_asp0',
                    C['a_sp'][:, 1, :], 'c_asp1', (sm[:, 6, :], sm[:, 7, :], sm[:, 3, :]),
                    ('p_sm6', 'p_sm7', 'p_sm3'), smi[:], 'p_smi')
        tr_b = C['trow'][:, :].unsqueeze(1).broadcast_to([128, 16, 128])
        v16 = lambda a: a.rearrange("p (c x) -> p c x", c=16)
        k.tt('dve', v16(t3[:]), tr_b, sm[:, 4, :].unsqueeze(2).broadcast_to([128, 16, 128]), ALU.mult,
             ['c_trow', 'p_sm4'], ['p_ldt'])
        k.tt('dve', v16(bre[:]), tr_b, sm[:, 5, :].unsqueeze(2).broadcast_to([128, 16, 128]), ALU.mult,
             ['c_trow', 'p_sm5'], ['p_bre'])
        _table_from(k, 'post', t3[:], 'p_ldt', bre[:], 'p_bre',
                    C['post'][:, 0].rearrange("p c x -> p (c x)"), 'c_post0',
                    C['post'][:, 1].rearrange("p c x -> p (c x)"), 'c_post1',
                    (t0[:], t1[:], t2[:]), ('p_t0', 'p_t1', 'p_t2'), ti[:], 'p_ti')
        c16 = lambda a: a.rearrange("p (c x) -> p c x", c=16)
        k.dma('sp', [(c16(t0[:]), T['c_reT_pad'][d]), (c16(t1[:]), T['c_imT_pad'][d])], writes=['p_t0', 'p_t1'])
        k.cp('dve', C['Cpad'][:, 0], c16(t0[:]), ['p_t0'], ['c_Cpad'])
        k.ts('dve', C['Cpad'][:, 1], c16(t1[:]), -1.0, None, ALU.mult, None, ['p_t1'], ['c_Cpad'])
        k.cp('dve', C['trib'][:], C['tri'][:], ['c_tri'], ['c_trib'])
        k.barrier()


def ssm_alloc(nc, es, pfx):
    def sb(n, s, dt=F32):
        return es.enter_context(nc.sbuf_tensor(pfx + n, s, dt))
    C = {}
    C['Bpad'] = sb("Bpad", [128, 4, 2, 512], BF16)
    C['pre'] = sb("pre", [128, 2, 2048])
    C['post'] = sb("post", [128, 2, 16, 128])
    C['Cpad'] = sb("Cpad", [128, 2, 16, 128], BF16)
    C['a_sp'] = sb("a_sp", [128, 2, 16])
    C['sig'] = sb("sig", [128, 1])
    C['trow'] = sb("trow", [128, 128])
    C['tri'] = sb("tri", [128, 128])
    C['trib'] = sb("trib", [128, 128], BF16)
    C['zre'] = sb("zre", [128, 2048], BF16)
    C['zim'] = sb("zim", [128, 2048], BF16)
    C['H'] = sb("H", [128, 2, 16, 128], BF16)
    C['cc'] = [sb("cc%d" % i, [128, 2, 16], BF16) for i in range(2)]
    C['ccf'] = sb("ccf", [128, 4, 16])
    C['w'] = [sb("w%d" % i, [128, 512]) for i in range(4)]
    return C


def ssm_block(k, C, uT, uTk, c0, PS, d, blk, first, ident, yps, ypsk, PK=None):
    if PK is None:
        PK = {n: n for n in ('bu0', 'bu1', 'cu0', 'cu1')}
    cin = C['cc'][blk % 2]
    cink = 'c_cc%d' % (blk % 2)
    cout = C['cc'][(blk + 1) % 2]
    coutk = 'c_cc%d' % ((blk + 1) % 2)
    pre = C['pre']
    post = C['post']
    H = C['H']
    w = C['w']
    for ct in range(4):
        for comp in range(2):
            k.op('pe', lambda e, ct=ct, comp=comp: e.matmul(PS['bu%d' % comp][:, :], lhsT=uT[:, ct, c0:c0 + 128],
                                                          rhs=C['Bpad'][:, ct, comp, :], start=True, stop=True),
                 reads=[uTk, 'c_Bpad'], writes=[PK['bu%d' % comp]])
        pr = pre[:, 0, ct * 512:(ct + 1) * 512]
        pi_ = pre[:, 1, ct * 512:(ct + 1) * 512]
        bur = PS['bu0'][:, :]
        bui = PS['bu1'][:, :]
        k.tt('dve', w[0][:], pr, bur, ALU.mult, ['c_pre0', PK['bu0']], ['c_w0'])
        k.tt('dve', w[1][:], pi_, bui, ALU.mult, ['c_pre1', PK['bu1']], ['c_w1'])
        k.tt('dve', w[2][:], pr, bui, ALU.mult, ['c_pre0', PK['bu1']], ['c_w2'])
        k.tt('dve', w[3][:], pi_, bur, ALU.mult, ['c_pre1', PK['bu0']], ['c_w3'])
        k.tt('pool', C['zre'][:, ct * 512:(ct + 1) * 512], w[0][:], w[1][:], ALU.subtract, ['c_w0', 'c_w1'],
             ['c_zre%d' % ct])
        k.tt('pool', C['zim'][:, ct * 512:(ct + 1) * 512], w[2][:], w[3][:], ALU.add, ['c_w2', 'c_w3'],
             ['c_zim%d' % ct])
        for comp, zz, zk in ((0, C['zre'], 'c_zre%d' % ct), (1, C['zim'], 'c_zim%d' % ct)):
            for q in range(4):
                pair = ct * 4 + q
                k.op('pe', lambda e, zz=zz, q=q, ct=ct, comp=comp: e.matmul(
                    PS['cu%d' % comp][:, q * 128:(q + 1) * 128],
                    lhsT=zz[:, ct * 512 + q * 128:ct * 512 + (q + 1) * 128], rhs=C['trib'][:, :],
                    start=True, stop=first), reads=[zk, 'c_trib'], writes=[PK['cu%d' % comp]], signal=first)
                if not first:
                    k.op('pe', lambda e, q=q, pair=pair, comp=comp: e.matmul(
                        PS['cu%d' % comp][:, q * 128:(q + 1) * 128], lhsT=ident,
                        rhs=cin[:, comp, pair:pair + 1].broadcast_to([128, 128]), start=False, stop=True),
                        reads=[cink, 'ident'], writes=[PK['cu%d' % comp]])
        por = post[:, 0, ct * 4:(ct + 1) * 4, :].rearrange("p c x -> p (c x)")
        poi = post[:, 1, ct * 4:(ct + 1) * 4, :].rearrange("p c x -> p (c x)")
        cur = PS['cu0'][:, :]
        cui = PS['cu1'][:, :]
        k.tt('dve', w[0][:], por, cur, ALU.mult, ['c_post0', PK['cu0']], ['c_w0'])
        k.tt('dve', w[1][:], poi, cui, ALU.mult, ['c_post1', PK['cu1']], ['c_w1'])
        k.tt('dve', w[2][:], por, cui, ALU.mult, ['c_post0', PK['cu1']], ['c_w2'])
        k.tt('dve', w[3][:], poi, cur, ALU.mult, ['c_post1', PK['cu0']], ['c_w3'])
        k.tt('pool', H[:, 0, ct * 4:(ct + 1) * 4, :].rearrange("p c x -> p (c x)"), w[0][:], w[1][:], ALU.subtract,
             ['c_w0', 'c_w1'], ['c_H0_%d' % ct])
        k.tt('pool', H[:, 1, ct * 4:(ct + 1) * 4, :].rearrange("p c x -> p (c x)"), w[2][:], w[3][:], ALU.add,
             ['c_w2', 'c_w3'], ['c_H1_%d' % ct])
        n = 0
        for comp in range(2):
            for q in range(4):
                pair = ct * 4 + q
                k.op('pe', lambda e, comp=comp, pair=pair, ct=ct, n=n: e.matmul(
                    yps[:, ct, :], lhsT=C['Cpad'][:, comp, pair, :], rhs=H[:, comp, pair, :],
                    start=(n == 0), stop=(n == 7)), reads=['c_Cpad', 'c_H%d_%d' % (comp, ct)], writes=[ypsk],
                    signal=(n == 7))
                n += 1
    last = 127 if d == 0 else 0
    hk = ['c_H%d_%d' % (comp, ct) for comp in range(2) for ct in range(4)]
    hr = H[:, 0, :, last]
    hi = H[:, 1, :, last]
    ar = C['a_sp'][:, 0, :]
    ai = C['a_sp'][:, 1, :]
    f = C['ccf']
    k.tt('pool', f[:, 0, :], ar, hr, ALU.mult, ['c_asp0'] + hk, ['c_ccf0'])
    k.tt('pool', f[:, 1, :], ai, hi, ALU.mult, ['c_asp1'] + hk, ['c_ccf1'])
    k.tt('pool', f[:, 2, :], ar, hi, ALU.mult, ['c_asp0'] + hk, ['c_ccf2'])
    k.tt('pool', f[:, 3, :], ai, hr, ALU.mult, ['c_asp1'] + hk, ['c_ccf3'])
    k.tt('pool', cout[:, 0, :], f[:, 0, :], f[:, 1, :], ALU.subtract, ['c_ccf0', 'c_ccf1'], [coutk])
    k.tt('pool', cout[:, 1, :], f[:, 2, :], f[:, 3, :], ALU.add, ['c_ccf2', 'c_ccf3'], [coutk])


def ssm_host_layouts(lam_re, lam_im, log_dt, b_re, b_im, c_re, c_im):
    f = np.float32
    out = {}
    out['lam_re_bp'] = np.ascontiguousarray(np.broadcast_to(lam_re.reshape(2, 1, 2048), (2, 128, 2048))).astype(f)
    out['lam_im_bp'] = np.ascontiguousarray(np.broadcast_to(lam_im.reshape(2, 1, 2048), (2, 128, 2048))).astype(f)
    out['log_dt_bp'] = np.ascontiguousarray(np.broadcast_to(np.repeat(log_dt, 64, axis=1).reshape(2, 1, 2048),
                                                            (2, 128, 2048))).astype(f)
    bre = np.zeros((2, 128, 4, 8, 64), f)
    bim = np.zeros((2, 128, 4, 8, 64), f)
    for d in range(2):
        for ct in range(4):
            for gl in range(8):
                bre[d, gl * 16:(gl + 1) * 16, ct, gl, :] = b_re[d, ct * 8 + gl].T
                bim[d, gl * 16:(gl + 1) * 16, ct, gl, :] = b_im[d, ct * 8 + gl].T
    out['b_reT_pad'] = bre.reshape(2, 128, 2048)
    out['b_imT_pad'] = bim.reshape(2, 128, 2048)
    out['lam_re_sp'] = np.ascontiguousarray(lam_re.reshape(2, 16, 2, 64).transpose(0, 2, 3, 1).reshape(2, 128, 16))
    out['lam_im_sp'] = np.ascontiguousarray(lam_im.reshape(2, 16, 2, 64).transpose(0, 2, 3, 1).reshape(2, 128, 16))
    ld = log_dt.reshape(2, 16, 2).transpose(0, 2, 1)
    out['log_dt_sp'] = np.ascontiguousarray(np.repeat(ld[:, :, None, :], 64, axis=2).reshape(2, 128, 16)).astype(f)
    cre = np.zeros((2, 128, 16, 128), f)
    cim = np.zeros((2, 128, 16, 128), f)
    for d in range(2):
        for g in range(32):
            pair, g2 = g // 2, g % 2
            col = (g % 8) * 16
            cre[d, g2 * 64:(g2 + 1) * 64, pair, col:col + 16] = c_re[d, g].T
            cim[d, g2 * 64:(g2 + 1) * 64, pair, col:col + 16] = c_im[d, g].T
    out['c_reT_pad'] = cre
    out['c_imT_pad'] = cim
    sig = np.arange(128, dtype=f)
    out['nsig'] = np.stack([-sig, -(127 - sig)]).reshape(2, 128, 1).astype(f)
    tr = np.broadcast_to(sig[None, :], (128, 128))
    out['trow'] = np.ascontiguousarray(np.stack([tr, 127 - tr])).astype(f)
    ii = np.arange(128)
    out['tri'] = np.stack([(ii[:, None] <= ii[None, :]), (ii[:, None] >= ii[None, :])]).astype(f)
    return out


SSM_SHAPES = {'lam_re_bp': [2, 128, 2048], 'lam_im_bp': [2, 128, 2048], 'log_dt_bp': [2, 128, 2048],
              'b_reT_pad': [2, 128, 2048], 'b_imT_pad': [2, 128, 2048], 'lam_re_sp': [2, 128, 16],
              'lam_im_sp': [2, 128, 16], 'log_dt_sp': [2, 128, 16], 'c_reT_pad': [2, 128, 16, 128],
              'c_imT_pad': [2, 128, 16, 128], 'nsig': [2, 128, 1], 'trow': [2, 128, 128], 'tri': [2, 128, 128]}


def prenorm4(k, src, t0, xt, xtk, hn, hnk, hT, hTk, ptr, ptrk, ident, stt, sk, junk, eps_eff, skip_load=()):
    for i in range(4):
        if i not in skip_load:
            k.dma('sp', [(xt[i][:], src[t0 + i * 128:t0 + (i + 1) * 128, :])], writes=KL(xtk[i]))
        k.act(junk[:], xt[i][:], AF.Square, KL(xtk[i]), [sk + 'junk', sk + 'ss%d' % i], accum=stt[:, i:i + 1])
    k.ts('dve', stt[:, 4:8], stt[:, 0:4], 1.0 / D, eps_eff, ALU.mult, ALU.add, [sk + 'ss%d' % i for i in range(4)],
         [sk + 'v'])
    rsqrt_dve(k, stt[:, 8:12], sk + 'r', stt[:, 4:8], sk + 'v', (stt[:, 12:16], stt[:, 16:20], stt[:, 20:24]),
              (sk + 't0', sk + 't1', sk + 't2'), 4)
    for i in range(4):
        h = hn[i % len(hn)]
        hk = hnk[i % len(hn)]
        k.ts('dve', h[:], xt[i][:], stt[:, 8 + i:9 + i], None, ALU.mult, None, KL(xtk[i]) + [sk + 'r'], [hk])
        p = ptr[i % len(ptr)]
        pk = ptrk[i % len(ptr)]
        for kt in range(8):
            k.op('pe', lambda e, kt=kt, p=p, h=h: e.transpose(p[:, kt, :], h[:, kt * 128:(kt + 1) * 128], ident),
                 reads=[hk], writes=[pk], signal=(kt == 7))
        k.cp('act' if i % 2 == 0 else 'dve', hT[:, :, i * 128:(i + 1) * 128], p[:, :, :], [pk], [hTk])


def att_rows(R):
    out = []
    for r in range(R):
        rs = min(max(r - 4, 0), R - 8)
        pr0, pr1 = rs // 2, (rs + 7) // 2
        lst = []
        for pr in range(pr0, pr1 + 1):
            lo_ok = (2 * pr >= rs)
            hi_ok = (2 * pr + 1 <= rs + 7)
            drf = 2 * pr - r
            if lo_ok and hi_ok:
                lst.append((pr, drf + 7))
            elif not lo_ok:
                assert drf == -5
                lst.append((pr, 14))
            else:
                assert drf == 3
                lst.append((pr, 18))
        out.append(lst)
    return out


def att_table_host(rpb):
    tbl = np.full((128, 8, 19, 64), NEG, np.float32)
    qc = np.arange(64)
    cs = np.clip(qc - 8, 0, 48)
    entries = [(e - 7, True, True) for e in range(14)] + [(-5, False, True), (-3, True, True), (-1, True, True),
                                                          (1, True, True), (3, True, False)]
    for e, (drf, lo, hi) in enumerate(entries):
        for half, ok in ((0, lo), (1, hi)):
            if not ok:
                continue
            dr = drf + half
            for kc in range(64):
                okc = (kc >= cs) & (kc < cs + 16)
                dc = kc - qc + 15
                vals = rpb[:, dr + 7, np.clip(dc, 0, 30)]
                tbl[half * 64 + kc, :, e, :] = np.where(okc[None, :], vals, NEG)
    return tbl


def pass1(k, cfg, T):
    nc = k.nc
    with ExitStack() as es:
        def sb(n, s, d=F32):
            return es.enter_context(nc.sbuf_tensor("p1_" + n, s, d))

        def pp(n, s, d=F32):
            return es.enter_context(nc.psum_tensor("p1_" + n, s, d))
        C = ssm_alloc(nc, es, "p1s_")
        ssm_prep(k, T, 0, C)
        W = sb("W", [128, 8, 2048], BF16)
        tbl = sb("tbl", [128, 8, 19, 64], BF16)
        xt = [sb("xt%d" % i, [128, D]) for i in range(4)]
        xtk = ['p1xt%d' % i for i in range(4)]
        hn = [sb("hn%d" % i, [128, D], BF16) for i in range(2)]
        hT = sb("hT", [128, 8, 512], BF16)
        qT = [sb("qT%d" % i, [128, 4, 512], BF16) for i in range(2)]
        kT = [sb("kT%d" % i, [128, 4, 512], BF16) for i in range(3)]
        va = [sb("va%d" % i, [128, 4, 8, 65], BF16) for i in range(3)]
        uT = sb("uT", [128, 4, 512], BF16)
        attT = sb("attT", [128, 4, 512], BF16)
        att = sb("att", [128, 512], BF16)
        sadd = [sb("sadd%d" % i, [128, 320]) for i in range(2)]
        pT = [sb("pT%d" % i, [128, 320], BF16) for i in range(2)]
        rec = sb("rec", [128, 8])
        yfb = [sb("yfb%d" % i, [128, 4, 128]) for i in range(2)]
        g1col = sb("g1col", [128, 8])
        ident = sb("ident", [128, 128], BF16)
        identf = sb("identf", [128, 128])
        stt = sb("stt", [128, 24])
        junk = sb("junk", [128, D], BF16)
        B0 = [pp("B0_%d" % i, [128, 8, 128], BF16) for i in range(1)]
        B1 = pp("B1", [128, 512])
        B2 = pp("B2", [128, 512])
        B3 = pp("B3", [128, 512])
        B4 = pp("B4", [128, 512])
        B5 = pp("B5", [128, 512])
        B6 = pp("B6", [128, 512])
        B7 = pp("B7", [128, 4, 128])
        PS = {'bu0': B3, 'bu1': B4, 'cu0': B5, 'cu1': B6}
        PK = {'bu0': 'B3', 'bu1': 'B4', 'cu0': 'B5', 'cu1': 'B6'}
        B7K = ['B7_%d' % i for i in range(4)]
        WH = [xt[2][:, 0:512], xt[2][:, 512:1024], xt[3][:, 0:512], xt[3][:, 512:1024]]
        WHK = [xtk[2], xtk[2] + 'b', xtk[3], xtk[3] + 'b']

        k.dma('sp', [(g1col[:], T['g_mix_pre_col'][:, :]), (identf[:], T['identf'][:, :])],
              writes=['p1g1col', 'p1identf'])
        k.cp('dve', ident[:], identf[:], ['p1identf'], ['ident'])
        win_v = T['w_in'].rearrange("(kt p) c -> p kt c", p=128)
        load_convert_weight(k, W, 'p1W', win_v[:, :, 0:2048], 8, 2048, [t[:] for t in xt], xtk, scale_col=g1col,
                            chunk=128)
        tb_v = T['att_tbl'].rearrange("p h e q -> p (h e q)")
        tb_d = tbl[:].rearrange("p h e q -> p (h e q)")
        for c0 in range(0, 8 * 19 * 64, 1024):
            cwid = min(1024, 8 * 19 * 64 - c0)
            i = (c0 // 1024) % 4
            k.dma('sp', [(xt[i][:, 0:cwid], tb_v[:, c0:c0 + cwid])], writes=[xtk[i]])
            k.cp('dve', tb_d[:, c0:c0 + cwid], xt[i][:, 0:cwid], [xtk[i]], ['p1tbl'])
        for i in range(3):
            k.op('dve', lambda e, i=i: e.memset(va[i][:], 1.0), writes=['va%d' % i])

        def make_att_items(am, off, rows):
            items = []
            q_ = qT[am % 2]
            qk = 'qT%d' % (am % 2)
            ta0 = off + am * 512
            state = {'ui': 0}

            SB = [(B1, 'B1'), (B0[0][:].rearrange("p a b -> p (a b)").bitcast(F32), 'B0')]

            def unit_s(tile, lr2, h):
                lr = tile * 2 + lr2
                lst = rows[am * 8 + lr]
                na = len(lst)
                ct, hp = h // 2, h % 2
                ps0, ps1 = 64 * hp, 64 * hp + 64
                ui = state['ui']
                state['ui'] += 1
                sbk, sk_ = SB[ui % 2]
                for a, (pr, ent) in enumerate(lst):
                    pm, pt = pr // 4, pr % 4
                    kt_ = kT[pm % 3]
                    k.op('pe', lambda e, a=a, kt_=kt_, pt=pt: e.matmul(
                        sbk[:, a * 64:(a + 1) * 64], lhsT=kt_[ps0:ps1, ct, pt * 128:(pt + 1) * 128],
                        rhs=q_[ps0:ps1, ct, lr * 64:(lr + 1) * 64], start=True, stop=True),
                        reads=['kT%d' % (pm % 3), qk], writes=[sk_], signal=(a == na - 1))
                sa = sadd[ui % 2]
                sak = 'sadd%d' % (ui % 2)
                e0 = lst[0][1]
                tv = tbl[:, h, e0:e0 + 7:2, :] if na == 4 else tbl[:, h, 14:19, :]
                k.tt('dve', sa[:, 0:na * 64].rearrange("p (a q) -> p a q", a=na),
                     sbk[:, 0:na * 64].rearrange("p (a q) -> p a q", a=na), tv, ALU.add, [sk_, 'p1tbl'], [sak])
                pt_ = pT[ui % 2]
                ptk = 'pT%d' % (ui % 2)
                k.act(pt_[:, 0:na * 64], sa[:, 0:na * 64], AF.Exp, [sak], [ptk])
                return (lr2, h, lst, pt_, ptk)

            def unit_pv(info):
                lr2, h, lst, pt_, ptk = info
                na = len(lst)
                hh = h % 4
                for a, (pr, ent) in enumerate(lst):
                    pm, pt = pr // 4, pr % 4
                    k.op('pe', lambda e, a=a, pm=pm, pt=pt: e.matmul(
                        B2[64 * lr2:64 * lr2 + 64, hh * 65:hh * 65 + 65],
                        lhsT=pt_[:, a * 64:(a + 1) * 64], rhs=va[pm % 3][:, pt, h, :],
                        start=(a == 0), stop=(a == na - 1)),
                        reads=[ptk, 'va%d' % (pm % 3)], writes=['B2'], signal=(a == na - 1))

            def norm(g):
                ov = B2[:, 0:260].rearrange("p (h d) -> p h d", h=4)
                k.op('dve', lambda e: e.reciprocal(out=rec[:, g * 4:(g + 1) * 4], in_=ov[:, :, 64]),
                     reads=['B2'], writes=['rec%d' % g])
                k.tt('dve', att[:, g * 256:(g + 1) * 256].rearrange("p (h d) -> p h d", h=4),
                     ov[:, :, 0:64], rec[:, g * 4:(g + 1) * 4].unsqueeze(2).broadcast_to([128, 4, 64]),
                     ALU.mult, ['B2', 'rec%d' % g], ['att'])

            def trans(tile):
                for ct in range(4):
                    k.op('pe', lambda e, ct=ct: e.transpose(B0[0][:, ct, :], att[:, ct * 128:(ct + 1) * 128],
                                                            ident[:]),
                         reads=['att'], writes=['B0'], signal=(ct == 3))
                k.cp('act', attT[:, :, tile * 128:(tile + 1) * 128], B0[0][:, 0:4, :], ['B0'], ['attT'])

            def spill():
                k.dma('pool', [(T['attT_s'][ct, :, ta0:ta0 + 512], attT[:, ct, :]) for ct in range(4)],
                      reads=['attT'], writes=['attT_s'])

            seq = []
            for tile in range(4):
                for g in range(2):
                    for lr2 in range(2):
                        for hh in range(4):
                            seq.append(('u', tile, lr2, g * 4 + hh))
                    seq.append(('n', g))
                seq.append(('t', tile))
            pend = {'info': None, 'post': []}

            def step(ent_):
                if ent_[0] == 'u':
                    info = unit_s(ent_[1], ent_[2], ent_[3])
                    flush()
                    pend['info'] = info
                else:
                    pend['post'].append(ent_)

            def flush():
                if pend['info'] is not None:
                    unit_pv(pend['info'])
                    pend['info'] = None
                for p_ in pend['post']:
                    if p_[0] == 'n':
                        norm(p_[1])
                    else:
                        trans(p_[1])
                pend['post'] = []

            for ent_ in seq:
                items.append(lambda ent_=ent_: step(ent_))
            items.append(flush)
            items.append(spill)
            return items

        x = T['x']
        starts1 = [off_ + m_ * 512 for (off_, L_) in cfg['seqs'] for m_ in range(L_ // 512)]
        pre1 = set()
        pre1b = set()
        for (off, L) in cfg['seqs']:
            R = L // 64
            nmb = L // 512
            rows = att_rows(R)
            att_items = []
            for mb in range(nmb + 1):
                t0 = off + mb * 512
                if mb < nmb:
                    xk2 = [(q_, q_ + 'b') for q_ in xtk]
                    prenorm4(k, x, t0, xt, xk2, hn, ['p1hn0', 'p1hn1'], hT, 'p1hT', B0, ['B0'], ident[:], stt,
                             'p1s_', junk, EPS, skip_load=((0, 1) if t0 in pre1 else ()) + ((2, 3) if t0 in pre1b else ()))
                    nxt = starts1.index(t0) + 1
                    if nxt < len(starts1):
                        tn = starts1[nxt]
                        for i_ in range(2):
                            k.dma('sp', [(xt[i_][:], x[tn + i_ * 128:tn + (i_ + 1) * 128, :])], writes=KL(xk2[i_]))
                        pre1.add(tn)
                    q_ = qT[mb % 2]
                    qk = 'qT%d' % (mb % 2)
                    k_ = kT[mb % 3]
                    kk = 'kT%d' % (mb % 3)
                    v_ = va[mb % 3]
                    vk = 'va%d' % (mb % 3)
                    nacc = 0
                    for grp, dst, dk, c0w, scale in ((0, q_, qk, 0, 0.125), (1, k_, kk, 512, 1.0),
                                                      (2, uT, 'uT', 1536, 1.0)):
                        for j in range(4):
                            pb, pbk = (B1, 'B1') if nacc % 2 == 0 else (B2, 'B2')
                            nacc += 1
                            for kt in range(8):
                                k.op('pe', lambda e, kt=kt, j=j, pb=pb, c0w=c0w: e.matmul(
                                    pb[:, :], lhsT=W[:, kt, c0w + j * 128:c0w + (j + 1) * 128], rhs=hT[:, kt, :],
                                    start=(kt == 0), stop=(kt == 7)), reads=['p1W', 'p1hT'], writes=[pbk],
                                    signal=(kt == 7))
                            if nacc % 2 == 0:
                                k.act(dst[:, j, :], pb[:, :], AF.Copy, [pbk], [dk], scale=scale)
                            else:
                                k.ts('dve', dst[:, j, :], pb[:, :], scale, None, ALU.mult, None, [pbk], [dk])
                    for i in range(4):
                        pb, pbk = (B1, 'B1') if nacc % 2 == 0 else (B2, 'B2')
                        nacc += 1
                        for kt in range(8):
                            k.op('pe', lambda e, kt=kt, i=i, pb=pb: e.matmul(
                                pb[:, :], lhsT=hT[:, kt, i * 128:(i + 1) * 128], rhs=W[:, kt, 1024:1536],
                                start=(kt == 0), stop=(kt == 7)), reads=['p1W', 'p1hT'], writes=[pbk],
                                signal=(kt == 7))
                        k.cp('act' if i % 2 == 0 else 'dve', v_[:, i, :, 0:64],
                             pb[:, :].rearrange("p (h d) -> p h d", h=8), [pbk], [vk])
                    k.dma('pool', [(T['uT_s'][ct, :, t0:t0 + 512], uT[:, ct, :]) for ct in range(4)], reads=['uT'],
                          writes=['uT_s'])
                if mb < nmb:
                    def on_blk(gb, c0, t0=t0):
                        b = c0 // 128
                        yb_ = yfb[b % 2]
                        ybk = 'yfb%d' % (b % 2)
                        k.cp('act', yb_[:], B7[:, :, :], B7K, [ybk])
                        k.dma('pool', [(T['yf_s'][ct, :, t0 + c0:t0 + c0 + 128], yb_[:, ct, :]) for ct in range(4)],
                              reads=[ybk], writes=['yf_s'])
                    ssm_mb(k, C, uT, 'uT', PS, PK, 0, [(mb * 4 + b, b * 128) for b in range(4)], ident[:], B7, B7K,
                           WH, WHK, on_blk, filler=att_items)
                    nxt = starts1.index(t0) + 1
                    if nxt < len(starts1):
                        tn = starts1[nxt]
                        for i_ in (2, 3):
                            k.dma('sp', [(xt[i_][:], x[tn + i_ * 128:tn + (i_ + 1) * 128, :])], writes=KL(xk2[i_]))
                        pre1b.add(tn)
                while att_items:
                    att_items.pop(0)()
                if mb < nmb:
                    att_items = make_att_items(mb, off, rows)
    k.barrier()


def pass2(k, cfg, T):
    nc = k.nc
    with ExitStack() as es:
        def sb(n, s, d=F32):
            return es.enter_context(nc.sbuf_tensor("p2_" + n, s, d))

        def pp(n, s, d=F32):
            return es.enter_context(nc.psum_tensor("p2_" + n, s, d))
        C = ssm_alloc(nc, es, "p2s_")
        ssm_prep(k, T, 1, C)
        Wg = sb("Wg", [128, 8, 2048], BF16)
        Wba = sb("Wba", [128, 4, D], BF16)
        Wbs = sb("Wbs", [128, 4, D], BF16)
        Wout = sb("Wout", [128, 8, D], BF16)
        Wglu = sb("Wglu", [128, 4, 512], BF16)
        xt = [sb("xt%d" % i, [128, D]) for i in range(4)]
        xtk = ['p2xt%d' % i for i in range(4)]
        hn = [sb("hn0", [128, D], BF16)]
        hT = sb("hT", [128, 8, 512], BF16)
        uT = sb("uT", [128, 4, 512], BF16)
        attT = sb("attT", [128, 4, 512], BF16)
        ssmT = sb("ssmT", [128, 4, 512], BF16)
        mT = sb("mT", [128, 8, 512], BF16)
        yfb = sb("yfb", [128, 4, 128])
        yw = [sb("yw%d" % i, [128, 4, 128]) for i in range(3)]
        ygT = sb("ygT", [128, 4, 128], BF16)
        ybuf = sb("ybuf", [128, 4, 128])
        gbc = sb("gbc", [128, D])
        g1col = sb("g1col", [128, 8])
        dcol = sb("dcol", [128, 4])
        bglu = sb("bglu", [128, 4])
        ident = sb("ident", [128, 128], BF16)
        identf = sb("identf", [128, 128])
        stt = sb("stt", [128, 32])
        junk = sb("junk", [128, D], BF16)
        B0 = [pp("B0", [128, 8, 128], BF16)]
        B1 = pp("B1", [128, 512])
        B2 = pp("B2", [128, 512])
        B3 = pp("B3", [128, 512])
        B4 = pp("B4", [128, 512])
        B5 = pp("B5", [128, 512])
        B6 = pp("B6", [128, 512])
        B7 = pp("B7", [128, 4, 128])
        PS = {'bu0': B3, 'bu1': B4, 'cu0': B5, 'cu1': B6}
        PK = {'bu0': 'B3', 'bu1': 'B4', 'cu0': 'B5', 'cu1': 'B6'}
        B7K = ['B7_%d' % i for i in range(4)]
        B0f = B0[0][:].rearrange("p a b -> p (a b)").bitcast(F32)
        WH = [xt[2][:, 0:512], xt[2][:, 512:1024], xt[3][:, 0:512], xt[3][:, 512:1024]]
        WHK = [xtk[2], xtk[2] + 'b', xtk[3], xtk[3] + 'b']

        k.dma('sp', [(g1col[:], T['g_mix_pre_col'][:, :]), (identf[:], T['identf'][:, :]),
                     (gbc[:], T['g_mix_post_bc'][:, :]), (dcol[:], T['ssm_d_col'][:, :]),
                     (bglu[:], T['b_glu_col'][:, :])],
              writes=['p2g1col', 'p2identf', 'p2gbc', 'p2dcol', 'p2bglu'])
        k.cp('dve', ident[:], identf[:], ['p2identf'], ['ident'])
        k.ts('dve', bglu[:], bglu[:], 0.5, None, ALU.mult, None, ['p2bglu'], ['p2bglu'])
        win_v = T['w_in'].rearrange("(kt p) c -> p kt c", p=128)
        xs = [t[:] for t in xt]
        load_convert_weight(k, Wg, 'p2Wg', win_v[:, :, 2048:4096], 8, 2048, xs, xtk, scale_col=g1col, chunk=128)
        load_convert_weight(k, Wba, 'p2Wba', T['w_branch_att'].rearrange("(kt p) c -> p kt c", p=128), 4, D, xs,
                            xtk, chunk=256)
        load_convert_weight(k, Wbs, 'p2Wbs', T['w_branch_ssm'].rearrange("(kt p) c -> p kt c", p=128), 4, D, xs,
                            xtk, const_scale=0.25, chunk=256)
        load_convert_weight(k, Wout, 'p2Wout', T['w_out'].rearrange("(kt p) c -> p kt c", p=128), 8, D, xs, xtk,
                            chunk=128)
        load_convert_weight(k, Wglu, 'p2Wglu', T['w_glu'].rearrange("(kt p) c -> p kt c", p=128), 4, 512, xs, xtk,
                            chunk=256)
        x = T['x']
        starts2 = [off_ + m_ * 512 for (off_, L_) in cfg['seqs'] for m_ in reversed(range(L_ // 512))]
        pre2 = set()
        for (off, L) in cfg['seqs']:
            nmb = L // 512
            for mi, mb in enumerate(reversed(range(nmb))):
                t0 = off + mb * 512
                xk2 = [(q_, q_ + 'b') for q_ in xtk]
                prenorm4(k, x, t0, xt, xk2, hn, ['p2hn0'], hT, 'p2hT', B0, ['B0'], ident[:], stt, 'p2s_',
                         junk, EPS, skip_load=(0, 1, 2, 3) if t0 in pre2 else ())
                k.dma('sp', [(uT[:, ct, :], T['uT_s'][ct, :, t0:t0 + 512]) for ct in range(4)], reads=['uT_s'],
                      writes=['uT'])
                k.dma('sp', [(attT[:, ct, :], T['attT_s'][ct, :, t0:t0 + 512]) for ct in range(4)],
                      reads=['attT_s'], writes=['attT'])
                fill = []

                def on_blk(gb, c0, t0=t0):
                    y0, y1, y2 = yw
                    pend_ = [f for f in fill if getattr(f, 'chain', False)]
                    for f in pend_:
                        fill.remove(f)
                        f()
                    it = []
                    k.cp('act', ybuf[:], B7[:, :, :], B7K, ['ybuf'])
                    it.append(lambda: k.dma('sp', [(yfb[:, ct, :], T['yf_s'][ct, :, t0 + c0:t0 + c0 + 128])
                                                   for ct in range(4)], reads=['yf_s'], writes=['yfb']))
                    it.append(lambda: k.tt('dve', y0[:], uT[:, :, c0:c0 + 128],
                                           dcol[:, :].unsqueeze(2).broadcast_to([128, 4, 128]), ALU.mult,
                                           ['uT', 'p2dcol'], ['yw0']))
                    it.append(lambda: k.tt('pool', y0[:], y0[:], yfb[:], ALU.add, ['yw0', 'yfb'], ['yw0']))
                    it.append(lambda: k.tt('dve', y0[:], y0[:], ybuf[:], ALU.add, ['yw0', 'ybuf'], ['yw0']))
                    it.append(lambda: k.act(y1[:], y0[:], AF.Square, ['yw0'], ['yw1']))
                    it.append(lambda: k.ts('pool', y1[:], y1[:], GC2, GC1, ALU.mult, ALU.add, ['yw1'], ['yw1']))
                    it.append(lambda: k.tt('pool', y1[:], y1[:], y0[:], ALU.mult, ['yw1', 'yw0'], ['yw1']))
                    it.append(lambda: k.act(y2[:], y1[:], AF.Tanh, ['yw1'], ['yw2'], scale=0.5))
                    it.append(lambda: k.stt('dve', y1[:], y2[:], 1.0, y0[:], ALU.add, ALU.mult, ['yw2', 'yw0'],
                                            ['yw1']))
                    it.append(lambda: k.cp('pool', ygT[:], y1[:], ['yw1'], ['ygT']))

                    def glu_mm():
                        for cto in range(4):
                            for kt in range(4):
                                k.op('pe', lambda e, cto=cto, kt=kt: e.matmul(
                                    B0f[:, cto * 128:(cto + 1) * 128], lhsT=Wglu[:, kt, cto * 128:(cto + 1) * 128],
                                    rhs=ygT[:, kt, :], start=(kt == 0), stop=(kt == 3)), reads=['p2Wglu', 'ygT'],
                                    writes=['B0'], signal=(kt == 3))
                    it.append(glu_mm)

                    def glu_tanh():
                        for cto in range(4):
                            k.act(y2[:, cto, :], B0f[:, cto * 128:(cto + 1) * 128], AF.Tanh, ['B0', 'p2bglu'], ['yw2'],
                                  scale=0.25, bias=bglu[:, cto:cto + 1])
                    it.append(glu_tanh)
                    it.append(lambda: k.stt('dve', ssmT[:, :, c0:c0 + 128], y2[:], 1.0, y1[:], ALU.add, ALU.mult,
                                            ['yw2', 'yw1'], ['ssmT']))
                    for f in it:
                        f.chain = True
                    fill[0:0] = it

                def gate_a(j):
                    for kt in range(8):
                        k.op('pe', lambda e, kt=kt: e.matmul(B1[:, :], lhsT=Wg[:, kt, j * 128:(j + 1) * 128],
                                                          rhs=hT[:, kt, :], start=(kt == 0), stop=(kt == 7)),
                             reads=['p2Wg', 'p2hT'], writes=['B1'], signal=(kt == 7))
                    for kt in range(4):
                        k.op('pe', lambda e, kt=kt: e.matmul(B2[:, :], lhsT=Wba[:, kt, j * 128:(j + 1) * 128],
                                                          rhs=attT[:, kt, :], start=(kt == 0), stop=(kt == 3)),
                             reads=['p2Wba', 'attT'], writes=['B2'], signal=(kt == 3))
                    ga = xt[0][:, 0:512]
                    k.act(ga, B1[:, :], AF.Tanh, ['B1'], [xtk[0]], scale=0.5)
                    k.stt('dve', mT[:, j, :], ga, 1.0, B2[:, :], ALU.add, ALU.mult, [xtk[0], 'B2'], ['mT'])
                fill.extend([(lambda j=j: gate_a(j)) for j in range(8)])
                ssm_mb(k, C, uT, 'uT', PS, PK, 1, [(mi * 4 + bi, b * 128) for bi, b in enumerate(reversed(range(4)))],
                       ident[:], B7, B7K, WH, WHK, on_blk, filler=fill)
                while fill:
                    fill.pop(0)()
                for j in range(8):
                    pa, pak = (B1, 'B1') if j % 2 == 0 else (B3, 'B3')
                    pb_, pbk = (B2, 'B2') if j % 2 == 0 else (B4, 'B4')
                    for kt in range(8):
                        k.op('pe', lambda e, kt=kt, j=j, pa=pa: e.matmul(
                            pa[:, :], lhsT=Wg[:, kt, 1024 + j * 128:1024 + (j + 1) * 128], rhs=hT[:, kt, :],
                            start=(kt == 0), stop=(kt == 7)), reads=['p2Wg', 'p2hT'], writes=[pak], signal=(kt == 7))
                    for kt in range(4):
                        k.op('pe', lambda e, kt=kt, j=j, pb_=pb_: e.matmul(
                            pb_[:, :], lhsT=Wbs[:, kt, j * 128:(j + 1) * 128], rhs=ssmT[:, kt, :],
                            start=(kt == 0), stop=(kt == 3)), reads=['p2Wbs', 'ssmT'], writes=[pbk], signal=(kt == 3))
                    gs = xt[0][:, (j % 2) * 512:(j % 2 + 1) * 512]
                    gsk = xtk[0] if j % 2 == 0 else xtk[0] + 'b'
                    m2 = xt[1][:, (j % 2) * 512:(j % 2 + 1) * 512]
                    m2k = xtk[1] if j % 2 == 0 else xtk[1] + 'b'
                    k.act(gs, pa[:, :], AF.Tanh, [pak], [gsk], scale=0.5)
                    k.stt('dve', m2, gs, 1.0, pb_[:, :], ALU.add, ALU.mult, [gsk, pbk], [m2k])
                    k.tt('pool', mT[:, j, :], mT[:, j, :], m2, ALU.add, ['mT', m2k], ['mT'])
                nxt = starts2.index(t0) + 1
                if nxt < len(starts2):
                    tn = starts2[nxt]
                    for i_ in range(4):
                        k.dma('sp', [(xt[i_][:], x[tn + i_ * 128:tn + (i_ + 1) * 128, :])], writes=KL(xk2[i_]))
                    pre2.add(tn)
                for i in range(4):
                    for h, (pb, pbk) in enumerate(((B1, 'B1'), (B2, 'B2'))):
                        for kt in range(8):
                            k.op('pe', lambda e, kt=kt, i=i, h=h, pb=pb: e.matmul(
                                pb[:, :], lhsT=mT[:, kt, i * 128:(i + 1) * 128], rhs=Wout[:, kt, h * 512:(h + 1) * 512],
                                start=(kt == 0), stop=(kt == 7)), reads=['mT', 'p2Wout'], writes=[pbk],
                                signal=(kt == 7))
                    if i % 2 == 0:
                        xb, xk = attT[:].rearrange("p a b -> p (a b)").bitcast(F32), 'attT'
                    else:
                        xb, xk = uT[:].rearrange("p a b -> p (a b)").bitcast(F32), 'uT'
                    tb_, tbk = ssmT[:].rearrange("p a b -> p (a b)").bitcast(F32), 'ssmT'
                    k.dma('sp', [(xb[:], x[t0 + i * 128:t0 + (i + 1) * 128, :])], writes=[xk])
                    k.act(junk[:, 0:512], B1[:, :], AF.Square, ['B1'], ['p2s_junk', 'p2s_a0'], accum=stt[:, 24:25])
                    k.act(junk[:, 512:1024], B2[:, :], AF.Square, ['B2'], ['p2s_junk', 'p2s_a1'], accum=stt[:, 25:26])
                    k.tt('dve', stt[:, 26:27], stt[:, 24:25], stt[:, 25:26], ALU.add, ['p2s_a0', 'p2s_a1'], ['p2s_a2'])
                    k.ts('dve', stt[:, 27:28], stt[:, 26:27], 1.0 / D, 4.0 * EPS, ALU.mult, ALU.add, ['p2s_a2'],
                         ['p2s_a3'])
                    rsqrt_dve(k, stt[:, 28:29], 'p2s_r2', stt[:, 27:28], 'p2s_a3',
                              (stt[:, 29:30], stt[:, 30:31], stt[:, 31:32]), ('p2s_u0', 'p2s_u1', 'p2s_u2'), 1)
                    k.tt('dve', tb_[:, 0:512], B1[:, :], gbc[:, 0:512], ALU.mult, ['B1', 'p2gbc'], [tbk])
                    k.tt('dve', tb_[:, 512:1024], B2[:, :], gbc[:, 512:1024], ALU.mult, ['B2', 'p2gbc'], [tbk])
                    k.stt('dve', xb[:], tb_[:], stt[:, 28:29], xb[:], ALU.mult, ALU.add, [tbk, 'p2s_r2', xk], [xk])
                    k.dma('pool', [(T['x1s'][t0 + i * 128:t0 + (i + 1) * 128, :], xb[:])], reads=[xk],
                          writes=['x1s_dram'])
    k.barrier()


IN_SHAPES = {
    'x': None, 'w_in': [D, 4096], 'w_up': [D, 2 * DFF], 'w_down': [DFF, D], 'w_branch_att': [512, D],
    'w_branch_ssm': [512, D], 'w_out': [D, D], 'w_glu': [512, 512],
    'g_mix_pre_col': [128, 8], 'g_ffn_pre_col': [128, 8], 'g_mix_post_bc': [128, D], 'g_ffn_post_bc': [128, D],
    'conv_w_l': [128, 44, 3], 'conv_b_l': [128, 44], 'identf': [128, 128], 'ssm_d_col': [128, 4],
    'b_glu_col': [128, 4], 'att_tbl': [128, 8, 19, 64],
}
IN_SHAPES.update(SSM_SHAPES)


def build_program(cfg):
    nc = bass.Bass("TRN2", target_bir_lowering=False)
    NT = cfg['NT']
    T = {}
    for n, shp in IN_SHAPES.items():
        shp = [NT, D] if n == 'x' else shp
        T[n] = nc.dram_tensor(n, shp, F32, kind="ExternalInput").ap()
    T['y'] = nc.dram_tensor('y', [NT, D], F32, kind="ExternalOutput").ap()
    T['x1s'] = nc.dram_tensor('x1s', [NT, D], F32, kind="Internal").ap()
    T['uT_s'] = nc.dram_tensor('uT_s', [4, 128, NT], BF16, kind="Internal").ap()
    T['attT_s'] = nc.dram_tensor('attT_s', [4, 128, NT], BF16, kind="Internal").ap()
    T['yf_s'] = nc.dram_tensor('yf_s', [4, 128, NT], F32, kind="Internal").ap()
    with ExitStack() as es:
        k = KB(nc, es)
        pass1(k, cfg, T)
        pass2(k, cfg, T)
        pass3(k, cfg, T)
        k.finish('pool')
        k.finish('sp')
    return nc


def host_consts(g_mix_pre, g_mix_post, attn_rpb, ssm_lam_re, ssm_lam_im, ssm_log_dt, ssm_b_re, ssm_b_im,
                ssm_c_re, ssm_c_im, ssm_d, b_glu, g_ffn_pre, g_ffn_post, conv_w, conv_b):
    f = np.float32
    c = {}
    c['g_mix_pre_col'] = np.ascontiguousarray(g_mix_pre.reshape(8, 128).T).astype(f)
    c['g_ffn_pre_col'] = np.ascontiguousarray(g_ffn_pre.reshape(8, 128).T).astype(f)
    c['g_mix_post_bc'] = np.ascontiguousarray(np.broadcast_to(g_mix_post.reshape(1, D), (128, D))).astype(f)
    c['g_ffn_post_bc'] = np.ascontiguousarray(np.broadcast_to(g_ffn_post.reshape(1, D), (128, D))).astype(f)
    c['conv_w_l'] = np.ascontiguousarray(conv_w.T.reshape(44, 128, 3).transpose(1, 0, 2)).astype(f)
    c['conv_b_l'] = np.ascontiguousarray(conv_b.reshape(44, 128).T).astype(f)
    c['identf'] = np.eye(128, dtype=f)
    c['ssm_d_col'] = np.ascontiguousarray(ssm_d.reshape(4, 128).T).astype(f)
    c['b_glu_col'] = np.ascontiguousarray(b_glu.reshape(4, 128).T).astype(f)
    c['att_tbl'] = att_table_host(np.asarray(attn_rpb, f))
    c.update(ssm_host_layouts(ssm_lam_re, ssm_lam_im, ssm_log_dt, ssm_b_re, ssm_b_im, ssm_c_re, ssm_c_im))
    return c


_PROG = {}


def run_layer(xs_per_core, cfg, weights, consts, n_cores):
    key = (cfg['NT'], tuple(cfg['seqs']))
    if key not in _PROG:
        _PROG[key] = build_program(cfg)
    nc = _PROG[key]
    in_maps = []
    for c in range(n_cores):
        m = {'x': xs_per_core[c]}
        m.update(weights)
        m.update(consts)
        in_maps.append(m)
    res = run_bass_kernel_spmd(nc, in_maps, core_ids=list(range(n_cores)))
    return [r['y'] for r in res.results]


def kernel(x_prompt, x_sample, g_mix_pre, g_mix_post, w_in, attn_rpb, ssm_lam_re, ssm_lam_im, ssm_log_dt,
           ssm_b_re, ssm_b_im, ssm_c_re, ssm_c_im, ssm_d, w_glu, b_glu, w_branch_att, w_branch_ssm, w_out,
           g_ffn_pre, g_ffn_post, w_up, conv_w, conv_b, w_down):
    f = np.float32
    A = lambda a: np.asarray(a, f)
    x_prompt = A(x_prompt)
    x_sample = A(x_sample)
    consts = host_consts(A(g_mix_pre)[0], A(g_mix_post)[0], A(attn_rpb)[0], A(ssm_lam_re)[0], A(ssm_lam_im)[0],
                         A(ssm_log_dt)[0], A(ssm_b_re)[0], A(ssm_b_im)[0], A(ssm_c_re)[0], A(ssm_c_im)[0],
                         A(ssm_d)[0], A(b_glu)[0], A(g_ffn_pre)[0], A(g_ffn_post)[0], A(conv_w)[0], A(conv_b)[0])
    weights = {'w_in': np.ascontiguousarray(A(w_in)[0]), 'w_up': np.ascontiguousarray(A(w_up)[0]),
               'w_down': np.ascontiguousarray(A(w_down)[0]), 'w_branch_att': np.ascontiguousarray(A(w_branch_att)[0]),
               'w_branch_ssm': np.ascontiguousarray(A(w_branch_ssm)[0]), 'w_out': np.ascontiguousarray(A(w_out)[0]),
               'w_glu': np.ascontiguousarray(A(w_glu)[0])}
    n = 8
    Lp, Ls = x_prompt.shape[1], x_sample.shape[1]
    cfg = {'NT': Lp + 2 * Ls, 'seqs': [(0, Lp), (Lp, Ls), (Lp + Ls, Ls)]}
    xs = [np.ascontiguousarray(np.concatenate([x_prompt[c], x_sample[2 * c], x_sample[2 * c + 1]], axis=0))
          for c in range(n)]
    ys = run_layer(xs, cfg, weights, consts, n)
    y_prompt = np.stack([y[0:Lp] for y in ys]).astype(f)
    y_sample = np.stack([ys[c // 2][Lp + (c % 2) * Ls:Lp + (c % 2 + 1) * Ls] for c in range(2 * n)]).astype(f)
    return (y_prompt, y_sample)


def ssm_mb(k, C, uT, uTk, PS, PK, d, blocks, ident, yps, ypk, wh, whk, on_block, filler=None):
    steps = [(gb, c0, ct) for (gb, c0) in blocks for ct in range(4)]
    n = len(steps)
    pre, post, H, wz = C['pre'], C['post'], C['H'], C['w']
    last = 127 if d == 0 else 0

    def s_bu(gb, c0, ct):
        for comp in range(2):
            k.op('pe', lambda e, comp=comp: e.matmul(PS['bu%d' % comp][:, :], lhsT=uT[:, ct, c0:c0 + 128],
                                                    rhs=C['Bpad'][:, ct, comp, :], start=True, stop=True),
                 reads=[uTk, 'c_Bpad'], writes=[PK['bu%d' % comp]])

    def s_zm(gb, c0, ct):
        pr = pre[:, 0, ct * 512:(ct + 1) * 512]
        pi_ = pre[:, 1, ct * 512:(ct + 1) * 512]
        bur, bui = PS['bu0'][:, :], PS['bu1'][:, :]
        k.tt('dve', wz[0][:], pr, bur, ALU.mult, ['c_pre0', PK['bu0']], ['c_w0'])
        k.tt('dve', wz[1][:], pi_, bui, ALU.mult, ['c_pre1', PK['bu1']], ['c_w1'])
        k.tt('dve', wz[2][:], pr, bui, ALU.mult, ['c_pre0', PK['bu1']], ['c_w2'])
        k.tt('dve', wz[3][:], pi_, bur, ALU.mult, ['c_pre1', PK['bu0']], ['c_w3'])

    def s_zc(gb, c0, ct):
        k.tt('pool', C['zre'][:, ct * 512:(ct + 1) * 512], wz[0][:], wz[1][:], ALU.subtract, ['c_w0', 'c_w1'],
             ['c_zre%d' % ct])
        k.tt('pool', C['zim'][:, ct * 512:(ct + 1) * 512], wz[2][:], wz[3][:], ALU.add, ['c_w2', 'c_w3'],
             ['c_zim%d' % ct])

    def s_cs(gb, c0, ct):
        first = (gb == 0)
        cin = C['cc'][gb % 2]
        cink = 'c_cc%d_%d' % (gb % 2, ct)
        for comp, zz, zk in ((0, C['zre'], 'c_zre%d' % ct), (1, C['zim'], 'c_zim%d' % ct)):
            for q in range(4):
                pair = ct * 4 + q
                k.op('pe', lambda e, zz=zz, q=q, comp=comp: e.matmul(
                    PS['cu%d' % comp][:, q * 128:(q + 1) * 128],
                    lhsT=zz[:, ct * 512 + q * 128:ct * 512 + (q + 1) * 128], rhs=C['trib'][:, :],
                    start=True, stop=first), reads=[zk, 'c_trib'], writes=[PK['cu%d' % comp]], signal=first)
                if not first:
                    k.op('pe', lambda e, q=q, pair=pair, comp=comp: e.matmul(
                        PS['cu%d' % comp][:, q * 128:(q + 1) * 128], lhsT=ident,
                        rhs=cin[:, comp, pair:pair + 1].broadcast_to([128, 128]), start=False, stop=True),
                        reads=[cink, 'ident'], writes=[PK['cu%d' % comp]])

    def s_hm(gb, c0, ct):
        por = post[:, 0, ct * 4:(ct + 1) * 4, :].rearrange("p c x -> p (c x)")
        poi = post[:, 1, ct * 4:(ct + 1) * 4, :].rearrange("p c x -> p (c x)")
        cur, cui = PS['cu0'][:, :], PS['cu1'][:, :]
        k.tt('dve', wh[0], por, cur, ALU.mult, ['c_post0', PK['cu0']], [whk[0]])
        k.tt('dve', wh[1], poi, cui, ALU.mult, ['c_post1', PK['cu1']], [whk[1]])
        k.tt('dve', wh[2], por, cui, ALU.mult, ['c_post0', PK['cu1']], [whk[2]])
        k.tt('dve', wh[3], poi, cur, ALU.mult, ['c_post1', PK['cu0']], [whk[3]])

    def s_hc(gb, c0, ct):
        k.tt('pool', H[:, 0, ct * 4:(ct + 1) * 4, :].rearrange("p c x -> p (c x)"), wh[0], wh[1], ALU.subtract,
             [whk[0], whk[1]], ['c_H0_%d' % ct])
        k.tt('pool', H[:, 1, ct * 4:(ct + 1) * 4, :].rearrange("p c x -> p (c x)"), wh[2], wh[3], ALU.add,
             [whk[2], whk[3]], ['c_H1_%d' % ct])

    def s_carry(gb, c0, ct):
        cout = C['cc'][(gb + 1) % 2]
        coutk = 'c_cc%d_%d' % ((gb + 1) % 2, ct)
        sl = slice(ct * 4, (ct + 1) * 4)
        hr, hi = H[:, 0, sl, last], H[:, 1, sl, last]
        ar, ai = C['a_sp'][:, 0, sl], C['a_sp'][:, 1, sl]
        f = C['ccf']
        hk0, hk1 = 'c_H0_%d' % ct, 'c_H1_%d' % ct
        fk = ['c_ccf%d_%d' % (i, ct) for i in range(4)]
        k.tt('dve', f[:, 0, sl], ar, hr, ALU.mult, ['c_asp0', hk0], [fk[0]])
        k.tt('dve', f[:, 1, sl], ai, hi, ALU.mult, ['c_asp1', hk1], [fk[1]])
        k.tt('dve', f[:, 2, sl], ar, hi, ALU.mult, ['c_asp0', hk1], [fk[2]])
        k.tt('dve', f[:, 3, sl], ai, hr, ALU.mult, ['c_asp1', hk0], [fk[3]])
        k.tt('dve', cout[:, 0, sl], f[:, 0, sl], f[:, 1, sl], ALU.subtract, [fk[0], fk[1]], [coutk])
        k.tt('dve', cout[:, 1, sl], f[:, 2, sl], f[:, 3, sl], ALU.add, [fk[2], fk[3]], [coutk])

    def s_cp(gb, c0, ct):
        nn = 0
        for comp in range(2):
            for q in range(4):
                pair = ct * 4 + q
                k.op('pe', lambda e, comp=comp, pair=pair, nn=nn: e.matmul(
                    yps[:, ct, :], lhsT=C['Cpad'][:, comp, pair, :], rhs=H[:, comp, pair, :],
                    start=(nn == 0), stop=(nn == 7)), reads=['c_Cpad', 'c_H%d_%d' % (comp, ct)], writes=[ypk[ct]],
                    signal=(nn == 7))
                nn += 1
        if ct == 3:
            on_block(gb, c0)

    for i in range(n + 2):
        if i < n:
            s_bu(*steps[i])
            s_zm(*steps[i])
            s_zc(*steps[i])
        if 0 <= i - 1 < n:
            s_cs(*steps[i - 1])
            s_hm(*steps[i - 1])
            s_hc(*steps[i - 1])
        if 0 <= i - 2 < n:
            s_carry(*steps[i - 2])
            s_cp(*steps[i - 2])
        if filler:
            cnt = max(-(-len(filler) // (n + 2 - i)), min(3, len(filler)))
            for _ in range(cnt):
                filler.pop(0)()
```

```python
import numpy as np
from contextlib import ExitStack
import concourse.bass as bass
import concourse.mybir as mybir
from concourse.bass_utils import run_bass_kernel_spmd

F32 = mybir.dt.float32
BF16 = mybir.dt.bfloat16
I32 = mybir.dt.int32
AF = mybir.ActivationFunctionType
ALU = mybir.AluOpType

D = 1024
DFF = 2816
EPS = 1e-6
GC1 = 1.5957691216057308
GC2 = GC1 * 0.044715
NEG = -30000.0
STORE_Q = 'sp'
STRICT = True
import os
P3_TINY = os.environ.get('P3_TINY', 'act')
P3_POLY = os.environ.get('P3_POLY', 'pool')
P3_DBG = os.environ.get('P3_DBG', '')


class KB:
    NDS = 24

    def __init__(self, nc, es):
        self.nc = nc
        self.E = {'pe': nc.tensor, 'act': nc.scalar, 'dve': nc.vector, 'pool': nc.gpsimd, 'sp': nc.sync}
        self.sem = {e: es.enter_context(nc.semaphore('s_' + e)) for e in ['pe', 'act', 'dve', 'pool']}
        self.cnt = {e: 0 for e in self.sem}
        self.dsem = [es.enter_context(nc.semaphore('d%d' % i)) for i in range(self.NDS)]
        self.dcnt = [0] * self.NDS
        self.dnext = 0
        self.seen = {e: {} for e in self.E}
        self.lastw = {}
        self.readers = {}
        self.unsig = {e: False for e in self.sem}
        self.excl = set(['B%d' % i for i in range(8)] + ['B7_%d' % i for i in range(4)] + ['bu0', 'bu1', 'cu0', 'cu1', 'yps']
                        + ['pup%d' % i for i in range(4)] + ['pdn0', 'pdn1', 'ptr0', 'ptr1'])

    def _wait(self, eng, ev):
        key, val = ev
        if self.seen[eng].get(key, 0) >= val:
            return
        sem = self.sem[key] if isinstance(key, str) else self.dsem[key]
        self.E[eng].wait_ge(sem, val)
        self.seen[eng][key] = val

    def _deps(self, eng, reads, writes):
        for b in reads:
            ev = self.lastw.get(b)
            if ev is not None and not (ev[0] == eng and eng == 'pe'):
                self._wait(eng, ev)
            if b in self.excl:
                for rv in self.readers.get(b, ()):
                    if rv[0] != eng:
                        self._wait(eng, rv)
        for b in writes:
            ev = self.lastw.get(b)
            if ev is not None and (ev[0] != eng or (STRICT and eng != 'pe')):
                self._wait(eng, ev)
            for rv in self.readers.get(b, ()):
                if rv[0] != eng or (STRICT and eng != 'pe'):
                    self._wait(eng, rv)

    def _reg(self, ev, reads, writes):
        for b in reads:
            self.readers.setdefault(b, []).append(ev)
        for b in writes:
            self.lastw[b] = ev
            self.readers[b] = []

    def op(self, eng, fn, reads=(), writes=(), signal=True):
        self._deps(eng, reads, writes)
        ins = fn(self.E[eng])
        if signal:
            self.cnt[eng] += 1
            ins.then_inc(self.sem[eng], 1)
            ev = (eng, self.cnt[eng])
            self.unsig[eng] = False
        else:
            ev = (eng, self.cnt[eng] + 1)
            self.unsig[eng] = True
        self._reg(ev, reads, writes)
        return ev

    def dma(self, q, pairs, reads=(), writes=()):
        if q == 'pool' and STORE_Q != 'pool':
            q = STORE_Q
        i = self.dnext
        self.dnext = (i + 1) % self.NDS
        if self.dcnt[i] > 0:
            self._wait(q, (i, self.dcnt[i]))
        self._deps(q, reads, writes)
        for (o, s) in pairs:
            self.E[q].dma_start(out=o, in_=s).then_inc(self.dsem[i], 16)
        self.dcnt[i] += 16 * len(pairs)
        ev = (i, self.dcnt[i])
        self._reg(ev, reads, writes)
        return ev

    def barrier(self):
        evs = [(e, self.cnt[e]) for e in self.sem if self.cnt[e] > 0]
        evs += [(i, self.dcnt[i]) for i in range(self.NDS) if self.dcnt[i] > 0]
        for eng in self.E:
            for ev in evs:
                if ev[0] != eng:
                    self._wait(eng, ev)

    def tt(self, eng, out, in0, in1, op, reads, writes):
        return self.op(eng, lambda e: e.tensor_tensor(out=out, in0=in0, in1=in1, op=op), reads, writes)

    def ts(self, eng, out, in0, s1, s2, op0, op1, reads, writes):
        if s2 is None:
            return self.op(eng, lambda e: e.tensor_scalar(out=out, in0=in0, scalar1=s1, scalar2=None, op0=op0),
                           reads, writes)
        return self.op(eng, lambda e: e.tensor_scalar(out=out, in0=in0, scalar1=s1, scalar2=s2, op0=op0, op1=op1),
                       reads, writes)

    def stt(self, eng, out, in0, scalar, in1, op0, op1, reads, writes):
        return self.op(eng, lambda e: e.scalar_tensor_tensor(out=out, in0=in0, scalar=scalar, in1=in1, op0=op0,
                                                             op1=op1), reads, writes)

    def act(self, out, in_, func, reads, writes, scale=1.0, bias=None, accum=None):
        kw = {}
        if bias is not None:
            kw['bias'] = bias
        if accum is not None:
            kw['accum_out'] = accum
        return self.op('act', lambda e: e.activation(out=out, in_=in_, func=func, scale=scale, **kw), reads, writes)

    def cp(self, eng, out, in_, reads, writes):
        if eng == 'act':
            return self.op('act', lambda e: e.activation(out=out, in_=in_, func=AF.Copy), reads, writes)
        return self.op(eng, lambda e: e.tensor_copy(out=out, in_=in_), reads, writes)

    def finish(self, q='sp'):
        for i in range(self.NDS):
            if self.dcnt[i] > 0:
                self._wait(q, (i, self.dcnt[i]))


def rsqrt_dve(k, out, outk, v, vk, tmp, tmpk, n):
    t0, t1, t2 = tmp
    k0, k1, k2 = tmpk
    k.op('dve', lambda e: e.tensor_single_scalar(out=t0.bitcast(I32), in_=v.bitcast(I32), scalar=1,
                                                 op=ALU.arith_shift_right), reads=[vk], writes=[k0])
    k.op('dve', lambda e: e.tensor_scalar(out=out.bitcast(I32), in0=t0.bitcast(I32), scalar1=-1,
                                          scalar2=0x5f3759df, op0=ALU.mult, op1=ALU.add),
         reads=[k0], writes=[outk])
    for _ in range(3):
        k.op('dve', lambda e: e.tensor_tensor(out=t1, in0=out, in1=out, op=ALU.mult), reads=[outk], writes=[k1])
        k.op('dve', lambda e: e.tensor_tensor(out=t2, in0=t1, in1=v, op=ALU.mult), reads=[k1, vk], writes=[k2])
        k.op('dve', lambda e: e.tensor_scalar(out=t1, in0=t2, scalar1=-0.5, scalar2=1.5, op0=ALU.mult,
                                              op1=ALU.add), reads=[k2], writes=[k1])
        k.op('dve', lambda e: e.tensor_tensor(out=out, in0=out, in1=t1, op=ALU.mult), reads=[outk, k1],
             writes=[outk])


def norm_transpose(k, pfx, xt, xtk, hn, hnk, hT, hTk, col0, ptr, ptrk, ident, st, stk, eps_eff, evac_eng):
    junk = st['junk']
    k.op('act', lambda e: e.activation(out=junk, in_=xt, func=AF.Square, accum_out=st['ss']),
         reads=[xtk], writes=[stk + 'junk', stk + 'ss'])
    k.op('dve', lambda e: e.tensor_scalar(out=st['v'], in0=st['ss'], scalar1=1.0 / D, scalar2=eps_eff,
                                          op0=ALU.mult, op1=ALU.add), reads=[stk + 'ss'], writes=[stk + 'v'])
    rsqrt_dve(k, st['r'], stk + 'r', st['v'], stk + 'v', (st['t0'], st['t1'], st['t2']),
              (stk + 't0', stk + 't1', stk + 't2'), 1)
    k.op('dve', lambda e: e.tensor_scalar(out=hn, in0=xt, scalar1=st['r'], scalar2=None, op0=ALU.mult),
         reads=[xtk, stk + 'r'], writes=[hnk])
    for kt in range(8):
        k.op('pe', lambda e, kt=kt: e.transpose(ptr[:, kt, :], hn[:, kt * 128:(kt + 1) * 128], ident),
             reads=[hnk], writes=[ptrk], signal=(kt == 7))
    if evac_eng == 'act':
        k.op('act', lambda e: e.activation(out=hT[:, :, col0:col0 + 128], in_=ptr[:, :, :], func=AF.Copy),
             reads=[ptrk], writes=[hTk])
    else:
        k.op(evac_eng, lambda e: e.tensor_copy(out=hT[:, :, col0:col0 + 128], in_=ptr[:, :, :]),
             reads=[ptrk], writes=[hTk])


def load_convert_weight(k, Wdst, Wk, wsrc, nkt, ncols, stage_bufs, stage_keys, scale_col=None,
                        const_scale=None, chunk=512):
    ci = 0
    cwid = min(512, ncols)
    kg = max(1, 1024 // cwid)
    for c0 in range(0, ncols, cwid):
        cw = min(cwid, ncols - c0)
        for k0 in range(0, nkt, kg):
            nk = min(kg, nkt - k0)
            sbuf = stage_bufs[ci % len(stage_bufs)]
            skey = KL(stage_keys[ci % len(stage_bufs)])
            view = sbuf[:, 0:nk * cw].rearrange("p (a b) -> p a b", a=nk)
            k.dma('sp', [(view, wsrc[:, k0:k0 + nk, c0:c0 + cw])], writes=skey)
            o = Wdst[:, k0:k0 + nk, c0:c0 + cw]
            if scale_col is not None:
                bc = scale_col[:, k0:k0 + nk].unsqueeze(2).broadcast_to([128, nk, cw])
                k.op('dve', lambda e, o=o, view=view, bc=bc: e.tensor_tensor(out=o, in0=view, in1=bc, op=ALU.mult),
                     reads=skey, writes=[Wk])
            else:
                cs = 1.0 if const_scale is None else const_scale
                if ci % 2 == 1:
                    k.op('act', lambda e, o=o, view=view: e.activation(out=o, in_=view, func=AF.Copy, scale=cs),
                         reads=skey, writes=[Wk])
                else:
                    k.op('dve', lambda e, o=o, view=view: e.tensor_scalar(out=o, in0=view, scalar1=cs, scalar2=None,
                                                                       op0=ALU.mult), reads=skey, writes=[Wk])
            ci += 1


def KL(x):
    return [x] if isinstance(x, str) else list(x)


def pass3(k, cfg, T):
    nc = k.nc
    with ExitStack() as es:
        def sb(n, s, d=F32):
            return es.enter_context(nc.sbuf_tensor("p3_" + n, s, d))

        def pp(n, s, d=F32):
            return es.enter_context(nc.psum_tensor("p3_" + n, s, d))
        Wup = sb("Wup", [128, 8, 2 * DFF], BF16)
        Wd = sb("Wd", [128, 22, D], BF16)
        tokf = [sb("tok%d" % i, [128, 1028]) for i in range(4)]
        tok = [t[:, 0:D] for t in tokf]
        tokk = [('tok%d' % i, 'tok%db' % i) for i in range(4)]
        hn = [sb("hn%d" % i, [128, D], BF16) for i in range(1)]
        h2T = sb("h2T", [128, 8, 512], BF16)
        gT = sb("gT", [128, 22, 512], BF16)
        gbc = sb("gbc", [128, D])
        g2col = sb("g2col", [128, 8])
        cw = sb("cw", [128, 44, 3])
        cb = sb("cb", [128, 44])
        carry = sb("carry", [128, 22, 2, 2])
        ident = sb("ident", [128, 128], BF16)
        identf = sb("identf", [128, 128])
        extA = sb("extA", [128, 2, 514])
        caccA = sb("caccA", [128, 2, 512])
        tmpA = sb("tmpA", [128, 2, 512])
        stt = sb("stt", [128, 40])
        junk = sb("junk", [128, D], BF16)
        ptr = [pp("ptr0", [128, 8, 128], BF16)]
        pup = [pp("pup%d" % i, [128, 512]) for i in range(4)]
        pdn = [pp("pdn%d" % i, [128, 512]) for i in range(2)]
        EXT = [extA, tokf[2][:, 0:1028].rearrange("p (h x) -> p h x", h=2)]
        EXTK = [('extA_a', 'extA_b'), tokk[2]]
        CAC = [caccA, tokf[3][:, 0:1024].rearrange("p (h x) -> p h x", h=2)]
        CACK = [('cacA_a', 'cacA_b'), tokk[3]]
        TMP = [tmpA, tokf[1][:, 0:1024].rearrange("p (h x) -> p h x", h=2)]
        TMPK = [('tmpA_a', 'tmpA_b'), tokk[1]]

        k.dma('sp', [(gbc[:], T['g_ffn_post_bc'][:, :]), (g2col[:], T['g_ffn_pre_col'][:, :]),
                     (cw[:], T['conv_w_l'][:, :, :]), (cb[:], T['conv_b_l'][:, :]),
                     (identf[:], T['identf'][:, :])], writes=['gbc', 'g2col', 'cw', 'cb', 'identf'])
        k.cp('dve', ident[:], identf[:], ['identf'], ['ident'])
        wup_v = T['w_up'].rearrange("(kt p) c -> p kt c", p=128)
        load_convert_weight(k, Wup, 'Wup', wup_v, 8, 2 * DFF, tok, tokk, scale_col=g2col, chunk=128)
        wd_v = T['w_down'].rearrange("(kt p) c -> p kt c", p=128)
        load_convert_weight(k, Wd, 'Wd', wd_v, 22, D, tok, tokk, chunk=32)

        tokc = 0
        x1s = T['x1s']
        y = T['y']
        starts3 = [off_ + m_ * 512 for (off_, L_) in cfg['seqs'] for m_ in range(L_ // 512)]
        pre3 = set()
        XB = [caccA[:].rearrange("p h x -> p (h x)"), extA[:].rearrange("p h x -> p (h x)")[:, 0:D]]
        XBK = [('cacA_a', 'cacA_b'), ('extA_a', 'extA_b')]
        TB = tmpA[:].rearrange("p h x -> p (h x)")
        TBK = ('tmpA_a', 'tmpA_b')
        for (off, L) in cfg['seqs']:
            nmb = L // 512
            k.op('dve', lambda e: e.memset(carry[:], 0.0), writes=['carry'])
            for mb in range(nmb + 1):
                tail = (mb == nmb)
                wdt = 4 if tail else 512
                t0 = off + mb * 512
                if not tail:
                    prenorm4(k, x1s, t0, tok, tokk, hn, ['p3hn0'], h2T, 'h2T', ptr, ['ptr0'], ident[:], stt, 'p3s_',
                             junk, EPS, skip_load=(0, 1, 2, 3) if t0 in pre3 else ())

                def st_U(j):
                    if tail:
                        return
                    for half in range(2):
                        tl = j + 22 * half
                        pb = pup[(2 * j + half) % 4]
                        pk = 'pup%d' % ((2 * j + half) % 4)
                        for kt in range(8):
                            k.op('pe', lambda e, kt=kt, tl=tl, pb=pb: e.matmul(
                                pb[:, :], lhsT=Wup[:, kt, tl * 128:(tl + 1) * 128], rhs=h2T[:, kt, :],
                                start=(kt == 0), stop=(kt == 7)), reads=['Wup', 'h2T'], writes=[pk],
                                signal=(kt == 7))

                def st_E(j):
                    par = j % 2
                    ex, exk = EXT[par], EXTK[par]
                    ca, cak = CAC[par], CACK[par]
                    for hf in range(2):
                        k.cp(P3_TINY, ex[:, hf, 0:2], carry[:, j, hf, :], ['carry'], [exk[hf]])
                    for half in range(2):
                        tl = j + 22 * half
                        pb = pup[(2 * j + half) % 4]
                        pk = 'pup%d' % ((2 * j + half) % 4)
                        if not tail:
                            k.act(ex[:, half, 2:514], pb[:, :], AF.Copy, [pk], [exk[half]])
                            k.act(ca[:, half, :], pb[:, :], AF.Copy, [pk, 'cw'], [cak[half]], scale=cw[:, tl, 2:3])
                        else:
                            k.op('pool', lambda e, ex=ex, half=half: e.memset(ex[:, half, 2:2 + wdt], 0.0),
                                 writes=[exk[half]])
                            k.act(ca[:, half, 0:wdt], ex[:, half, 2:2 + wdt], AF.Copy, [exk[half], 'cw'], [cak[half]],
                                  scale=cw[:, tl, 2:3])
                    if not tail:
                        for hf in range(2):
                            k.cp(P3_TINY, carry[:, j, hf, :], ex[:, hf, 512:514], [exk[hf]], ['carry'])

                def st_C(j):
                    par = j % 2
                    ex, exk = EXT[par], EXTK[par]
                    ca, cak = CAC[par], CACK[par]
                    for (c0_, wi) in ((0, 0), (1, 1)):
                        for half in range(2):
                            tl = j + 22 * half
                            k.stt('dve', ca[:, half, 0:wdt], ex[:, half, c0_:c0_ + wdt], cw[:, tl, wi:wi + 1],
                                  ca[:, half, 0:wdt], ALU.mult, ALU.add, [exk[half], cak[half], 'cw'], [cak[half]])

                def st_B(j):
                    par = j % 2
                    ca, cak = CAC[par], CACK[par]
                    k.act(ca[:, 0, 0:wdt], ca[:, 0, 0:wdt], AF.Identity, [cak[0], 'cb'], [cak[0]], bias=cb[:, j:j + 1])

                def st_Sq(j):
                    par = j % 2
                    k.act(TMP[par][:, 0, 0:wdt], CAC[par][:, 0, 0:wdt], AF.Square, [CACK[par][0]], [TMPK[par][0]])

                def st_P(j):
                    par = j % 2
                    ta, tak = TMP[par][:, 0, 0:wdt], TMPK[par][0]
                    k.ts(P3_POLY, ta, ta, GC2, GC1, ALU.mult, ALU.add, [tak], [tak])
                    k.tt(P3_POLY, ta, ta, CAC[par][:, 0, 0:wdt], ALU.mult, [tak, CACK[par][0]], [tak])

                def st_T(j):
                    par = j % 2
                    k.act(TMP[par][:, 1, 0:wdt], TMP[par][:, 0, 0:wdt], AF.Sigmoid, [TMPK[par][0]], [TMPK[par][1]])

                def st_G(j):
                    par = j % 2
                    tb2, tbk2 = TMP[par][:, 1, 0:wdt], TMPK[par][1]
                    k.tt(P3_POLY, tb2, tb2, CAC[par][:, 0, 0:wdt], ALU.mult, [tbk2, CACK[par][0]], [tbk2])
                    k.stt('dve', gT[:, j, 0:wdt], CAC[par][:, 1, 0:wdt], cb[:, 22 + j:23 + j], tb2, ALU.add, ALU.mult,
                          [tbk2, CACK[par][1], 'cb'], ['gT'])

                if 'nogelu' in P3_DBG:
                    def st_Sq(j): pass
                    def st_B(j): pass
                    def st_P(j): pass
                    def st_T(j): pass
                    def st_G(j):
                        k.cp('dve', gT[:, j, :], CAC[j % 2][:, 0, :], [CACK[j % 2][0]], ['gT'])
                if 'noconv' in P3_DBG:
                    def st_C(j): pass
                if 'noE' in P3_DBG:
                    def st_E(j): pass
                if 'noU' in P3_DBG:
                    def st_U(j): pass
                for j in range(23):
                    if j >= 1:
                        st_B(j - 1)
                        st_Sq(j - 1)
                    if j < 22:
                        st_U(j)
                        st_E(j)
                    if j >= 1:
                        st_P(j - 1)
                    if j < 22:
                        st_C(j)
                    if j >= 1:
                        st_T(j - 1)
                        st_G(j - 1)

                if not tail and mb + 1 < nmb:
                    nxt = starts3.index(t0) + 1
                    if nxt < len(starts3):
                        tn = starts3[nxt]
                        for i_ in range(4):
                            k.dma('sp', [(tok[i_][:], x1s[tn + i_ * 128:tn + (i_ + 1) * 128, :])], reads=['x1s_dram'],
                                  writes=KL(tokk[i_]))
                        pre3.add(tn)
                ntile = 1 if tail else 4
                if 'nodown' in P3_DBG:
                    ntile = 0
                for i in range(ntile):
                    tk0 = t0 - 1 + i * 128
                    p_lo = 1 if tk0 < off else 0
                    p_hi = 1 if tail else 128
                    xb, xk = XB[tokc % 2], XBK[tokc % 2]
                    tokc += 1
                    tb_, tbk = TB, TBK
                    if i % 2 == 0 or 'pdn' in P3_DBG:
                        pd, pdk = pdn, ['pdn0', 'pdn1']
                    else:
                        pd, pdk = pup[0:2], ['pup0', 'pup1']
                    if 'noload' not in P3_DBG:
                        k.dma('sp', [(xb[p_lo:p_hi, :], x1s[tk0 + p_lo:tk0 + p_hi, :])], reads=['x1s_dram'],
                              writes=list(xk))
                    for h in range(2 if 'nomm' not in P3_DBG else 0):
                        for kt in range(22):
                            k.op('pe', lambda e, kt=kt, h=h, i=i, pd=pd: e.matmul(
                                pd[h][:, :], lhsT=gT[:, kt, i * 128:(i + 1) * 128],
                                rhs=Wd[:, kt, h * 512:(h + 1) * 512], start=(kt == 0), stop=(kt == 21)),
                                reads=['gT', 'Wd'], writes=[pdk[h]], signal=(kt == 21))
                    if 'nonorm' in P3_DBG:
                        continue
                    k.act(junk[:, 0:512], pd[0][:, :], AF.Square, [pdk[0]], ['p3s_junk', 'p3s_a0'], accum=stt[:, 24:25])
                    k.act(junk[:, 512:1024], pd[1][:, :], AF.Square, [pdk[1]], ['p3s_junk', 'p3s_a1'],
                          accum=stt[:, 25:26])
                    for h in range(2):
                        k.tt('dve', tb_[:, h * 512:(h + 1) * 512], pd[h][:, :], gbc[:, h * 512:(h + 1) * 512],
                             ALU.mult, [pdk[h], 'gbc'], list(tbk))
                    k.tt('dve', stt[:, 26:27], stt[:, 24:25], stt[:, 25:26], ALU.add, ['p3s_a0', 'p3s_a1'], ['p3s_a2'])
                    k.ts('dve', stt[:, 27:28], stt[:, 26:27], 1.0 / D, EPS, ALU.mult, ALU.add, ['p3s_a2'],
                         ['p3s_a3'])
                    rsqrt_dve(k, stt[:, 28:29], 'p3s_r2', stt[:, 27:28], 'p3s_a3',
                              (stt[:, 29:30], stt[:, 30:31], stt[:, 31:32]), ('p3s_u0', 'p3s_u1', 'p3s_u2'), 1)
                    k.stt('dve', xb[:, :], tb_[:, :], stt[:, 28:29], xb[:, :], ALU.mult, ALU.add,
                          list(tbk) + ['p3s_r2'] + list(xk), list(xk))
                    if 'nostore' not in P3_DBG:
                        k.dma('pool', [(y[tk0 + p_lo:tk0 + p_hi, :], xb[p_lo:p_hi, :])], reads=list(xk),
                              writes=['y_dram'])


TWO_PI = 2.0 * np.pi


def _table_from(k, pfx, expo, ek, ang, ak, re, rek, im, imk, tmp, tmpk, ti, tik):
    E, f, sc = tmp
    Ek, fk, sck = tmpk
    k.act(E, expo, AF.Exp, [ek], [Ek])
    k.cp('dve', ti, ang, [ak], [tik])
    k.cp('dve', f, ti, [tik], [fk])
    k.tt('dve', f, ang, f, ALU.subtract, [ak, fk], [fk])
    k.act(sc, f, AF.Sin, [fk], [sck], scale=TWO_PI)
    k.tt('dve', im, E, sc, ALU.mult, [Ek, sck], [imk])
    k.ts('dve', f, ang, 0.25, None, ALU.add, None, [ak], [fk])
    k.cp('dve', ti, f, [fk], [tik])
    k.cp('dve', sc, ti, [tik], [sck])
    k.tt('dve', f, f, sc, ALU.subtract, [fk, sck], [fk])
    k.act(sc, f, AF.Sin, [fk], [sck], scale=TWO_PI)
    k.tt('dve', re, E, sc, ALU.mult, [Ek, sck], [rek])


def ssm_prep(k, T, d, C):
    nc = k.nc
    k.barrier()
    with ExitStack() as es:
        def sb(n, s, dt=F32):
            return es.enter_context(nc.sbuf_tensor("sp%d_" % d + n, s, dt))
        N = 2048
        lre = sb("lre", [128, N]); lim = sb("lim", [128, N]); ldt = sb("ldt", [128, N])
        bre = sb("bre", [128, N]); bim = sb("bim", [128, N])
        xr = sb("xr", [128, N]); xi = sb("xi", [128, N])
        are = sb("are", [128, N]); aim = sb("aim", [128, N])
        t0 = sb("t0", [128, N]); t1 = sb("t1", [128, N]); t2 = sb("t2", [128, N]); t3 = ldt
        ti = sb("ti", [128, N], I32)
        sm = sb("sm", [128, 8, 16])
        smi = sb("smi", [128, 16], I32)
        k.dma('sp', [(lre[:], T['lam_re_bp'][d]), (lim[:], T['lam_im_bp'][d]), (ldt[:], T['log_dt_bp'][d]),
                     (bre[:], T['b_reT_pad'][d]), (bim[:], T['b_imT_pad'][d])],
              writes=['p_lre', 'p_lim', 'p_ldt', 'p_bre', 'p_bim'])
        k.dma('sp', [(sm[:, 0, :], T['lam_re_sp'][d]), (sm[:, 1, :], T['lam_im_sp'][d]),
                     (sm[:, 2, :], T['log_dt_sp'][d]), (C['sig'][:], T['nsig'][d]), (C['trow'][:], T['trow'][d]),
                     (C['tri'][:], T['tri'][d])],
              writes=['p_sm', 'c_sig', 'c_trow', 'c_tri'])
        k.act(t0[:], ldt[:], AF.Exp, ['p_ldt'], ['p_t0'])
        k.tt('dve', xr[:], lre[:], t0[:], ALU.mult, ['p_lre', 'p_t0'], ['p_xr'])
        k.tt('dve', xi[:], lim[:], t0[:], ALU.mult, ['p_lim', 'p_t0'], ['p_xi'])
        k.ts('dve', xi[:], xi[:], 1.0 / TWO_PI, None, ALU.mult, None, ['p_xi'], ['p_xi'])
        _table_from(k, 'a', xr[:], 'p_xr', xi[:], 'p_xi', are[:], 'p_are', aim[:], 'p_aim',
                    (t0[:], t1[:], t2[:]), ('p_t0', 'p_t1', 'p_t2'), ti[:], 'p_ti')
        k.ts('dve', t0[:], are[:], -1.0, None, ALU.add, None, ['p_are'], ['p_t0'])
        k.tt('dve', t1[:], lre[:], lre[:], ALU.mult, ['p_lre'], ['p_t1'])
        k.tt('dve', t2[:], lim[:], lim[:], ALU.mult, ['p_lim'], ['p_t2'])
        k.tt('dve', t1[:], t1[:], t2[:], ALU.add, ['p_t1', 'p_t2'], ['p_t1'])
        k.op('dve', lambda e: e.reciprocal(out=t1[:], in_=t1[:]), ['p_t1'], ['p_t1'])
        k.tt('dve', t2[:], t0[:], lre[:], ALU.mult, ['p_t0', 'p_lre'], ['p_t2'])
        k.tt('dve', t3[:], aim[:], lim[:], ALU.mult, ['p_aim', 'p_lim'], ['p_ldt'])
        k.tt('dve', t2[:], t2[:], t3[:], ALU.add, ['p_t2', 'p_ldt'], ['p_t2'])
        k.tt('dve', t2[:], t2[:], t1[:], ALU.mult, ['p_t2', 'p_t1'], ['p_t2'])
        k.tt('dve', t3[:], aim[:], lre[:], ALU.mult, ['p_aim', 'p_lre'], ['p_ldt'])
        k.tt('dve', t0[:], t0[:], lim[:], ALU.mult, ['p_t0', 'p_lim'], ['p_t0'])
        k.tt('dve', t3[:], t3[:], t0[:], ALU.subtract, ['p_ldt', 'p_t0'], ['p_ldt'])
        k.tt('dve', t3[:], t3[:], t1[:], ALU.mult, ['p_ldt', 'p_t1'], ['p_ldt'])
        Bp = C['Bpad']
        v4 = lambda a: a.rearrange("p (c x) -> p c x", c=4)
        k.tt('dve', t0[:], t2[:], bre[:], ALU.mult, ['p_t2', 'p_bre'], ['p_t0'])
        k.tt('dve', t1[:], t3[:], bim[:], ALU.mult, ['p_ldt', 'p_bim'], ['p_t1'])
        k.tt('dve', Bp[:, :, 0, :], v4(t0[:]), v4(t1[:]), ALU.subtract, ['p_t0', 'p_t1'], ['c_Bpad'])
        k.tt('dve', t0[:], t2[:], bim[:], ALU.mult, ['p_t2', 'p_bim'], ['p_t0'])
        k.tt('dve', t1[:], t3[:], bre[:], ALU.mult, ['p_ldt', 'p_bre'], ['p_t1'])
        k.tt('dve', Bp[:, :, 1, :], v4(t0[:]), v4(t1[:]), ALU.add, ['p_t0', 'p_t1'], ['c_Bpad'])
        k.ts('dve', t3[:], xr[:], C['sig'][:, 0:1], None, ALU.mult, None, ['p_xr', 'c_sig'], ['p_ldt'])
        k.ts('dve', bre[:], xi[:], C['sig'][:, 0:1], None, ALU.mult, None, ['p_xi', 'c_sig'], ['p_bre'])
        _table_from(k, 'pre', t3[:], 'p_ldt', bre[:], 'p_bre', C['pre'][:, 0, :], 'c_pre0', C['pre'][:, 1, :],
                    'c_pre1', (t0[:], t1[:], t2[:]), ('p_t0', 'p_t1', 'p_t2'), ti[:], 'p_ti')
        k.act(sm[:, 3, :], sm[:, 2, :], AF.Exp, ['p_sm'], ['p_sm3'])
        k.tt('dve', sm[:, 4, :], sm[:, 0, :], sm[:, 3, :], ALU.mult, ['p_sm', 'p_sm3'], ['p_sm4'])
        k.tt('dve', sm[:, 5, :], sm[:, 1, :], sm[:, 3, :], ALU.mult, ['p_sm', 'p_sm3'], ['p_sm5'])
        k.ts('dve', sm[:, 5, :], sm[:, 5, :], 1.0 / TWO_PI, None, ALU.mult, None, ['p_sm5'], ['p_sm5'])
        _table_from(k, 'as', sm[:, 4, :], 'p_sm4', sm[:, 5, :], 'p_sm5', C['a_sp'][:, 0, :], 'c_asp0',
                    C['a_sp'][:, 1, :], 'c_asp1', (sm[:, 6, :], sm[:, 7, :], sm[:, 3, :]),
                    ('p_sm6', 'p_sm7', 'p_sm3'), smi[:], 'p_smi')
        tr_b = C['trow'][:, :].unsqueeze(1).broadcast_to([128, 16, 128])
        v16 = lambda a: a.rearrange("p (c x) -> p c x", c=16)
        k.tt('dve', v16(t3[:]), tr_b, sm[:, 4, :].unsqueeze(2).broadcast_to([128, 16, 128]), ALU.mult,
             ['c_trow', 'p_sm4'], ['p_ldt'])
        k.tt('dve', v16(bre[:]), tr_b, sm[:, 5, :].unsqueeze(2).broadcast_to([128, 16, 128]), ALU.mult,
             ['c_trow', 'p_sm5'], ['p_bre'])
        _table_from(k, 'post', t3[:], 'p_ldt', bre[:], 'p_bre',
                    C['post'][:, 0].rearrange("p c x -> p (c x)"), 'c_post0',
                    C['post'][:, 1].rearrange("p c x -> p (c x)"), 'c_post1',
                    (t0[:], t1[:], t2[:]), ('p_t0', 'p_t1', 'p_t2'), ti[:], 'p_ti')
        c16 = lambda a: a.rearrange("p (c x) -> p c x", c=16)
        k.dma('sp', [(c16(t0[:]), T['c_reT_pad'][d]), (c16(t1[:]), T['c_imT_pad'][d])], writes=['p_t0', 'p_t1'])
        k.cp('dve', C['Cpad'][:, 0], c16(t0[:]), ['p_t0'], ['c_Cpad'])
        k.ts('dve', C['Cpad'][:, 1], c16(t1[:]), -1.0, None, ALU.mult, None, ['p_t1'], ['c_Cpad'])
        k.cp('dve', C['trib'][:], C['tri'][:], ['c_tri'], ['c_trib'])
        k.barrier()


def ssm_alloc(nc, es, pfx):
    def sb(n, s, dt=F32):
        return es.enter_context(nc.sbuf_tensor(pfx + n, s, dt))
    C = {}
    C['Bpad'] = sb("Bpad", [128, 4, 2, 512], BF16)
    C['pre'] = sb("pre", [128, 2, 2048])
    C['post'] = sb("post", [128, 2, 16, 128])
    C['Cpad'] = sb("Cpad", [128, 2, 16, 128], BF16)
    C['a_sp'] = sb("a_sp", [128, 2, 16])
    C['sig'] = sb("sig", [128, 1])
    C['trow'] = sb("trow", [128, 128])
    C['tri'] = sb("tri", [128, 128])
    C['trib'] = sb("trib", [128, 128], BF16)
    C['zre'] = sb("zre", [128, 2048], BF16)
    C['zim'] = sb("zim", [128, 2048], BF16)
    C['H'] = sb("H", [128, 2, 16, 128], BF16)
    C['cc'] = [sb("cc%d" % i, [128, 2, 16], BF16) for i in range(2)]
    C['ccf'] = sb("ccf", [128, 4, 16])
    C['w'] = [sb("w%d" % i, [128, 512]) for i in range(4)]
    return C


def ssm_block(k, C, uT, uTk, c0, PS, d, blk, first, ident, yps, ypsk, PK=None):
    if PK is None:
        PK = {n: n for n in ('bu0', 'bu1', 'cu0', 'cu1')}
    cin = C['cc'][blk % 2]
    cink = 'c_cc%d' % (blk % 2)
    cout = C['cc'][(blk + 1) % 2]
    coutk = 'c_cc%d' % ((blk + 1) % 2)
    pre = C['pre']
    post = C['post']
    H = C['H']
    w = C['w']
    for ct in range(4):
        for comp in range(2):
            k.op('pe', lambda e, ct=ct, comp=comp: e.matmul(PS['bu%d' % comp][:, :], lhsT=uT[:, ct, c0:c0 + 128],
                                                          rhs=C['Bpad'][:, ct, comp, :], start=True, stop=True),
                 reads=[uTk, 'c_Bpad'], writes=[PK['bu%d' % comp]])
        pr = pre[:, 0, ct * 512:(ct + 1) * 512]
        pi_ = pre[:, 1, ct * 512:(ct + 1) * 512]
        bur = PS['bu0'][:, :]
        bui = PS['bu1'][:, :]
        k.tt('dve', w[0][:], pr, bur, ALU.mult, ['c_pre0', PK['bu0']], ['c_w0'])
        k.tt('dve', w[1][:], pi_, bui, ALU.mult, ['c_pre1', PK['bu1']], ['c_w1'])
        k.tt('dve', w[2][:], pr, bui, ALU.mult, ['c_pre0', PK['bu1']], ['c_w2'])
        k.tt('dve', w[3][:], pi_, bur, ALU.mult, ['c_pre1', PK['bu0']], ['c_w3'])
        k.tt('pool', C['zre'][:, ct * 512:(ct + 1) * 512], w[0][:], w[1][:], ALU.subtract, ['c_w0', 'c_w1'],
             ['c_zre%d' % ct])
        k.tt('pool', C['zim'][:, ct * 512:(ct + 1) * 512], w[2][:], w[3][:], ALU.add, ['c_w2', 'c_w3'],
             ['c_zim%d' % ct])
        for comp, zz, zk in ((0, C['zre'], 'c_zre%d' % ct), (1, C['zim'], 'c_zim%d' % ct)):
            for q in range(4):
                pair = ct * 4 + q
                k.op('pe', lambda e, zz=zz, q=q, ct=ct, comp=comp: e.matmul(
                    PS['cu%d' % comp][:, q * 128:(q + 1) * 128],
                    lhsT=zz[:, ct * 512 + q * 128:ct * 512 + (q + 1) * 128], rhs=C['trib'][:, :],
                    start=True, stop=first), reads=[zk, 'c_trib'], writes=[PK['cu%d' % comp]], signal=first)
                if not first:
                    k.op('pe', lambda e, q=q, pair=pair, comp=comp: e.matmul(
                        PS['cu%d' % comp][:, q * 128:(q + 1) * 128], lhsT=ident,
                        rhs=cin[:, comp, pair:pair + 1].broadcast_to([128, 128]), start=False, stop=True),
                        reads=[cink, 'ident'], writes=[PK['cu%d' % comp]])
        por = post[:, 0, ct * 4:(ct + 1) * 4, :].rearrange("p c x -> p (c x)")
        poi = post[:, 1, ct * 4:(ct + 1) * 4, :].rearrange("p c x -> p (c x)")
        cur = PS['cu0'][:, :]
        cui = PS['cu1'][:, :]
        k.tt('dve', w[0][:], por, cur, ALU.mult, ['c_post0', PK['cu0']], ['c_w0'])
        k.tt('dve', w[1][:], poi, cui, ALU.mult, ['c_post1', PK['cu1']], ['c_w1'])
        k.tt('dve', w[2][:], por, cui, ALU.mult, ['c_post0', PK['cu1']], ['c_w2'])
        k.tt('dve', w[3][:], poi, cur, ALU.mult, ['c_post1', PK['cu0']], ['c_w3'])
        k.tt('pool', H[:, 0, ct * 4:(ct + 1) * 4, :].rearrange("p c x -> p (c x)"), w[0][:], w[1][:], ALU.subtract,
             ['c_w0', 'c_w1'], ['c_H0_%d' % ct])
        k.tt('pool', H[:, 1, ct * 4:(ct + 1) * 4, :].rearrange("p c x -> p (c x)"), w[2][:], w[3][:], ALU.add,
             ['c_w2', 'c_w3'], ['c_H1_%d' % ct])
        n = 0
        for comp in range(2):
            for q in range(4):
                pair = ct * 4 + q
                k.op('pe', lambda e, comp=comp, pair=pair, ct=ct, n=n: e.matmul(
                    yps[:, ct, :], lhsT=C['Cpad'][:, comp, pair, :], rhs=H[:, comp, pair, :],
                    start=(n == 0), stop=(n == 7)), reads=['c_Cpad', 'c_H%d_%d' % (comp, ct)], writes=[ypsk],
                    signal=(n == 7))
                n += 1
    last = 127 if d == 0 else 0
    hk = ['c_H%d_%d' % (comp, ct) for comp in range(2) for ct in range(4)]
    hr = H[:, 0, :, last]
    hi = H[:, 1, :, last]
    ar = C['a_sp'][:, 0, :]
    ai = C['a_sp'][:, 1, :]
    f = C['ccf']
    k.tt('pool', f[:, 0, :], ar, hr, ALU.mult, ['c_asp0'] + hk, ['c_ccf0'])
    k.tt('pool', f[:, 1, :], ai, hi, ALU.mult, ['c_asp1'] + hk, ['c_ccf1'])
    k.tt('pool', f[:, 2, :], ar, hi, ALU.mult, ['c_asp0'] + hk, ['c_ccf2'])
    k.tt('pool', f[:, 3, :], ai, hr, ALU.mult, ['c_asp1'] + hk, ['c_ccf3'])
    k.tt('pool', cout[:, 0, :], f[:, 0, :], f[:, 1, :], ALU.subtract, ['c_ccf0', 'c_ccf1'], [coutk])
    k.tt('pool', cout[:, 1, :], f[:, 2, :], f[:, 3, :], ALU.add, ['c_ccf2', 'c_ccf3'], [coutk])


def ssm_host_layouts(lam_re, lam_im, log_dt, b_re, b_im, c_re, c_im):
    f = np.float32
    out = {}
    out['lam_re_bp'] = np.ascontiguousarray(np.broadcast_to(lam_re.reshape(2, 1, 2048), (2, 128, 2048))).astype(f)
    out['lam_im_bp'] = np.ascontiguousarray(np.broadcast_to(lam_im.reshape(2, 1, 2048), (2, 128, 2048))).astype(f)
    out['log_dt_bp'] = np.ascontiguousarray(np.broadcast_to(np.repeat(log_dt, 64, axis=1).reshape(2, 1, 2048),
                                                            (2, 128, 2048))).astype(f)
    bre = np.zeros((2, 128, 4, 8, 64), f)
    bim = np.zeros((2, 128, 4, 8, 64), f)
    for d in range(2):
        for ct in range(4):
            for gl in range(8):
                bre[d, gl * 16:(gl + 1) * 16, ct, gl, :] = b_re[d, ct * 8 + gl].T
                bim[d, gl * 16:(gl + 1) * 16, ct, gl, :] = b_im[d, ct * 8 + gl].T
    out['b_reT_pad'] = bre.reshape(2, 128, 2048)
    out['b_imT_pad'] = bim.reshape(2, 128, 2048)
    out['lam_re_sp'] = np.ascontiguousarray(lam_re.reshape(2, 16, 2, 64).transpose(0, 2, 3, 1).reshape(2, 128, 16))
    out['lam_im_sp'] = np.ascontiguousarray(lam_im.reshape(2, 16, 2, 64).transpose(0, 2, 3, 1).reshape(2, 128, 16))
    ld = log_dt.reshape(2, 16, 2).transpose(0, 2, 1)
    out['log_dt_sp'] = np.ascontiguousarray(np.repeat(ld[:, :, None, :], 64, axis=2).reshape(2, 128, 16)).astype(f)
    cre = np.zeros((2, 128, 16, 128), f)
    cim = np.zeros((2, 128, 16, 128), f)
    for d in range(2):
        for g in range(32):
            pair, g2 = g // 2, g % 2
            col = (g % 8) * 16
            cre[d, g2 * 64:(g2 + 1) * 64, pair, col:col + 16] = c_re[d, g].T
            cim[d, g2 * 64:(g2 + 1) * 64, pair, col:col + 16] = c_im[d, g].T
    out['c_reT_pad'] = cre
    out['c_imT_pad'] = cim
    sig = np.arange(128, dtype=f)
    out['nsig'] = np.stack([-sig, -(127 - sig)]).reshape(2, 128, 1).astype(f)
    tr = np.broadcast_to(sig[None, :], (128, 128))
    out['trow'] = np.ascontiguousarray(np.stack([tr, 127 - tr])).astype(f)
    ii = np.arange(128)
    out['tri'] = np.stack([(ii[:, None] <= ii[None, :]), (ii[:, None] >= ii[None, :])]).astype(f)
    return out


SSM_SHAPES = {'lam_re_bp': [2, 128, 2048], 'lam_im_bp': [2, 128, 2048], 'log_dt_bp': [2, 128, 2048],
              'b_reT_pad': [2, 128, 2048], 'b_imT_pad': [2, 128, 2048], 'lam_re_sp': [2, 128, 16],
              'lam_im_sp': [2, 128, 16], 'log_dt_sp': [2, 128, 16], 'c_reT_pad': [2, 128, 16, 128],
              'c_imT_pad': [2, 128, 16, 128], 'nsig': [2, 128, 1], 'trow': [2, 128, 128], 'tri': [2, 128, 128]}


def prenorm4(k, src, t0, xt, xtk, hn, hnk, hT, hTk, ptr, ptrk, ident, stt, sk, junk, eps_eff, skip_load=()):
    for i in range(4):
        if i not in skip_load:
            k.dma('sp', [(xt[i][:], src[t0 + i * 128:t0 + (i + 1) * 128, :])], writes=KL(xtk[i]))
        k.act(junk[:], xt[i][:], AF.Square, KL(xtk[i]), [sk + 'junk', sk + 'ss%d' % i], accum=stt[:, i:i + 1])
    k.ts('dve', stt[:, 4:8], stt[:, 0:4], 1.0 / D, eps_eff, ALU.mult, ALU.add, [sk + 'ss%d' % i for i in range(4)],
         [sk + 'v'])
    rsqrt_dve(k, stt[:, 8:12], sk + 'r', stt[:, 4:8], sk + 'v', (stt[:, 12:16], stt[:, 16:20], stt[:, 20:24]),
              (sk + 't0', sk + 't1', sk + 't2'), 4)
    for i in range(4):
        h = hn[i % len(hn)]
        hk = hnk[i % len(hn)]
        k.ts('dve', h[:], xt[i][:], stt[:, 8 + i:9 + i], None, ALU.mult, None, KL(xtk[i]) + [sk + 'r'], [hk])
        p = ptr[i % len(ptr)]
        pk = ptrk[i % len(ptr)]
        for kt in range(8):
            k.op('pe', lambda e, kt=kt, p=p, h=h: e.transpose(p[:, kt, :], h[:, kt * 128:(kt + 1) * 128], ident),
                 reads=[hk], writes=[pk], signal=(kt == 7))
        k.cp('act' if i % 2 == 0 else 'dve', hT[:, :, i * 128:(i + 1) * 128], p[:, :, :], [pk], [hTk])


def att_rows(R):
    out = []
    for r in range(R):
        rs = min(max(r - 4, 0), R - 8)
        pr0, pr1 = rs // 2, (rs + 7) // 2
        lst = []
        for pr in range(pr0, pr1 + 1):
            lo_ok = (2 * pr >= rs)
            hi_ok = (2 * pr + 1 <= rs + 7)
            drf = 2 * pr - r
            if lo_ok and hi_ok:
                lst.append((pr, drf + 7))
            elif not lo_ok:
                assert drf == -5
                lst.append((pr, 14))
            else:
                assert drf == 3
                lst.append((pr, 18))
        out.append(lst)
    return out


def att_table_host(rpb):
    tbl = np.full((128, 8, 19, 64), NEG, np.float32)
    qc = np.arange(64)
    cs = np.clip(qc - 8, 0, 48)
    entries = [(e - 7, True, True) for e in range(14)] + [(-5, False, True), (-3, True, True), (-1, True, True),
                                                          (1, True, True), (3, True, False)]
    for e, (drf, lo, hi) in enumerate(entries):
        for half, ok in ((0, lo), (1, hi)):
            if not ok:
                continue
            dr = drf + half
            for kc in range(64):
                okc = (kc >= cs) & (kc < cs + 16)
                dc = kc - qc + 15
                vals = rpb[:, dr + 7, np.clip(dc, 0, 30)]
                tbl[half * 64 + kc, :, e, :] = np.where(okc[None, :], vals, NEG)
    return tbl


def pass1(k, cfg, T):
    nc = k.nc
    with ExitStack() as es:
        def sb(n, s, d=F32):
            return es.enter_context(nc.sbuf_tensor("p1_" + n, s, d))

        def pp(n, s, d=F32):
            return es.enter_context(nc.psum_tensor("p1_" + n, s, d))
        C = ssm_alloc(nc, es, "p1s_")
        ssm_prep(k, T, 0, C)
        W = sb("W", [128, 8, 2048], BF16)
        tbl = sb("tbl", [128, 8, 19, 64], BF16)
        xt = [sb("xt%d" % i, [128, D]) for i in range(4)]
        xtk = ['p1xt%d' % i for i in range(4)]
        hn = [sb("hn%d" % i, [128, D], BF16) for i in range(2)]
        hT = sb("hT", [128, 8, 512], BF16)
        qT = [sb("qT%d" % i, [128, 4, 512], BF16) for i in range(2)]
        kT = [sb("kT%d" % i, [128, 4, 512], BF16) for i in range(3)]
        va = [sb("va%d" % i, [128, 4, 8, 65], BF16) for i in range(3)]
        uT = sb("uT", [128, 4, 512], BF16)
        attT = sb("attT", [128, 4, 512], BF16)
        att = sb("att", [128, 512], BF16)
        sadd = [sb("sadd%d" % i, [128, 320]) for i in range(2)]
        pT = [sb("pT%d" % i, [128, 320], BF16) for i in range(2)]
        rec = sb("rec", [128, 8])
        yfb = [sb("yfb%d" % i, [128, 4, 128]) for i in range(2)]
        g1col = sb("g1col", [128, 8])
        ident = sb("ident", [128, 128], BF16)
        identf = sb("identf", [128, 128])
        stt = sb("stt", [128, 24])
        junk = sb("junk", [128, D], BF16)
        B0 = [pp("B0_%d" % i, [128, 8, 128], BF16) for i in range(1)]
        B1 = pp("B1", [128, 512])
        B2 = pp("B2", [128, 512])
        B3 = pp("B3", [128, 512])
        B4 = pp("B4", [128, 512])
        B5 = pp("B5", [128, 512])
        B6 = pp("B6", [128, 512])
        B7 = pp("B7", [128, 4, 128])
        PS = {'bu0': B3, 'bu1': B4, 'cu0': B5, 'cu1': B6}
        PK = {'bu0': 'B3', 'bu1': 'B4', 'cu0': 'B5', 'cu1': 'B6'}
        B7K = ['B7_%d' % i for i in range(4)]
        WH = [xt[2][:, 0:512], xt[2][:, 512:1024], xt[3][:, 0:512], xt[3][:, 512:1024]]
        WHK = [xtk[2], xtk[2] + 'b', xtk[3], xtk[3] + 'b']

        k.dma('sp', [(g1col[:], T['g_mix_pre_col'][:, :]), (identf[:], T['identf'][:, :])],
              writes=['p1g1col', 'p1identf'])
        k.cp('dve', ident[:], identf[:], ['p1identf'], ['ident'])
        win_v = T['w_in'].rearrange("(kt p) c -> p kt c", p=128)
        load_convert_weight(k, W, 'p1W', win_v[:, :, 0:2048], 8, 2048, [t[:] for t in xt], xtk, scale_col=g1col,
                            chunk=128)
        tb_v = T['att_tbl'].rearrange("p h e q -> p (h e q)")
        tb_d = tbl[:].rearrange("p h e q -> p (h e q)")
        for c0 in range(0, 8 * 19 * 64, 1024):
            cwid = min(1024, 8 * 19 * 64 - c0)
            i = (c0 // 1024) % 4
            k.dma('sp', [(xt[i][:, 0:cwid], tb_v[:, c0:c0 + cwid])], writes=[xtk[i]])
            k.cp('dve', tb_d[:, c0:c0 + cwid], xt[i][:, 0:cwid], [xtk[i]], ['p1tbl'])
        for i in range(3):
            k.op('dve', lambda e, i=i: e.memset(va[i][:], 1.0), writes=['va%d' % i])

        def make_att_items(am, off, rows):
            items = []
            q_ = qT[am % 2]
            qk = 'qT%d' % (am % 2)
            ta0 = off + am * 512
            state = {'ui': 0}

            SB = [(B1, 'B1'), (B0[0][:].rearrange("p a b -> p (a b)").bitcast(F32), 'B0')]

            def unit_s(tile, lr2, h):
                lr = tile * 2 + lr2
                lst = rows[am * 8 + lr]
                na = len(lst)
                ct, hp = h // 2, h % 2
                ps0, ps1 = 64 * hp, 64 * hp + 64
                ui = state['ui']
                state['ui'] += 1
                sbk, sk_ = SB[ui % 2]
                for a, (pr, ent) in enumerate(lst):
                    pm, pt = pr // 4, pr % 4
                    kt_ = kT[pm % 3]
                    k.op('pe', lambda e, a=a, kt_=kt_, pt=pt: e.matmul(
                        sbk[:, a * 64:(a + 1) * 64], lhsT=kt_[ps0:ps1, ct, pt * 128:(pt + 1) * 128],
                        rhs=q_[ps0:ps1, ct, lr * 64:(lr + 1) * 64], start=True, stop=True),
                        reads=['kT%d' % (pm % 3), qk], writes=[sk_], signal=(a == na - 1))
                sa = sadd[ui % 2]
                sak = 'sadd%d' % (ui % 2)
                e0 = lst[0][1]
                tv = tbl[:, h, e0:e0 + 7:2, :] if na == 4 else tbl[:, h, 14:19, :]
                k.tt('dve', sa[:, 0:na * 64].rearrange("p (a q) -> p a q", a=na),
                     sbk[:, 0:na * 64].rearrange("p (a q) -> p a q", a=na), tv, ALU.add, [sk_, 'p1tbl'], [sak])
                pt_ = pT[ui % 2]
                ptk = 'pT%d' % (ui % 2)
                k.act(pt_[:, 0:na * 64], sa[:, 0:na * 64], AF.Exp, [sak], [ptk])
                return (lr2, h, lst, pt_, ptk)

            def unit_pv(info):
                lr2, h, lst, pt_, ptk = info
                na = len(lst)
                hh = h % 4
                for a, (pr, ent) in enumerate(lst):
                    pm, pt = pr // 4, pr % 4
                    k.op('pe', lambda e, a=a, pm=pm, pt=pt: e.matmul(
                        B2[64 * lr2:64 * lr2 + 64, hh * 65:hh * 65 + 65],
                        lhsT=pt_[:, a * 64:(a + 1) * 64], rhs=va[pm % 3][:, pt, h, :],
                        start=(a == 0), stop=(a == na - 1)),
                        reads=[ptk, 'va%d' % (pm % 3)], writes=['B2'], signal=(a == na - 1))

            def norm(g):
                ov = B2[:, 0:260].rearrange("p (h d) -> p h d", h=4)
                k.op('dve', lambda e: e.reciprocal(out=rec[:, g * 4:(g + 1) * 4], in_=ov[:, :, 64]),
                     reads=['B2'], writes=['rec%d' % g])
                k.tt('dve', att[:, g * 256:(g + 1) * 256].rearrange("p (h d) -> p h d", h=4),
                     ov[:, :, 0:64], rec[:, g * 4:(g + 1) * 4].unsqueeze(2).broadcast_to([128, 4, 64]),
                     ALU.mult, ['B2', 'rec%d' % g], ['att'])

            def trans(tile):
                for ct in range(4):
                    k.op('pe', lambda e, ct=ct: e.transpose(B0[0][:, ct, :], att[:, ct * 128:(ct + 1) * 128],
                                                            ident[:]),
                         reads=['att'], writes=['B0'], signal=(ct == 3))
                k.cp('act', attT[:, :, tile * 128:(tile + 1) * 128], B0[0][:, 0:4, :], ['B0'], ['attT'])

            def spill():
                k.dma('pool', [(T['attT_s'][ct, :, ta0:ta0 + 512], attT[:, ct, :]) for ct in range(4)],
                      reads=['attT'], writes=['attT_s'])

            seq = []
            for tile in range(4):
                for g in range(2):
                    for lr2 in range(2):
                        for hh in range(4):
                            seq.append(('u', tile, lr2, g * 4 + hh))
                    seq.append(('n', g))
                seq.append(('t', tile))
            pend = {'info': None, 'post': []}

            def step(ent_):
                if ent_[0] == 'u':
                    info = unit_s(ent_[1], ent_[2], ent_[3])
                    flush()
                    pend['info'] = info
                else:
                    pend['post'].append(ent_)

            def flush():
                if pend['info'] is not None:
                    unit_pv(pend['info'])
                    pend['info'] = None
                for p_ in pend['post']:
                    if p_[0] == 'n':
                        norm(p_[1])
                    else:
                        trans(p_[1])
                pend['post'] = []

            for ent_ in seq:
                items.append(lambda ent_=ent_: step(ent_))
            items.append(flush)
            items.append(spill)
            return items

        x = T['x']
        starts1 = [off_ + m_ * 512 for (off_, L_) in cfg['seqs'] for m_ in range(L_ // 512)]
        pre1 = set()
        pre1b = set()
        for (off, L) in cfg['seqs']:
            R = L // 64
            nmb = L // 512
            rows = att_rows(R)
            att_items = []
            for mb in range(nmb + 1):
                t0 = off + mb * 512
                if mb < nmb:
                    xk2 = [(q_, q_ + 'b') for q_ in xtk]
                    prenorm4(k, x, t0, xt, xk2, hn, ['p1hn0', 'p1hn1'], hT, 'p1hT', B0, ['B0'], ident[:], stt,
                             'p1s_', junk, EPS, skip_load=((0, 1) if t0 in pre1 else ()) + ((2, 3) if t0 in pre1b else ()))
                    nxt = starts1.index(t0) + 1
                    if nxt < len(starts1):
                        tn = starts1[nxt]
                        for i_ in range(2):
                            k.dma('sp', [(xt[i_][:], x[tn + i_ * 128:tn + (i_ + 1) * 128, :])], writes=KL(xk2[i_]))
                        pre1.add(tn)
                    q_ = qT[mb % 2]
                    qk = 'qT%d' % (mb % 2)
                    k_ = kT[mb % 3]
                    kk = 'kT%d' % (mb % 3)
                    v_ = va[mb % 3]
                    vk = 'va%d' % (mb % 3)
                    nacc = 0
                    for grp, dst, dk, c0w, scale in ((0, q_, qk, 0, 0.125), (1, k_, kk, 512, 1.0),
                                                      (2, uT, 'uT', 1536, 1.0)):
                        for j in range(4):
                            pb, pbk = (B1, 'B1') if nacc % 2 == 0 else (B2, 'B2')
                            nacc += 1
                            for kt in range(8):
                                k.op('pe', lambda e, kt=kt, j=j, pb=pb, c0w=c0w: e.matmul(
                                    pb[:, :], lhsT=W[:, kt, c0w + j * 128:c0w + (j + 1) * 128], rhs=hT[:, kt, :],
                                    start=(kt == 0), stop=(kt == 7)), reads=['p1W', 'p1hT'], writes=[pbk],
                                    signal=(kt == 7))
                            if nacc % 2 == 0:
                                k.act(dst[:, j, :], pb[:, :], AF.Copy, [pbk], [dk], scale=scale)
                            else:
                                k.ts('dve', dst[:, j, :], pb[:, :], scale, None, ALU.mult, None, [pbk], [dk])
                    for i in range(4):
                        pb, pbk = (B1, 'B1') if nacc % 2 == 0 else (B2, 'B2')
                        nacc += 1
                        for kt in range(8):
                            k.op('pe', lambda e, kt=kt, i=i, pb=pb: e.matmul(
                                pb[:, :], lhsT=hT[:, kt, i * 128:(i + 1) * 128], rhs=W[:, kt, 1024:1536],
                                start=(kt == 0), stop=(kt == 7)), reads=['p1W', 'p1hT'], writes=[pbk],
                                signal=(kt == 7))
                        k.cp('act' if i % 2 == 0 else 'dve', v_[:, i, :, 0:64],
                             pb[:, :].rearrange("p (h d) -> p h d", h=8), [pbk], [vk])
                    k.dma('pool', [(T['uT_s'][ct, :, t0:t0 + 512], uT[:, ct, :]) for ct in range(4)], reads=['uT'],
                          writes=['uT_s'])
                if mb < nmb:
                    def on_blk(gb, c0, t0=t0):
                        b = c0 // 128
                        yb_ = yfb[b % 2]
                        ybk = 'yfb%d' % (b % 2)
                        k.cp('act', yb_[:], B7[:, :, :], B7K, [ybk])
                        k.dma('pool', [(T['yf_s'][ct, :, t0 + c0:t0 + c0 + 128], yb_[:, ct, :]) for ct in range(4)],
                              reads=[ybk], writes=['yf_s'])
                    ssm_mb(k, C, uT, 'uT', PS, PK, 0, [(mb * 4 + b, b * 128) for b in range(4)], ident[:], B7, B7K,
                           WH, WHK, on_blk, filler=att_items)
                    nxt = starts1.index(t0) + 1
                    if nxt < len(starts1):
                        tn = starts1[nxt]
                        for i_ in (2, 3):
                            k.dma('sp', [(xt[i_][:], x[tn + i_ * 128:tn + (i_ + 1) * 128, :])], writes=KL(xk2[i_]))
                        pre1b.add(tn)
                while att_items:
                    att_items.pop(0)()
                if mb < nmb:
                    att_items = make_att_items(mb, off, rows)
    k.barrier()


def pass2(k, cfg, T):
    nc = k.nc
    with ExitStack() as es:
        def sb(n, s, d=F32):
            return es.enter_context(nc.sbuf_tensor("p2_" + n, s, d))

        def pp(n, s, d=F32):
            return es.enter_context(nc.psum_tensor("p2_" + n, s, d))
        C = ssm_alloc(nc, es, "p2s_")
        ssm_prep(k, T, 1, C)
        Wg = sb("Wg", [128, 8, 2048], BF16)
        Wba = sb("Wba", [128, 4, D], BF16)
        Wbs = sb("Wbs", [128, 4, D], BF16)
        Wout = sb("Wout", [128, 8, D], BF16)
        Wglu = sb("Wglu", [128, 4, 512], BF16)
        xt = [sb("xt%d" % i, [128, D]) for i in range(4)]
        xtk = ['p2xt%d' % i for i in range(4)]
        hn = [sb("hn0", [128, D], BF16)]
        hT = sb("hT", [128, 8, 512], BF16)
        uT = sb("uT", [128, 4, 512], BF16)
        attT = sb("attT", [128, 4, 512], BF16)
        ssmT = sb("ssmT", [128, 4, 512], BF16)
        mT = sb("mT", [128, 8, 512], BF16)
        yfb = sb("yfb", [128, 4, 128])
        yw = [sb("yw%d" % i, [128, 4, 128]) for i in range(3)]
        ygT = sb("ygT", [128, 4, 128], BF16)
        ybuf = sb("ybuf", [128, 4, 128])
        gbc = sb("gbc", [128, D])
        g1col = sb("g1col", [128, 8])
        dcol = sb("dcol", [128, 4])
        bglu = sb("bglu", [128, 4])
        ident = sb("ident", [128, 128], BF16)
        identf = sb("identf", [128, 128])
        stt = sb("stt", [128, 32])
        junk = sb("junk", [128, D], BF16)
        B0 = [pp("B0", [128, 8, 128], BF16)]
        B1 = pp("B1", [128, 512])
        B2 = pp("B2", [128, 512])
        B3 = pp("B3", [128, 512])
        B4 = pp("B4", [128, 512])
        B5 = pp("B5", [128, 512])
        B6 = pp("B6", [128, 512])
        B7 = pp("B7", [128, 4, 128])
        PS = {'bu0': B3, 'bu1': B4, 'cu0': B5, 'cu1': B6}
        PK = {'bu0': 'B3', 'bu1': 'B4', 'cu0': 'B5', 'cu1': 'B6'}
        B7K = ['B7_%d' % i for i in range(4)]
        B0f = B0[0][:].rearrange("p a b -> p (a b)").bitcast(F32)
        WH = [xt[2][:, 0:512], xt[2][:, 512:1024], xt[3][:, 0:512], xt[3][:, 512:1024]]
        WHK = [xtk[2], xtk[2] + 'b', xtk[3], xtk[3] + 'b']

        k.dma('sp', [(g1col[:], T['g_mix_pre_col'][:, :]), (identf[:], T['identf'][:, :]),
                     (gbc[:], T['g_mix_post_bc'][:, :]), (dcol[:], T['ssm_d_col'][:, :]),
                     (bglu[:], T['b_glu_col'][:, :])],
              writes=['p2g1col', 'p2identf', 'p2gbc', 'p2dcol', 'p2bglu'])
        k.cp('dve', ident[:], identf[:], ['p2identf'], ['ident'])
        k.ts('dve', bglu[:], bglu[:], 0.5, None, ALU.mult, None, ['p2bglu'], ['p2bglu'])
        win_v = T['w_in'].rearrange("(kt p) c -> p kt c", p=128)
        xs = [t[:] for t in xt]
        load_convert_weight(k, Wg, 'p2Wg', win_v[:, :, 2048:4096], 8, 2048, xs, xtk, scale_col=g1col, chunk=128)
        load_convert_weight(k, Wba, 'p2Wba', T['w_branch_att'].rearrange("(kt p) c -> p kt c", p=128), 4, D, xs,
                            xtk, chunk=256)
        load_convert_weight(k, Wbs, 'p2Wbs', T['w_branch_ssm'].rearrange("(kt p) c -> p kt c", p=128), 4, D, xs,
                            xtk, const_scale=0.25, chunk=256)
        load_convert_weight(k, Wout, 'p2Wout', T['w_out'].rearrange("(kt p) c -> p kt c", p=128), 8, D, xs, xtk,
                            chunk=128)
        load_convert_weight(k, Wglu, 'p2Wglu', T['w_glu'].rearrange("(kt p) c -> p kt c", p=128), 4, 512, xs, xtk,
                            chunk=256)
        x = T['x']
        starts2 = [off_ + m_ * 512 for (off_, L_) in cfg['seqs'] for m_ in reversed(range(L_ // 512))]
        pre2 = set()
        for (off, L) in cfg['seqs']:
            nmb = L // 512
            for mi, mb in enumerate(reversed(range(nmb))):
                t0 = off + mb * 512
                xk2 = [(q_, q_ + 'b') for q_ in xtk]
                prenorm4(k, x, t0, xt, xk2, hn, ['p2hn0'], hT, 'p2hT', B0, ['B0'], ident[:], stt, 'p2s_',
                         junk, EPS, skip_load=(0, 1, 2, 3) if t0 in pre2 else ())
                k.dma('sp', [(uT[:, ct, :], T['uT_s'][ct, :, t0:t0 + 512]) for ct in range(4)], reads=['uT_s'],
                      writes=['uT'])
                k.dma('sp', [(attT[:, ct, :], T['attT_s'][ct, :, t0:t0 + 512]) for ct in range(4)],
                      reads=['attT_s'], writes=['attT'])
                fill = []

                def on_blk(gb, c0, t0=t0):
                    y0, y1, y2 = yw
                    pend_ = [f for f in fill if getattr(f, 'chain', False)]
                    for f in pend_:
                        fill.remove(f)
                        f()
                    it = []
                    k.cp('act', ybuf[:], B7[:, :, :], B7K, ['ybuf'])
                    it.append(lambda: k.dma('sp', [(yfb[:, ct, :], T['yf_s'][ct, :, t0 + c0:t0 + c0 + 128])
                                                   for ct in range(4)], reads=['yf_s'], writes=['yfb']))
                    it.append(lambda: k.tt('dve', y0[:], uT[:, :, c0:c0 + 128],
                                           dcol[:, :].unsqueeze(2).broadcast_to([128, 4, 128]), ALU.mult,
                                           ['uT', 'p2dcol'], ['yw0']))
                    it.append(lambda: k.tt('pool', y0[:], y0[:], yfb[:], ALU.add, ['yw0', 'yfb'], ['yw0']))
                    it.append(lambda: k.tt('dve', y0[:], y0[:], ybuf[:], ALU.add, ['yw0', 'ybuf'], ['yw0']))
                    it.append(lambda: k.act(y1[:], y0[:], AF.Square, ['yw0'], ['yw1']))
                    it.append(lambda: k.ts('pool', y1[:], y1[:], GC2, GC1, ALU.mult, ALU.add, ['yw1'], ['yw1']))
                    it.append(lambda: k.tt('pool', y1[:], y1[:], y0[:], ALU.mult, ['yw1', 'yw0'], ['yw1']))
                    it.append(lambda: k.act(y2[:], y1[:], AF.Tanh, ['yw1'], ['yw2'], scale=0.5))
                    it.append(lambda: k.stt('dve', y1[:], y2[:], 1.0, y0[:], ALU.add, ALU.mult, ['yw2', 'yw0'],
                                            ['yw1']))
                    it.append(lambda: k.cp('pool', ygT[:], y1[:], ['yw1'], ['ygT']))

                    def glu_mm():
                        for cto in range(4):
                            for kt in range(4):
                                k.op('pe', lambda e, cto=cto, kt=kt: e.matmul(
                                    B0f[:, cto * 128:(cto + 1) * 128], lhsT=Wglu[:, kt, cto * 128:(cto + 1) * 128],
                                    rhs=ygT[:, kt, :], start=(kt == 0), stop=(kt == 3)), reads=['p2Wglu', 'ygT'],
                                    writes=['B0'], signal=(kt == 3))
                    it.append(glu_mm)

                    def glu_tanh():
                        for cto in range(4):
                            k.act(y2[:, cto, :], B0f[:, cto * 128:(cto + 1) * 128], AF.Tanh, ['B0', 'p2bglu'], ['yw2'],
                                  scale=0.25, bias=bglu[:, cto:cto + 1])
                    it.append(glu_tanh)
                    it.append(lambda: k.stt('dve', ssmT[:, :, c0:c0 + 128], y2[:], 1.0, y1[:], ALU.add, ALU.mult,
                                            ['yw2', 'yw1'], ['ssmT']))
                    for f in it:
                        f.chain = True
                    fill[0:0] = it

                def gate_a(j):
                    for kt in range(8):
                        k.op('pe', lambda e, kt=kt: e.matmul(B1[:, :], lhsT=Wg[:, kt, j * 128:(j + 1) * 128],
                                                          rhs=hT[:, kt, :], start=(kt == 0), stop=(kt == 7)),
                             reads=['p2Wg', 'p2hT'], writes=['B1'], signal=(kt == 7))
                    for kt in range(4):
                        k.op('pe', lambda e, kt=kt: e.matmul(B2[:, :], lhsT=Wba[:, kt, j * 128:(j + 1) * 128],
                                                          rhs=attT[:, kt, :], start=(kt == 0), stop=(kt == 3)),
                             reads=['p2Wba', 'attT'], writes=['B2'], signal=(kt == 3))
                    ga = xt[0][:, 0:512]
                    k.act(ga, B1[:, :], AF.Tanh, ['B1'], [xtk[0]], scale=0.5)
                    k.stt('dve', mT[:, j, :], ga, 1.0, B2[:, :], ALU.add, ALU.mult, [xtk[0], 'B2'], ['mT'])
                fill.extend([(lambda j=j: gate_a(j)) for j in range(8)])
                ssm_mb(k, C, uT, 'uT', PS, PK, 1, [(mi * 4 + bi, b * 128) for bi, b in enumerate(reversed(range(4)))],
                       ident[:], B7, B7K, WH, WHK, on_blk, filler=fill)
                while fill:
                    fill.pop(0)()
                for j in range(8):
                    pa, pak = (B1, 'B1') if j % 2 == 0 else (B3, 'B3')
                    pb_, pbk = (B2, 'B2') if j % 2 == 0 else (B4, 'B4')
                    for kt in range(8):
                        k.op('pe', lambda e, kt=kt, j=j, pa=pa: e.matmul(
                            pa[:, :], lhsT=Wg[:, kt, 1024 + j * 128:1024 + (j + 1) * 128], rhs=hT[:, kt, :],
                            start=(kt == 0), stop=(kt == 7)), reads=['p2Wg', 'p2hT'], writes=[pak], signal=(kt == 7))
                    for kt in range(4):
                        k.op('pe', lambda e, kt=kt, j=j, pb_=pb_: e.matmul(
                            pb_[:, :], lhsT=Wbs[:, kt, j * 128:(j + 1) * 128], rhs=ssmT[:, kt, :],
                            start=(kt == 0), stop=(kt == 3)), reads=['p2Wbs', 'ssmT'], writes=[pbk], signal=(kt == 3))
                    gs = xt[0][:, (j % 2) * 512:(j % 2 + 1) * 512]
                    gsk = xtk[0] if j % 2 == 0 else xtk[0] + 'b'
                    m2 = xt[1][:, (j % 2) * 512:(j % 2 + 1) * 512]
                    m2k = xtk[1] if j % 2 == 0 else xtk[1] + 'b'
                    k.act(gs, pa[:, :], AF.Tanh, [pak], [gsk], scale=0.5)
                    k.stt('dve', m2, gs, 1.0, pb_[:, :], ALU.add, ALU.mult, [gsk, pbk], [m2k])
                    k.tt('pool', mT[:, j, :], mT[:, j, :], m2, ALU.add, ['mT', m2k], ['mT'])
                nxt = starts2.index(t0) + 1
                if nxt < len(starts2):
                    tn = starts2[nxt]
                    for i_ in range(4):
                        k.dma('sp', [(xt[i_][:], x[tn + i_ * 128:tn + (i_ + 1) * 128, :])], writes=KL(xk2[i_]))
                    pre2.add(tn)
                for i in range(4):
                    WB = ((B1, 'B1'), (B2, 'B2')) if i % 2 == 0 else ((B3, 'B3'), (B4, 'B4'))
                    for h, (pb, pbk) in enumerate(WB):
                        for kt in range(8):
                            k.op('pe', lambda e, kt=kt, i=i, h=h, pb=pb: e.matmul(
                                pb[:, :], lhsT=mT[:, kt, i * 128:(i + 1) * 128], rhs=Wout[:, kt, h * 512:(h + 1) * 512],
                                start=(kt == 0), stop=(kt == 7)), reads=['mT', 'p2Wout'], writes=[pbk],
                                signal=(kt == 7))
                    if i % 2 == 0:
                        xb, xk = attT[:].rearrange("p a b -> p (a b)").bitcast(F32), 'attT'
                    else:
                        xb, xk = uT[:].rearrange("p a b -> p (a b)").bitcast(F32), 'uT'
                    tb_, tbk = ssmT[:].rearrange("p a b -> p (a b)").bitcast(F32), 'ssmT'
                    k.dma('sp', [(xb[:], x[t0 + i * 128:t0 + (i + 1) * 128, :])], writes=[xk])
                    k.act(junk[:, 0:512], WB[0][0][:, :], AF.Square, [WB[0][1]], ['p2s_junk', 'p2s_a0'], accum=stt[:, 24:25])
                    k.act(junk[:, 512:1024], WB[1][0][:, :], AF.Square, [WB[1][1]], ['p2s_junk', 'p2s_a1'], accum=stt[:, 25:26])
                    k.tt('dve', stt[:, 26:27], stt[:, 24:25], stt[:, 25:26], ALU.add, ['p2s_a0', 'p2s_a1'], ['p2s_a2'])
                    k.ts('dve', stt[:, 27:28], stt[:, 26:27], 1.0 / D, 4.0 * EPS, ALU.mult, ALU.add, ['p2s_a2'],
                         ['p2s_a3'])
                    rsqrt_dve(k, stt[:, 28:29], 'p2s_r2', stt[:, 27:28], 'p2s_a3',
                              (stt[:, 29:30], stt[:, 30:31], stt[:, 31:32]), ('p2s_u0', 'p2s_u1', 'p2s_u2'), 1)
                    k.tt('dve', tb_[:, 0:512], WB[0][0][:, :], gbc[:, 0:512], ALU.mult, [WB[0][1], 'p2gbc'], [tbk])
                    k.tt('dve', tb_[:, 512:1024], WB[1][0][:, :], gbc[:, 512:1024], ALU.mult, [WB[1][1], 'p2gbc'], [tbk])
                    k.stt('dve', xb[:], tb_[:], stt[:, 28:29], xb[:], ALU.mult, ALU.add, [tbk, 'p2s_r2', xk], [xk])
                    k.dma('pool', [(T['x1s'][t0 + i * 128:t0 + (i + 1) * 128, :], xb[:])], reads=[xk],
                          writes=['x1s_dram'])
    k.barrier()


IN_SHAPES = {
    'x': None, 'w_in': [D, 4096], 'w_up': [D, 2 * DFF], 'w_down': [DFF, D], 'w_branch_att': [512, D],
    'w_branch_ssm': [512, D], 'w_out': [D, D], 'w_glu': [512, 512],
    'g_mix_pre_col': [128, 8], 'g_ffn_pre_col': [128, 8], 'g_mix_post_bc': [128, D], 'g_ffn_post_bc': [128, D],
    'conv_w_l': [128, 44, 3], 'conv_b_l': [128, 44], 'identf': [128, 128], 'ssm_d_col': [128, 4],
    'b_glu_col': [128, 4], 'att_tbl': [128, 8, 19, 64],
}
IN_SHAPES.update(SSM_SHAPES)


def build_program(cfg):
    nc = bass.Bass("TRN2", target_bir_lowering=False)
    NT = cfg['NT']
    T = {}
    for n, shp in IN_SHAPES.items():
        shp = [NT, D] if n == 'x' else shp
        T[n] = nc.dram_tensor(n, shp, F32, kind="ExternalInput").ap()
    T['y'] = nc.dram_tensor('y', [NT, D], F32, kind="ExternalOutput").ap()
    T['x1s'] = nc.dram_tensor('x1s', [NT, D], F32, kind="Internal").ap()
    T['uT_s'] = nc.dram_tensor('uT_s', [4, 128, NT], BF16, kind="Internal").ap()
    T['attT_s'] = nc.dram_tensor('attT_s', [4, 128, NT], BF16, kind="Internal").ap()
    T['yf_s'] = nc.dram_tensor('yf_s', [4, 128, NT], F32, kind="Internal").ap()
    with ExitStack() as es:
        k = KB(nc, es)
        pass1(k, cfg, T)
        pass2(k, cfg, T)
        pass3(k, cfg, T)
        k.finish('pool')
        k.finish('sp')
    return nc


def host_consts(g_mix_pre, g_mix_post, attn_rpb, ssm_lam_re, ssm_lam_im, ssm_log_dt, ssm_b_re, ssm_b_im,
                ssm_c_re, ssm_c_im, ssm_d, b_glu, g_ffn_pre, g_ffn_post, conv_w, conv_b):
    f = np.float32
    c = {}
    c['g_mix_pre_col'] = np.ascontiguousarray(g_mix_pre.reshape(8, 128).T).astype(f)
    c['g_ffn_pre_col'] = np.ascontiguousarray(g_ffn_pre.reshape(8, 128).T).astype(f)
    c['g_mix_post_bc'] = np.ascontiguousarray(np.broadcast_to(g_mix_post.reshape(1, D), (128, D))).astype(f)
    c['g_ffn_post_bc'] = np.ascontiguousarray(np.broadcast_to(g_ffn_post.reshape(1, D), (128, D))).astype(f)
    c['conv_w_l'] = np.ascontiguousarray(conv_w.T.reshape(44, 128, 3).transpose(1, 0, 2)).astype(f)
    c['conv_b_l'] = np.ascontiguousarray(conv_b.reshape(44, 128).T).astype(f)
    c['identf'] = np.eye(128, dtype=f)
    c['ssm_d_col'] = np.ascontiguousarray(ssm_d.reshape(4, 128).T).astype(f)
    c['b_glu_col'] = np.ascontiguousarray(b_glu.reshape(4, 128).T).astype(f)
    c['att_tbl'] = att_table_host(np.asarray(attn_rpb, f))
    c.update(ssm_host_layouts(ssm_lam_re, ssm_lam_im, ssm_log_dt, ssm_b_re, ssm_b_im, ssm_c_re, ssm_c_im))
    return c


_PROG = {}


def run_layer(xs_per_core, cfg, weights, consts, n_cores):
    key = (cfg['NT'], tuple(cfg['seqs']))
    if key not in _PROG:
        _PROG[key] = build_program(cfg)
    nc = _PROG[key]
    in_maps = []
    for c in range(n_cores):
        m = {'x': xs_per_core[c]}
        m.update(weights)
        m.update(consts)
        in_maps.append(m)
    res = run_bass_kernel_spmd(nc, in_maps, core_ids=list(range(n_cores)))
    return [r['y'] for r in res.results]


def kernel(x_prompt, x_sample, g_mix_pre, g_mix_post, w_in, attn_rpb, ssm_lam_re, ssm_lam_im, ssm_log_dt,
           ssm_b_re, ssm_b_im, ssm_c_re, ssm_c_im, ssm_d, w_glu, b_glu, w_branch_att, w_branch_ssm, w_out,
           g_ffn_pre, g_ffn_post, w_up, conv_w, conv_b, w_down):
    f = np.float32
    A = lambda a: np.asarray(a, f)
    x_prompt = A(x_prompt)
    x_sample = A(x_sample)
    consts = host_consts(A(g_mix_pre)[0], A(g_mix_post)[0], A(attn_rpb)[0], A(ssm_lam_re)[0], A(ssm_lam_im)[0],
                         A(ssm_log_dt)[0], A(ssm_b_re)[0], A(ssm_b_im)[0], A(ssm_c_re)[0], A(ssm_c_im)[0],
                         A(ssm_d)[0], A(b_glu)[0], A(g_ffn_pre)[0], A(g_ffn_post)[0], A(conv_w)[0], A(conv_b)[0])
    weights = {'w_in': np.ascontiguousarray(A(w_in)[0]), 'w_up': np.ascontiguousarray(A(w_up)[0]),
               'w_down': np.ascontiguousarray(A(w_down)[0]), 'w_branch_att': np.ascontiguousarray(A(w_branch_att)[0]),
               'w_branch_ssm': np.ascontiguousarray(A(w_branch_ssm)[0]), 'w_out': np.ascontiguousarray(A(w_out)[0]),
               'w_glu': np.ascontiguousarray(A(w_glu)[0])}
    n = 8
    Lp, Ls = x_prompt.shape[1], x_sample.shape[1]
    cfg = {'NT': Lp + 2 * Ls, 'seqs': [(0, Lp), (Lp, Ls), (Lp + Ls, Ls)]}
    xs = [np.ascontiguousarray(np.concatenate([x_prompt[c], x_sample[2 * c], x_sample[2 * c + 1]], axis=0))
          for c in range(n)]
    ys = run_layer(xs, cfg, weights, consts, n)
    y_prompt = np.stack([y[0:Lp] for y in ys]).astype(f)
    y_sample = np.stack([ys[c // 2][Lp + (c % 2) * Ls:Lp + (c % 2 + 1) * Ls] for c in range(2 * n)]).astype(f)
    return (y_prompt, y_sample)


def ssm_mb(k, C, uT, uTk, PS, PK, d, blocks, ident, yps, ypk, wh, whk, on_block, filler=None):
    steps = [(gb, c0, ct) for (gb, c0) in blocks for ct in range(4)]
    n = len(steps)
    pre, post, H, wz = C['pre'], C['post'], C['H'], C['w']
    last = 127 if d == 0 else 0

    def s_bu(gb, c0, ct):
        for comp in range(2):
            k.op('pe', lambda e, comp=comp: e.matmul(PS['bu%d' % comp][:, :], lhsT=uT[:, ct, c0:c0 + 128],
                                                    rhs=C['Bpad'][:, ct, comp, :], start=True, stop=True),
                 reads=[uTk, 'c_Bpad'], writes=[PK['bu%d' % comp]])

    def s_zm(gb, c0, ct):
        pr = pre[:, 0, ct * 512:(ct + 1) * 512]
        pi_ = pre[:, 1, ct * 512:(ct + 1) * 512]
        bur, bui = PS['bu0'][:, :], PS['bu1'][:, :]
        k.tt('dve', wz[0][:], pr, bur, ALU.mult, ['c_pre0', PK['bu0']], ['c_w0'])
        k.tt('dve', wz[1][:], pi_, bui, ALU.mult, ['c_pre1', PK['bu1']], ['c_w1'])
        k.tt('dve', wz[2][:], pr, bui, ALU.mult, ['c_pre0', PK['bu1']], ['c_w2'])
        k.tt('dve', wz[3][:], pi_, bur, ALU.mult, ['c_pre1', PK['bu0']], ['c_w3'])

    def s_zc(gb, c0, ct):
        k.tt('pool', C['zre'][:, ct * 512:(ct + 1) * 512], wz[0][:], wz[1][:], ALU.subtract, ['c_w0', 'c_w1'],
             ['c_zre%d' % ct])
        k.tt('pool', C['zim'][:, ct * 512:(ct + 1) * 512], wz[2][:], wz[3][:], ALU.add, ['c_w2', 'c_w3'],
             ['c_zim%d' % ct])

    def s_cs(gb, c0, ct):
        first = (gb == 0)
        cin = C['cc'][gb % 2]
        cink = 'c_cc%d_%d' % (gb % 2, ct)
        for comp, zz, zk in ((0, C['zre'], 'c_zre%d' % ct), (1, C['zim'], 'c_zim%d' % ct)):
            for q in range(4):
                pair = ct * 4 + q
                k.op('pe', lambda e, zz=zz, q=q, comp=comp: e.matmul(
                    PS['cu%d' % comp][:, q * 128:(q + 1) * 128],
                    lhsT=zz[:, ct * 512 + q * 128:ct * 512 + (q + 1) * 128], rhs=C['trib'][:, :],
                    start=True, stop=first), reads=[zk, 'c_trib'], writes=[PK['cu%d' % comp]], signal=first)
                if not first:
                    k.op('pe', lambda e, q=q, pair=pair, comp=comp: e.matmul(
                        PS['cu%d' % comp][:, q * 128:(q + 1) * 128], lhsT=ident,
                        rhs=cin[:, comp, pair:pair + 1].broadcast_to([128, 128]), start=False, stop=True),
                        reads=[cink, 'ident'], writes=[PK['cu%d' % comp]])

    def s_hm(gb, c0, ct):
        por = post[:, 0, ct * 4:(ct + 1) * 4, :].rearrange("p c x -> p (c x)")
        poi = post[:, 1, ct * 4:(ct + 1) * 4, :].rearrange("p c x -> p (c x)")
        cur, cui = PS['cu0'][:, :], PS['cu1'][:, :]
        k.tt('dve', wh[0], por, cur, ALU.mult, ['c_post0', PK['cu0']], [whk[0]])
        k.tt('dve', wh[1], poi, cui, ALU.mult, ['c_post1', PK['cu1']], [whk[1]])
        k.tt('dve', wh[2], por, cui, ALU.mult, ['c_post0', PK['cu1']], [whk[2]])
        k.tt('dve', wh[3], poi, cur, ALU.mult, ['c_post1', PK['cu0']], [whk[3]])

    def s_hc(gb, c0, ct):
        k.tt('pool', H[:, 0, ct * 4:(ct + 1) * 4, :].rearrange("p c x -> p (c x)"), wh[0], wh[1], ALU.subtract,
             [whk[0], whk[1]], ['c_H0_%d' % ct])
        k.tt('pool', H[:, 1, ct * 4:(ct + 1) * 4, :].rearrange("p c x -> p (c x)"), wh[2], wh[3], ALU.add,
             [whk[2], whk[3]], ['c_H1_%d' % ct])

    def s_carry(gb, c0, ct):
        cout = C['cc'][(gb + 1) % 2]
        coutk = 'c_cc%d_%d' % ((gb + 1) % 2, ct)
        sl = slice(ct * 4, (ct + 1) * 4)
        hr, hi = H[:, 0, sl, last], H[:, 1, sl, last]
        ar, ai = C['a_sp'][:, 0, sl], C['a_sp'][:, 1, sl]
        f = C['ccf']
        hk0, hk1 = 'c_H0_%d' % ct, 'c_H1_%d' % ct
        fk = ['c_ccf%d_%d' % (i, ct) for i in range(4)]
        k.tt('dve', f[:, 0, sl], ar, hr, ALU.mult, ['c_asp0', hk0], [fk[0]])
        k.tt('dve', f[:, 1, sl], ai, hi, ALU.mult, ['c_asp1', hk1], [fk[1]])
        k.tt('dve', f[:, 2, sl], ar, hi, ALU.mult, ['c_asp0', hk1], [fk[2]])
        k.tt('dve', f[:, 3, sl], ai, hr, ALU.mult, ['c_asp1', hk0], [fk[3]])
        k.tt('dve', cout[:, 0, sl], f[:, 0, sl], f[:, 1, sl], ALU.subtract, [fk[0], fk[1]], [coutk])
        k.tt('dve', cout[:, 1, sl], f[:, 2, sl], f[:, 3, sl], ALU.add, [fk[2], fk[3]], [coutk])

    def s_cp(gb, c0, ct):
        nn = 0
        for comp in range(2):
            for q in range(4):
                pair = ct * 4 + q
                k.op('pe', lambda e, comp=comp, pair=pair, nn=nn: e.matmul(
                    yps[:, ct, :], lhsT=C['Cpad'][:, comp, pair, :], rhs=H[:, comp, pair, :],
                    start=(nn == 0), stop=(nn == 7)), reads=['c_Cpad', 'c_H%d_%d' % (comp, ct)], writes=[ypk[ct]],
                    signal=(nn == 7))
                nn += 1
        if ct == 3:
            on_block(gb, c0)

    for i in range(n + 2):
        if i < n:
            s_bu(*steps[i])
            s_zm(*steps[i])
            s_zc(*steps[i])
        if 0 <= i - 1 < n:
            s_cs(*steps[i - 1])
            s_hm(*steps[i - 1])
            s_hc(*steps[i - 1])
        if 0 <= i - 2 < n:
            s_carry(*steps[i - 2])
            s_cp(*steps[i - 2])
        if filler:
            cnt = max(-(-len(filler) // (n + 2 - i)), min(3, len(filler)))
            for _ in range(cnt):
                filler.pop(0)()
```
